# Optimizing a Trainium2 kernel written in Bass

```python
import jax, jax.numpy as jnp
from jax import lax
import numpy as np

D_MODEL = 1024
BATCH = 32
SEQ = 2048
DEPTH = 1

GRID_W = 64
NA_HEADS = 8
NA_HEAD_DIM = 64
NA_WIDTH = NA_HEADS * NA_HEAD_DIM
NA_KH_MAX = 8
NA_KW = 16
SG_GROUPS = 8
SG_GROUP_DIM = 64
SG_WIDTH = SG_GROUPS * SG_GROUP_DIM
SG_CHUNK = 128
MIX_WIDTH = NA_WIDTH + SG_WIDTH
IN_COLS = 3 * NA_WIDTH + 2 * SG_WIDTH
D_FF = 4 * D_MODEL
EPS = 1e-6

kernel_name = "hybrid_natten_sgu_encoder_block"


def rmsnorm(x, g):
    xf = x.astype(jnp.float32)
    y = xf * lax.rsqrt(jnp.mean(xf * xf, axis=-1, keepdims=True) + EPS) * g.astype(jnp.float32)
    return y.astype(x.dtype)


def layernorm(x, g, b):
    xf = x.astype(jnp.float32)
    mu = jnp.mean(xf, axis=-1, keepdims=True)
    xc = xf - mu
    y = xc * lax.rsqrt(jnp.mean(xc * xc, axis=-1, keepdims=True) + EPS)
    return (y * g.astype(jnp.float32) + b.astype(jnp.float32)).astype(x.dtype)


def neighbourhood_attention(q, k, v, rpb):
    b, s, h, d = q.shape
    rows = s // GRID_W
    kh = min(NA_KH_MAX, rows)
    scale = NA_HEAD_DIM ** -0.5
    qg = q.reshape(b, rows, GRID_W, h, d)
    kg = k.reshape(b, rows, GRID_W, h, d)
    vg = v.reshape(b, rows, GRID_W, h, d)

    cols = jnp.arange(GRID_W)
    col_start = jnp.clip(cols - NA_KW // 2, 0, GRID_W - NA_KW)
    col_mask = (cols[None, :] >= col_start[:, None]) & (cols[None, :] < col_start[:, None] + NA_KW)
    dc_idx = jnp.clip(cols[None, :] - cols[:, None], -(NA_KW - 1), NA_KW - 1) + (NA_KW - 1)
    col_bias = rpb.astype(jnp.float32)[:, :, dc_idx]

    def one_row(r):
        rs = jnp.clip(r - kh // 2, 0, rows - kh)
        q_r = lax.dynamic_index_in_dim(qg, r, axis=1, keepdims=False)
        k_blk = lax.dynamic_slice_in_dim(kg, rs, kh, axis=1)
        v_blk = lax.dynamic_slice_in_dim(vg, rs, kh, axis=1)
        dr_idx = rs + jnp.arange(kh) - r + (NA_KH_MAX - 1)
        bias = jnp.transpose(col_bias[:, dr_idx], (0, 2, 1, 3))
        sc = jnp.einsum('bqhd,bikhd->bhqik', q_r, k_blk,
                        preferred_element_type=jnp.float32) * scale + bias[None]
        sc = jnp.where(col_mask[:, None, :], sc, -jnp.inf)
        p = jax.nn.softmax(sc.reshape(b, h, GRID_W, kh * GRID_W), axis=-1)
        p = p.reshape(b, h, GRID_W, kh, GRID_W).astype(v.dtype)
        return jnp.einsum('bhqik,bikhd->bqhd', p, v_blk)

    out = lax.map(one_row, jnp.arange(rows))
    return jnp.transpose(out, (1, 0, 2, 3, 4)).reshape(b, s, h * d)


def spatial_gating(u, v, ln_g, ln_b, w_s, b_s):
    b, s, _ = v.shape
    n_chunks = s // SG_CHUNK
    v = layernorm(v, ln_g, ln_b)
    vc = v.reshape(b, n_chunks, SG_CHUNK, SG_GROUPS, SG_GROUP_DIM)
    mixed = jnp.einsum('gpq,bcqgd->bcpgd', w_s.astype(v.dtype), vc) \
        + jnp.transpose(b_s, (1, 0)).astype(v.dtype)[None, None, :, :, None]
    return u * mixed.reshape(b, s, SG_WIDTH)


def setup_inputs(seed: int = 0) -> dict:
    key = jax.random.key(seed)
    ks = jax.random.split(key, 17)
    f32 = jnp.float32

    def nrm(k, shape, scale):
        return jax.random.normal(k, shape, f32) * scale

    def gain(k, shape):
        return 1.0 + 0.05 * jax.random.normal(k, shape, f32)

    return {
        "x": jax.random.normal(ks[0], (BATCH, SEQ, D_MODEL), f32),
        "norm_mix_pre": gain(ks[1], (DEPTH, D_MODEL)),
        "w_in": nrm(ks[2], (DEPTH, D_MODEL, IN_COLS), D_MODEL ** -0.5),
        "na_rpb": nrm(ks[3], (DEPTH, NA_HEADS, 2 * NA_KH_MAX - 1, 2 * NA_KW - 1), 0.5),
        "sg_ln_g": gain(ks[4], (DEPTH, SG_WIDTH)),
        "sg_ln_b": nrm(ks[5], (DEPTH, SG_WIDTH), 0.02),
        "sg_w_s": nrm(ks[6], (DEPTH, SG_GROUPS, SG_CHUNK, SG_CHUNK), SG_CHUNK ** -0.5),
        "sg_b_s": gain(ks[7], (DEPTH, SG_GROUPS, SG_CHUNK)),
        "g_out_na": gain(ks[8], (DEPTH, NA_WIDTH)),
        "g_out_sg": gain(ks[9], (DEPTH, SG_WIDTH)),
        "w_out": nrm(ks[10], (DEPTH, MIX_WIDTH, D_MODEL), MIX_WIDTH ** -0.5),
        "norm_mix_post": gain(ks[11], (DEPTH, D_MODEL)),
        "norm_ffn_pre": gain(ks[12], (DEPTH, D_MODEL)),
        "w_ff1": nrm(ks[13], (DEPTH, D_MODEL, D_FF), D_MODEL ** -0.5),
        "w_ff2": nrm(ks[14], (DEPTH, D_FF, D_MODEL), D_FF ** -0.5),
        "norm_ffn_post": gain(ks[15], (DEPTH, D_MODEL)),
    }


def reference(x, norm_mix_pre, w_in, na_rpb, sg_ln_g, sg_ln_b, sg_w_s, sg_b_s,
              g_out_na, g_out_sg, w_out, norm_mix_post, norm_ffn_pre, w_ff1, w_ff2,
              norm_ffn_post):
    b, s, _ = x.shape
    splits = [NA_WIDTH, 2 * NA_WIDTH, 3 * NA_WIDTH, 3 * NA_WIDTH + SG_WIDTH]
    for l in range(DEPTH):
        h = rmsnorm(x, norm_mix_pre[l])
        proj = h @ w_in[l].astype(x.dtype)
        q, k, v, su, sv = jnp.split(proj, splits, axis=-1)
        q = q.reshape(b, s, NA_HEADS, NA_HEAD_DIM)
        k = k.reshape(b, s, NA_HEADS, NA_HEAD_DIM)
        v = v.reshape(b, s, NA_HEADS, NA_HEAD_DIM)
        attn = neighbourhood_attention(q, k, v, na_rpb[l])
        sgu = spatial_gating(jax.nn.gelu(su), jax.nn.gelu(sv), sg_ln_g[l], sg_ln_b[l],
                             sg_w_s[l], sg_b_s[l])
        mix = jnp.concatenate([rmsnorm(attn, g_out_na[l]), rmsnorm(sgu, g_out_sg[l])], axis=-1)
        x = x + rmsnorm(mix @ w_out[l].astype(x.dtype), norm_mix_post[l])
        h = rmsnorm(x, norm_ffn_pre[l])
        f = jnp.square(jax.nn.relu(h @ w_ff1[l].astype(x.dtype))) @ w_ff2[l].astype(x.dtype)
        x = x + rmsnorm(f, norm_ffn_post[l])
    return x
```

```python
import numpy as np
from contextlib import ExitStack
import concourse.bass as bass
import concourse.mybir as mybir
from concourse.bass_utils import run_bass_kernel_spmd
from concourse.alu_op_type import AluOpType as ALU

AF = mybir.ActivationFunctionType
F32, BF16 = mybir.dt.float32, mybir.dt.bfloat16

NCORES = 8
D = 1024
SEQ = 2048
NSEQ = 4
GRID_W = 64
ROWS = 32
EPS = 1e-6
NEG = -30000.0
ENGS = ("pe", "act", "dve", "pool", "sp")
CFG = dict(ringS=[0, 1, 4], ringPT=[5, 6, 7], ringO=[2, 3], ringA=[0, 1, 4, 5], ringT=[6, 7], ringTiny=[2, 3], sem_lat=350.0, prio="cp", prio2="prog")


class Res:
    __slots__ = ("w", "r", "const")

    def __init__(self, const=False):
        self.w = None
        self.r = []
        self.const = const


class Op:
    __slots__ = ("eng", "instrs", "deps", "odeps", "idx", "seg", "kind", "semkey", "val",
                 "occ", "lat", "n", "succ", "start", "finish", "cp", "pr")


def _free_elems(ap):
    try:
        sh = ap.shape
        n = 1
        for d in sh[1:]:
            n *= int(d)
        return n
    except Exception:
        return 512


def _estimate(eng, instrs):
    occ = 0.0
    for name, kw in instrs:
        if name == "matmul":
            occ += max(70.0, _free_elems(kw["rhs"]) * 0.5) + 10
        elif name == "transpose":
            occ += 100
        elif name == "dma_start":
            occ += 120
        elif eng == "act":
            occ += 220 + 0.85 * _free_elems(kw["in_"]) + (90 if kw.get("accum_out") is not None else 0)
        elif eng == "dve":
            key = "in_" if "in_" in kw else ("in0" if "in0" in kw else "ap")
            n = _free_elems(kw[key])
            occ += 110 + (8.0 if name == "reciprocal" else 1.05) * n
        elif eng == "pool":
            key = "in_" if "in_" in kw else ("in0" if "in0" in kw else "ap")
            occ += 260 + 1.9 * _free_elems(kw[key])
        else:
            occ += 100
    return occ


class Sched:
    SEM_LAT = 350.0

    def __init__(self, nc, st):
        self.nc = nc
        self.st = st
        self.semh = {}
        self.cnt = {}
        self.waited = {e: {} for e in ENGS}
        self.ops = []
        self.seg = 0
        self.nidx = 0
        self.last_dma = {}
        self.marks = []
        for e in ENGS:
            self.new_sem("e_" + e)

    def new_sem(self, key):
        self.semh[key] = self.st.enter_context(self.nc.semaphore(key))
        self.cnt[key] = 0
        return key

    def _new_op(self, eng, instrs, kind, reads, writes):
        o = Op()
        o.eng, o.instrs, o.kind = eng, instrs, kind
        o.idx = self.nidx
        self.nidx += 1
        o.seg = self.seg
        o.semkey = None
        o.val = None
        o.odeps = []
        deps = {}
        for r in reads:
            if r.w is not None:
                deps[id(r.w)] = r.w
        for w in writes:
            if w.w is not None:
                deps[id(w.w)] = w.w
            for x in w.r:
                deps[id(x)] = x
        o.deps = list(deps.values())
        for r in reads:
            if not r.const:
                r.r.append(o)
        for w in writes:
            w.w = o
            w.r = []
        self.ops.append(o)
        return o

    def op(self, eng, fns, reads=(), writes=()):
        if isinstance(fns, tuple):
            fns = [fns]
        o = self._new_op(eng, list(fns), "compute", reads, writes)
        o.semkey = "e_" + eng
        o.occ = _estimate(eng, o.instrs)
        o.lat = o.occ + CFG["sem_lat"]
        return o

    def dma(self, semkey, fn, reads=(), writes=(), eng="sp"):
        o = self._new_op(eng, [fn], "dma", reads, writes)
        o.semkey = semkey
        prev = self.last_dma.get(semkey)
        if prev is not None:
            o.odeps.append(prev)
        self.last_dma[semkey] = o
        nbytes = 4 * 128 * _free_elems(fn[1]["out"])
        o.occ = 120.0
        o.lat = 6000.0 + nbytes / 100.0
        return o

    def final_wait(self, eng, res_list):
        o = self._new_op(eng, [], "wait", (), ())
        deps = {}
        for r in res_list:
            if r.w is not None:
                deps[id(r.w)] = r.w
            for x in r.r:
                deps[id(x)] = x
        o.deps = list(deps.values())
        o.occ = 10.0
        o.lat = 10.0
        return o

    def _schedule(self, ops):
        import heapq
        seg = self.seg
        for o in ops:
            o.n = 0
            o.succ = []
            o.start = 0.0
            o.finish = 0.0
        for o in ops:
            for d in o.deps:
                if d.seg == seg:
                    d.succ.append(o)
                    o.n += 1
            for d in o.odeps:
                if d.seg == seg:
                    d.succ.append(o)
                    o.n += 1
        mode = CFG.get("prio%d" % self.seg, CFG.get("prio", "prog"))
        for o in reversed(ops):
            c = 0.0
            for s_ in o.succ:
                if s_.cp > c:
                    c = s_.cp
            o.cp = c + o.lat
        if mode == "prog":
            for o in ops:
                o.pr = o.idx
        elif mode == "cp":
            for o in ops:
                o.pr = -o.cp
        else:
            w = CFG.get("mixw", 1.0)
            for o in ops:
                o.pr = o.idx * CFG.get("idx_ns", 300.0) - w * o.cp
        free = {e: 0.0 for e in ENGS}
        future = {e: [] for e in ENGS}
        ready = {e: [] for e in ENGS}
        order = {e: [] for e in ENGS}

        def push(o):
            rt = 0.0
            for d in o.deps:
                if d.seg == seg and d.finish > rt:
                    rt = d.finish
            for d in o.odeps:
                if d.seg == seg and d.start > rt:
                    rt = d.start
            heapq.heappush(future[o.eng], (rt, o.idx, o))

        for o in ops:
            if o.n == 0:
                push(o)
        remaining = len(ops)
        while remaining:
            best = None
            for e in ENGS:
                f, r = future[e], ready[e]
                fe = free[e]
                while f and f[0][0] <= fe:
                    rt, idx, o = heapq.heappop(f)
                    heapq.heappush(r, (o.pr, idx, o))
                if r:
                    cand = (fe, r[0][0], e, 0)
                elif f:
                    cand = (f[0][0], f[0][1], e, 1)
                else:
                    continue
                if best is None or cand < best:
                    best = cand
            start, _, e, kind = best
            o = heapq.heappop(ready[e])[2] if kind == 0 else heapq.heappop(future[e])[2]
            o.start = start
            o.finish = start + o.lat
            free[e] = start + o.occ
            order[e].append(o)
            remaining -= 1
            for s_ in o.succ:
                s_.n -= 1
                if s_.n == 0:
                    push(s_)
        self.sim_span = max(free.values())
        return order

    def flush(self, name=None):
        ops = self.ops
        self.ops = []
        order = self._schedule(ops)
        seg = self.seg
        for e in ENGS:
            for o in order[e]:
                if o.kind == "compute":
                    self.cnt[o.semkey] += 1
                    o.val = self.cnt[o.semkey]
                elif o.kind == "dma":
                    self.cnt[o.semkey] += 16
                    o.val = self.cnt[o.semkey]
        queues = {}
        for e in ENGS:
            q = []
            wd = self.waited[e]
            for o in order[e]:
                need = {}
                for d in o.deps:
                    if d.seg != seg and d.kind != "dma":
                        continue
                    if d.val is None:
                        continue
                    if need.get(d.semkey, 0) < d.val:
                        need[d.semkey] = d.val
                for k, v in need.items():
                    if wd.get(k, 0) >= v:
                        continue
                    wd[k] = v
                    q.append(("wait", self.semh[k], v))
                if o.kind == "compute":
                    h = self.semh[o.semkey]
                    for ins in o.instrs[:-1]:
                        q.append(("ins", ins, None, 0))
                    q.append(("ins", o.instrs[-1], h, 1))
                elif o.kind == "dma":
                    q.append(("ins", o.instrs[0], self.semh[o.semkey], 16))
            queues[e] = q
        self.seg += 1
        with self.nc.Block() as blk:
            for ename, attr in (("pe", "tensor"), ("act", "scalar"), ("dve", "vector"),
                                ("pool", "gpsimd"), ("sp", "sync")):
                q = queues[ename]

                def body(e, q=q):
                    for it in q:
                        if it[0] == "wait":
                            e.wait_ge(it[1], it[2])
                        else:
                            (n_, kw_), h, inc = it[1], it[2], it[3]
                            r = getattr(e, n_)(**kw_)
                            if h is not None:
                                r.then_inc(h, inc)
                getattr(blk, attr)(body)


def I(name, **kw):
    return (name, kw)


class Ring:
    def __init__(self, items):
        self.items = items
        self.i = 0

    def next(self):
        it = self.items[self.i % len(self.items)]
        self.i += 1
        return it


def build(nseq=NSEQ, debug=False):
    nc = bass.Bass("TRN2", target_bir_lowering=False, dynamic_dma_scratch_size=1024)
    NT = nseq * SEQ
    NTILE = NT // 128

    def din(name, shape):
        return nc.dram_tensor(name, list(shape), F32, kind="ExternalInput").ap()

    x_d = din("x", (NT, D))
    win_d = din("w_in", (D, 2560))
    wout_d = din("w_out", (D, D))
    w1_d = din("w_ff1", (D, 4096))
    w2_d = din("w_ff2", (4096, D))
    gpre_pp_d = din("gpre_pp", (128, 8))
    gmix_pp_d = din("gmix_pp", (128, 8))
    lng_pp_d = din("lng_pp", (128, 4))
    gpost_bc_d = din("gpost_bc", (128, D))
    gpre2_bc_d = din("gpre2_bc", (128, D))
    gpost2_bc_d = din("gpost2_bc", (128, D))
    relb_d = din("relb", (4, 128, 960))
    mask_d = din("mask", (128, 960))
    wsT_d = din("wsT", (128, 8, 128))
    lnb_bc_d = din("lnb_bc", (128, 512))
    bs_bc_d = din("bs_bc", (128, 4, 128))
    ident_d = din("ident", (128, 128))
    x1s_d = nc.dram_tensor("x1s", [NT, D], F32).ap()
    out_d = nc.dram_tensor("out", [NT, D], F32, kind="ExternalOutput").ap()

    with ExitStack() as st:
        S = Sched(nc, st)

        def sb(name, shape, dt):
            return st.enter_context(nc.sbuf_tensor(name, list(shape), dt))

        pf = [st.enter_context(nc.psum_tensor("pf%d" % i, [128, 512], F32)) for i in range(8)]
        pf_res = [Res() for _ in range(8)]
        pb = {i: pf[i][:].bitcast(BF16) for i in range(8)}
        pb_res = pf_res
        ringA = Ring(CFG["ringA"])
        ringO = Ring(CFG["ringO"])
        ringTiny = Ring(CFG["ringTiny"])
        ringW = Ring([0, 1, 4, 5, 6, 7])
        ringS = Ring(CFG["ringS"])
        ringPT = Ring(CFG["ringPT"])
        ringB = Ring([2, 3, 4, 5])
        ringT = Ring(CFG["ringT"])

        ident_bf = sb("ident_bf", (128, 128), BF16)
        ones_bf = sb("ones_bf", (128, 2), BF16)
        neghalf = sb("neghalf", (128, 2), F32)
        junk_l = [sb("junk_act%d" % i, (128, 1024), BF16) for i in range(1)]
        junk_res = [Res() for _ in range(1)]
        junk_ring = Ring([0])

        def junk():
            i = junk_ring.next()
            return junk_l[i], junk_res[i]
        c_res = Res(const=True)

        ph1 = ExitStack()
        st.enter_context(ph1)

        def sb1(name, shape, dt):
            return ph1.enter_context(nc.sbuf_tensor(name, list(shape), dt))

        Win = sb1("Win", (128, 8, 2560), BF16)
        Wout = sb1("Wout", (128, 8, 1024), BF16)
        WsT = sb1("WsT", (128, 8, 128), BF16)
        gpost_bc = sb1("gpost_bc_s", (128, D), F32)
        Bias = sb1("Bias_s", (128, 4, 960), BF16)
        bias2 = sb1("bias2_s", (128, 4, 128), F32)
        gpp = sb1("gpp", (128, 24), F32)

        hT_l = [sb1("hT0", (128, 8, 512), BF16)]
        QT = sb1("QT", (128, 4, SEQ), BF16)
        KTb = [sb1("KT%d" % i, (128, SEQ), BF16) for i in range(4)]
        V = sb1("V", (128, 16, 512), BF16)
        guT = sb1("guT", (128, 4, SEQ), BF16)
        ssqS = sb1("ssqS", (128, 16), F32)

        hT_resl = [[Res() for _ in range(4)] for _ in range(2)]
        QT_res = [[Res() for _ in range(4)] for _ in range(4)]
        KT_res = [[Res() for _ in range(4)] for _ in range(4)]
        at0_res = [Res() for _ in range(4)]
        V_res = [Res() for _ in range(16)]
        gu_res = [[Res() for _ in range(16)] for _ in range(4)]
        ssqS_res = [Res() for _ in range(16)]

        def attn_buf(hp):
            return QT[:, hp, :], QT_res[hp]

        NX = 3
        xin = [sb1("xin%d" % i, (128, D), F32) for i in range(3)]
        xin_res = [Res() for _ in range(NX)]
        xin_sem = [S.new_sem("xin%d" % i) for i in range(NX)]
        xring = Ring(list(range(NX)))
        NXR = 3
        xr = [sb1("xr%d" % i, (128, D), F32) for i in range(NXR)]
        xr_res = [Res() for _ in range(NXR)]
        xr_sem = [S.new_sem("xr%d" % i) for i in range(NXR)]
        xrring = Ring(list(range(NXR)))
        NY = 2
        yt = [sb1("yt%d" % i, (128, D), F32) for i in range(NY)]
        yt_res = [[Res(), Res()] for _ in range(NY)]
        yring = Ring(list(range(NY)))
        hb = [sb1("hb%d" % i, (128, D), BF16) for i in range(2)]
        hb_res = [Res() for _ in range(3)]
        hbring = Ring([0, 1, 2])
        NP = 6
        Pp = [sb1("Pp%d" % i, (128, 640), BF16) for i in range(3)]
        Pp_res = [Res() for _ in range(NP)]
        Ppring = Ring(list(range(NP)))
        PTs = [sb1("PTs%d" % i, (128, 640), BF16) for i in range(2)]
        PTs_res = [Res() for _ in range(4)]
        PTring = Ring([0, 1, 2, 3])
        NBDB = 3
        BDall = sb1("BDall", (128, 4 * NBDB, 128), BF16)
        BD_res = [Res() for _ in range(NBDB)]
        BDring = Ring(list(range(NBDB)))
        gv = [sb1("gv%d" % i, (128, 512), F32) for i in range(2)]
        gv_res = [Res() for _ in range(3)]
        gvring = Ring([0, 1, 2])
        nrm = [sb1("nrm%d" % i, (128, 512), BF16) for i in range(2)]
        nrm_res = [Res() for _ in range(2)]
        nrmring = Ring([0, 1])
        t1 = sb1("t1", (128, 4, 128), F32)
        t1_res = Res()
        sq = [sb1("sq%d" % i, (128, 4, 128), BF16) for i in range(2)]
        sq_res = [Res() for _ in range(2)]
        sqring = Ring([0, 1])
        NS = 16
        stt = sb1("stt", (128, NS, 16), F32)
        stt_res = [Res() for _ in range(NS)]
        sring = Ring(list(range(NS)))

        csem = S.new_sem("csem")
        stg_cm = ExitStack()
        stg = [stg_cm.enter_context(nc.sbuf_tensor("stg%d" % i, [128, 2560], F32)) for i in range(2)]
        stg_res = [Res(), Res()]
        stg_sem = [S.new_sem("stg0"), S.new_sem("stg1")]
        stgring = Ring([0, 1])
        cast_engs = Ring(["act", "dve", "pool"])

        S.dma(csem, I("dma_start", out=gpp[:, 0:8], in_=gpre_pp_d[:, :]), writes=[c_res])
        S.dma(csem, I("dma_start", out=gpp[:, 8:16], in_=gmix_pp_d[:, :]), writes=[c_res])
        S.dma(csem, I("dma_start", out=gpp[:, 16:20], in_=lng_pp_d[:, :]), writes=[c_res])
        S.dma(csem, I("dma_start", out=gpost_bc[:], in_=gpost_bc_d[:, :]), writes=[c_res])
        cs_res = Res(const=True)
        S.op("dve", [I("memset", ap=ones_bf[:], constant=1.0),
                     I("memset", ap=neghalf[:], constant=-0.5)], writes=[cs_res])
        S.op("pool", I("memset", ap=BDall[:], constant=0.0), writes=BD_res)
        for i in range(3):
            S.op("pool", I("memset", ap=Pp[i][:], constant=0.0), writes=[Pp_res[i]])

        def scaled_cast(eng, out_ap, in_ap, sc_ap, reads, writes):
            if eng == "act":
                S.op("act", I("activation", out=out_ap, in_=in_ap, func=AF.Copy, scale=sc_ap),
                     reads=reads, writes=writes)
            elif eng == "dve":
                S.op("dve", I("tensor_scalar", out=out_ap, in0=in_ap, scalar1=sc_ap, scalar2=None,
                                                      op0=ALU.mult), reads=reads, writes=writes)
            else:
                S.op("pool", I("tensor_scalar", out=out_ap, in0=in_ap, scalar1=sc_ap, scalar2=1.0,
                                                       op0=ALU.mult, op1=ALU.mult), reads=reads, writes=writes)

        def plain_cast(eng, out_ap, in_ap, reads, writes):
            if eng == "act":
                S.op("act", I("activation", out=out_ap, in_=in_ap, func=AF.Copy),
                     reads=reads, writes=writes)
            elif eng == "dve":
                S.op("dve", I("tensor_copy", out=out_ap, in_=in_ap), reads=reads, writes=writes)
            else:
                S.op("pool", I("tensor_copy", out=out_ap, in_=in_ap), reads=reads, writes=writes)

        for kc in range(8):
            si = stgring.next()
            S.dma(stg_sem[si], I("dma_start",
                out=stg[si][:, 0:2560], in_=win_d[kc * 128:(kc + 1) * 128, :]), writes=[stg_res[si]])
            for hlf in range(2):
                scaled_cast(cast_engs.next(), Win[:, kc, hlf * 1280:(hlf + 1) * 1280],
                            stg[si][:, hlf * 1280:(hlf + 1) * 1280], gpp[:, kc:kc + 1],
                            reads=[stg_res[si], c_res], writes=[c_res] if False else [Res()])
        for kc in range(8):
            si = stgring.next()
            S.dma(stg_sem[si], I("dma_start",
                out=stg[si][:, 0:1024], in_=wout_d[kc * 128:(kc + 1) * 128, :]), writes=[stg_res[si]])
            scaled_cast(cast_engs.next(), Wout[:, kc, :], stg[si][:, 0:1024], gpp[:, 8 + kc:9 + kc],
                        reads=[stg_res[si], c_res], writes=[Res()])
        si_m = stgring.next()
        S.dma(stg_sem[si_m], I("dma_start", out=stg[si_m][:, 0:960], in_=mask_d[:, :]),
              writes=[stg_res[si_m]])
        si_b = stgring.next()
        for hp in range(4):
            S.dma(stg_sem[si_b], I("dma_start", out=stg[si_b][:, 1000:1960], in_=relb_d[hp, :, :]),
                  writes=[stg_res[si_b]])
            S.op("dve", I("tensor_tensor", out=Bias[:, hp, :], in0=stg[si_b][:, 1000:1960],
                                                         in1=stg[si_m][:, 0:960], op=ALU.add),
                 reads=[stg_res[si_b], stg_res[si_m]], writes=[Res()])
        si = stgring.next()
        S.dma(stg_sem[si], I("dma_start", out=stg[si][:, 0:128], in_=ident_d[:, :]),
              writes=[stg_res[si]])
        S.op("dve", I("tensor_copy", out=ident_bf[:], in_=stg[si][:, 0:128]),
             reads=[stg_res[si]], writes=[cs_res])
        S.dma(stg_sem[si], I("dma_start", out=stg[si][:, 128:1152], in_=wsT_d.rearrange("p g q -> p (g q)")),
              writes=[stg_res[si]])
        S.op("dve", I("tensor_copy", out=WsT[:].rearrange("p g q -> p (g q)"), in_=stg[si][:, 128:1152]),
             reads=[stg_res[si]], writes=[cs_res])
        S.dma(stg_sem[si], I("dma_start", out=stg[si][:, 1152:1664], in_=lnb_bc_d[:, :]),
              writes=[stg_res[si]])
        S.dma(stg_sem[si], I("dma_start", out=stg[si][:, 1664:2176], in_=bs_bc_d.rearrange("p g q -> p (g q)")),
              writes=[stg_res[si]])
        for gp in range(4):
            for gg in range(2):
                g = 2 * gp + gg
                bk = ringA.next()
                S.op("pe", I("matmul",
                    out=pf[bk][:, 0:128], lhsT=stg[si][:, 1152 + gp * 128:1152 + (gp + 1) * 128],
                    rhs=stg[si][:, 128 + g * 128:128 + (g + 1) * 128], start=True, stop=True),
                    reads=[stg_res[si]], writes=[pf_res[bk]])
                S.op("dve", I("tensor_tensor",
                    out=bias2[gg * 64:(gg + 1) * 64, gp, :], in0=pf[bk][gg * 64:(gg + 1) * 64, 0:128],
                    in1=stg[si][gg * 64:(gg + 1) * 64, 1664 + gp * 128:1664 + (gp + 1) * 128], op=ALU.add),
                    reads=[pf_res[bk], stg_res[si]], writes=[cs_res])
        S.flush()
        stg_cm.close()
        hT_l.append(sb1("hT1", (128, 8, 512), BF16))
        hb.append(sb1("hb2", (128, D), BF16))
        Pp.append(sb1("Pp3", (128, 640), BF16))
        Pp.append(sb1("Pp4", (128, 640), BF16))
        Pp.append(sb1("Pp5", (128, 640), BF16))
        PTs.append(sb1("PTs3", (128, 640), BF16))
        PTs.append(sb1("PTs2", (128, 640), BF16))
        gv.append(sb1("gv2", (128, 512), F32))
        for i in (3, 4, 5):
            S.op("pool", I("memset", ap=Pp[i][:], constant=0.0), writes=[Pp_res[i]])
        hTring = Ring([0, 1])

        def rstd_from_ssq(ssq_ap, ssq_res, n, sres_slot, col):
            v_ap = stt[:, sres_slot, col:col + 1]
            r_ap = stt[:, sres_slot, col + 1:col + 2]
            S.op("dve", I("tensor_scalar", out=v_ap, in0=ssq_ap, scalar1=1.0 / n, scalar2=EPS,
                                                  op0=ALU.mult, op1=ALU.add),
                 reads=[ssq_res], writes=[stt_res[sres_slot]])
            S.op("pool", I("tensor_tensor", out=r_ap, in0=v_ap, in1=neghalf[:, 0:1], op=ALU.pow),
                 reads=[stt_res[sres_slot], cs_res], writes=[stt_res[sres_slot]])
            return r_ap

        def load_tile(dram_ap, row0):
            xi = xring.next()
            S.dma(xin_sem[xi], I("dma_start", out=xin[xi][:], in_=dram_ap[row0:row0 + 128, :]),
                  writes=[xin_res[xi]])
            return xi

        def norm_transpose(xi, dstT, dst_res, col0, gbc=None):
            ss = sring.next()
            jk, jr = junk()
            S.op("act", I("activation", out=jk[:], in_=xin[xi][:], func=AF.Square,
                                               accum_out=stt[:, ss, 0:1]),
                 reads=[xin_res[xi]], writes=[stt_res[ss], jr])
            r_ap = rstd_from_ssq(stt[:, ss, 0:1], stt_res[ss], float(D), ss, 1)
            hi = hbring.next()
            if gbc is None:
                S.op("dve", I("tensor_scalar", out=hb[hi][:], in0=xin[xi][:], scalar1=r_ap, scalar2=None,
                                                      op0=ALU.mult),
                     reads=[xin_res[xi], stt_res[ss]], writes=[hb_res[hi]])
            else:
                S.op("dve", I("scalar_tensor_tensor", out=hb[hi][:], in0=xin[xi][:], scalar=r_ap,
                                                             in1=gbc[:], op0=ALU.mult, op1=ALU.mult),
                     reads=[xin_res[xi], stt_res[ss], c_res], writes=[hb_res[hi]])
            tb = ringT.next()
            S.op("pe", [I("transpose", out=pb[tb][:, kc * 128:(kc + 1) * 128],
                                                     in_=hb[hi][:, kc * 128:(kc + 1) * 128], identity=ident_bf[:])
                        for kc in range(8)],
                 reads=[hb_res[hi], cs_res], writes=[pb_res[tb]])
            S.op("act", I("activation", out=dstT[:, :, col0:col0 + 128],
                                               in_=pb[tb].rearrange("p (k c) -> p k c", k=8), func=AF.Copy),
                 reads=[pb_res[tb]], writes=[dst_res])

        for s in range(nseq):
            tok0 = s * SEQ
            def p1ab(g):
                S.marks.append(("s%d P1ab g%d" % (s, g), S.nidx))
                hbuf = hTring.next()
                hT = hT_l[hbuf]
                hT_res = hT_resl[hbuf]
                for tt in range(4):
                    xi = load_tile(x_d, tok0 + (4 * g + tt) * 128)
                    norm_transpose(xi, hT, hT_res[tt], tt * 128)
                gsl = slice(g * 512, (g + 1) * 512)

                def proj_fm(col0, evac):
                    bk = ringA.next()
                    S.op("pe", [I("matmul", out=pf[bk][:], lhsT=Win[:, kc, col0:col0 + 128],
                                                          rhs=hT[:, kc, :], start=(kc == 0), stop=(kc == 7))
                                for kc in range(8)],
                         reads=hT_res + [c_res], writes=[pf_res[bk]])
                    evac(bk)

                def proj_tm(tt, col0, evac):
                    bk = ringA.next()
                    S.op("pe", [I("matmul", out=pf[bk][:], lhsT=hT[:, kc, tt * 128:(tt + 1) * 128],
                                                          rhs=Win[:, kc, col0:col0 + 512], start=(kc == 0),
                                                          stop=(kc == 7))
                                for kc in range(8)],
                         reads=[hT_res[tt], c_res], writes=[pf_res[bk]])
                    evac(bk)

                for c in range(4):
                    def ev_u(bk, c=c):
                        S.op("act", I("activation", out=guT[:, c, gsl], in_=pf[bk][:],
                                                           func=AF.Gelu_apprx_tanh),
                             reads=[pf_res[bk]], writes=[gu_res[c][4 * g + k] for k in range(4)])
                    proj_fm(1536 + c * 128, ev_u)
                for tt in range(4):
                    t = 4 * g + tt

                    def ev_v(bk, t=t):
                        S.op("dve", I("tensor_copy", out=V[:, t, :], in_=pf[bk][:]),
                             reads=[pf_res[bk]], writes=[V_res[t]])
                    proj_tm(tt, 1024, ev_v)

                    def ev_sg(bk, t=t):
                        gi = gvring.next()
                        S.op("act", I("activation", out=gv[gi][:], in_=pf[bk][:], func=AF.Gelu_apprx_tanh),
                             reads=[pf_res[bk]], writes=[gv_res[gi]])
                        ss = sring.next()
                        S.op("dve", I("bn_stats", out=stt[:, ss, 0:6], in_=gv[gi][:]),
                             reads=[gv_res[gi]], writes=[stt_res[ss]])
                        S.op("dve", I("bn_aggr", out=stt[:, ss, 6:8], in_=stt[:, ss, 0:6]),
                             reads=[stt_res[ss]], writes=[stt_res[ss]])
                        S.op("dve", I("tensor_scalar", out=stt[:, ss, 8:9], in0=stt[:, ss, 7:8], scalar1=EPS,
                                                              scalar2=None, op0=ALU.add),
                             reads=[stt_res[ss]], writes=[stt_res[ss]])
                        S.op("pool", I("tensor_tensor", out=stt[:, ss, 9:10], in0=stt[:, ss, 8:9],
                                                               in1=neghalf[:, 0:1], op=ALU.pow),
                             reads=[stt_res[ss], cs_res], writes=[stt_res[ss]])
                        ni = nrmring.next()
                        S.op("dve", I("tensor_scalar", out=nrm[ni][:], in0=gv[gi][:], scalar1=stt[:, ss, 6:7],
                                                              scalar2=stt[:, ss, 9:10], op0=ALU.subtract,
                                                              op1=ALU.mult),
                             reads=[gv_res[gi], stt_res[ss]], writes=[nrm_res[ni]])
                        b2 = ringA.next()
                        S.op("pe", [I("matmul",
                            out=pf[b2][(gq % 2) * 64:(gq % 2 + 1) * 64, (gq // 2) * 128:(gq // 2 + 1) * 128],
                            lhsT=nrm[ni][:, gq * 64:(gq + 1) * 64], rhs=WsT[:, gq, :], start=True, stop=True)
                            for gq in range(8)],
                            reads=[nrm_res[ni], cs_res], writes=[pf_res[b2]])
                        for gp in range(4):
                            S.op("dve", I("scalar_tensor_tensor",
                                out=t1[:, gp, :], in0=pf[b2][:, gp * 128:(gp + 1) * 128],
                                scalar=gpp[:, 16 + gp:17 + gp], in1=bias2[:, gp, :], op0=ALU.mult, op1=ALU.add),
                                reads=[pf_res[b2], c_res, cs_res], writes=[t1_res])
                        tsl = slice(t * 128, (t + 1) * 128)
                        gur = [gu_res[c][t] for c in range(4)]
                        S.op("pool", I("tensor_tensor", out=guT[:, :, tsl], in0=t1[:], in1=guT[:, :, tsl],
                                                               op=ALU.mult),
                             reads=[t1_res] + gur, writes=gur)
                        qi = sqring.next()
                        S.op("pool", I("tensor_tensor", out=sq[qi][:], in0=guT[:, :, tsl], in1=guT[:, :, tsl],
                                                               op=ALU.mult),
                             reads=gur, writes=[sq_res[qi]])
                        b3 = ringTiny.next()
                        S.op("pe", [I("matmul", out=pf[b3][:, 0:1], lhsT=sq[qi][:, gp, :],
                                                              rhs=ones_bf[:, 0:1], start=(gp == 0), stop=(gp == 3))
                                    for gp in range(4)],
                             reads=[sq_res[qi], cs_res], writes=[pf_res[b3]])
                        S.op("dve", I("tensor_copy", out=ssqS[:, t:t + 1], in_=pf[b3][:, 0:1]),
                             reads=[pf_res[b3]], writes=[ssqS_res[t]])
                    proj_tm(tt, 2048, ev_sg)
                for hp in range(4):
                    def ev_k(bk, hp=hp):
                        S.op("act", I("activation", out=KTb[hp][:, gsl], in_=pf[bk][:], func=AF.Copy),
                             reads=[pf_res[bk]], writes=[KT_res[hp][g]])
                    proj_fm(512 + hp * 128, ev_k)
                for hp in range(4):
                    def ev_q(bk, hp=hp):
                        S.op("dve", I("tensor_scalar", out=QT[:, hp, gsl], in0=pf[bk][:], scalar1=0.125,
                                                              scalar2=None, op0=ALU.mult),
                             reads=[pf_res[bk]], writes=[QT_res[hp][g]])
                    proj_fm(hp * 128, ev_q)

            def attn_rb(hps, rb):
                for hp in hps:
                    if rb == 0:
                        S.marks.append(("s%d attn hp%d" % (s, hp), S.nidx))
                    abuf, ares = attn_buf(hp)
                    ob = ringO.next()
                    for r in range(rb * 8, rb * 8 + 8):
                        rs = min(max(r - 4, 0), ROWS - 8)
                        dr0 = rs - r + 7
                        g_q = r // 8
                        if r % 4 == 0:
                            bb = BDring.next()
                            S.op("pool", [I("tensor_copy", out=BDall[0:64, bb * 4:bb * 4 + 4, 0:64],
                                            in_=QT[0:64, hp, r * 64:(r + 4) * 64].rearrange("p (r q) -> p r q", r=4)),
                                          I("tensor_copy", out=BDall[64:128, bb * 4:bb * 4 + 4, 64:128],
                                            in_=QT[64:128, hp, r * 64:(r + 4) * 64].rearrange("p (r q) -> p r q", r=4))],
                                 reads=[QT_res[hp][g_q]], writes=[BD_res[bb]])
                        bslot = bb * 4 + (r % 4)
                        kgs = sorted(set([(rs * 64) // 512, (rs * 64 + 511) // 512]))
                        sbk = ringS.next()
                        S.op("pe", I("matmul", out=pf[sbk][:], lhsT=BDall[:, bslot, :],
                                     rhs=KTb[hp][:, rs * 64:rs * 64 + 512], start=True, stop=True),
                             reads=[BD_res[bb]] + [KT_res[hp][k] for k in kgs],
                             writes=[pf_res[sbk]])
                        S.op("dve", I("tensor_tensor", out=pf[sbk][:], in0=pf[sbk][:],
                                      in1=Bias[:, hp, dr0 * 64:dr0 * 64 + 512], op=ALU.add),
                             reads=[pf_res[sbk], c_res], writes=[pf_res[sbk]])
                        ss = sring.next()
                        S.op("dve", I("tensor_reduce", out=stt[:, ss, 0:1], in_=pf[sbk][:],
                                                              axis=mybir.AxisListType.X, op=ALU.max, negate=True),
                             reads=[pf_res[sbk]], writes=[stt_res[ss]])
                        pi = Ppring.next()
                        S.op("act", I("activation", out=Pp[pi][:, 64:576], in_=pf[sbk][:], func=AF.Exp,
                                                           bias=stt[:, ss, 0:1], scale=1.0,
                                                           accum_out=stt[:, ss, 1:2]),
                             reads=[pf_res[sbk], stt_res[ss]], writes=[Pp_res[pi], stt_res[ss]])
                        S.op("dve", I("reciprocal", out=stt[:, ss, 2:3], in_=stt[:, ss, 1:2]),
                             reads=[stt_res[ss]], writes=[stt_res[ss]])
                        S.op("pool", I("tensor_scalar", out=Pp[pi][:, 64:576], in0=Pp[pi][:, 64:576],
                                                               scalar1=stt[:, ss, 2:3], scalar2=1.0,
                                                               op0=ALU.mult, op1=ALU.mult),
                             reads=[stt_res[ss], Pp_res[pi]], writes=[Pp_res[pi]])
                        if rs % 2 == 0:
                            nch, c0, t0 = 4, 64, rs // 2
                        else:
                            nch, c0, t0 = 5, 0, (rs - 1) // 2
                        tb = ringPT.next()
                        S.op("pe", [I("transpose", out=pb[tb][:, c * 128:(c + 1) * 128],
                                                               in_=Pp[pi][:, c0 + c * 128:c0 + (c + 1) * 128],
                                                               identity=ident_bf[:])
                                    for c in range(nch)],
                             reads=[Pp_res[pi], cs_res], writes=[pb_res[tb]])
                        ti = PTring.next()
                        S.op("act", I("activation", out=PTs[ti][:, 0:nch * 128], in_=pb[tb][:, 0:nch * 128],
                                                           func=AF.Copy),
                             reads=[pb_res[tb]], writes=[PTs_res[ti]])
                        oc = (r % 8) * 64
                        fns = []
                        for hh in range(2):
                            for c in range(nch):
                                fns.append(I("matmul",
                                    out=pf[ob][hh * 64:(hh + 1) * 64, oc:oc + 64],
                                    lhsT=V[:, t0 + c, (2 * hp + hh) * 64:(2 * hp + hh + 1) * 64],
                                    rhs=PTs[ti][:, c * 128 + hh * 64:c * 128 + (hh + 1) * 64],
                                    start=(c == 0), stop=(c == nch - 1)))
                        S.op("pe", fns, reads=[PTs_res[ti]] + [V_res[t0 + c] for c in range(nch)],
                             writes=[pf_res[ob]])
                    S.op("dve", I("tensor_copy", out=abuf[:, rb * 512:(rb + 1) * 512], in_=pf[ob][:]),
                         reads=[pf_res[ob]], writes=[ares[rb]])

            def p1e(t):
                if t % 4 == 0:
                    S.marks.append(("s%d P1e t%d" % (s, t), S.nidx))
                tsl = slice(t * 128, (t + 1) * 128)
                g_t = t // 4
                qi = sqring.next()
                for hp in range(4):
                    abuf, ares = attn_buf(hp)
                    S.op("act", I("activation", out=sq[qi][:, hp, :], in_=abuf[:, tsl],
                                                                         func=AF.Square),
                         reads=[ares[g_t]], writes=[sq_res[qi]])
                b3 = ringTiny.next()
                S.op("pe", [I("matmul", out=pf[b3][:, 0:1], lhsT=sq[qi][:, hp, :], rhs=ones_bf[:, 0:1],
                                                      start=(hp == 0), stop=(hp == 3)) for hp in range(4)],
                     reads=[sq_res[qi], cs_res], writes=[pf_res[b3]])
                ss = sring.next()
                S.op("dve", I("tensor_scalar", out=stt[:, ss, 0:1], in0=pf[b3][:, 0:1], scalar1=1.0 / 512,
                                                      scalar2=EPS, op0=ALU.mult, op1=ALU.add),
                     reads=[pf_res[b3]], writes=[stt_res[ss]])
                S.op("dve", I("tensor_scalar", out=stt[:, ss, 1:2], in0=ssqS[:, t:t + 1], scalar1=1.0 / 512,
                                                      scalar2=EPS, op0=ALU.mult, op1=ALU.add),
                     reads=[ssqS_res[t]], writes=[stt_res[ss]])
                S.op("pool", I("tensor_tensor", out=stt[:, ss, 2:4], in0=stt[:, ss, 0:2], in1=neghalf[:, 0:2],
                                                       op=ALU.pow),
                     reads=[stt_res[ss], cs_res], writes=[stt_res[ss]])
                yi = yring.next()
                for c in range(2):
                    csl = slice(c * 512, (c + 1) * 512)
                    bA = ringW.next()
                    fa = []
                    for hp in range(4):
                        abuf, ares = attn_buf(hp)
                        fa.append(I("matmul", out=pf[bA][:], lhsT=abuf[:, tsl],
                                                                       rhs=Wout[:, hp, csl], start=(hp == 0),
                                                                       stop=(hp == 3)))
                    S.op("pe", fa, reads=[attn_buf(hp)[1][g_t] for hp in range(4)] + [c_res],
                         writes=[pf_res[bA]])
                    bB = ringW.next()
                    S.op("pe", [I("matmul", out=pf[bB][:], lhsT=guT[:, gp, tsl],
                                                          rhs=Wout[:, 4 + gp, csl], start=(gp == 0), stop=(gp == 3))
                                for gp in range(4)],
                         reads=[gu_res[gp][t] for gp in range(4)] + [c_res], writes=[pf_res[bB]])
                    S.op("act", I("activation", out=yt[yi][:, csl], in_=pf[bA][:], func=AF.Copy,
                                                            scale=stt[:, ss, 2:3]),
                         reads=[pf_res[bA], stt_res[ss]], writes=[yt_res[yi][c]])
                    S.op("dve", I("scalar_tensor_tensor", out=yt[yi][:, csl], in0=pf[bB][:],
                                                                      scalar=stt[:, ss, 3:4], in1=yt[yi][:, csl],
                                                                      op0=ALU.mult, op1=ALU.add),
                         reads=[pf_res[bB], stt_res[ss], yt_res[yi][c]], writes=[yt_res[yi][c]])
                s2 = sring.next()
                jk, jr = junk()
                S.op("act", I("activation", out=jk[:], in_=yt[yi][:], func=AF.Square,
                                                   accum_out=stt[:, s2, 0:1]),
                     reads=yt_res[yi], writes=[stt_res[s2], jr])
                r_ap = rstd_from_ssq(stt[:, s2, 0:1], stt_res[s2], float(D), s2, 1)
                S.op("dve", I("scalar_tensor_tensor", out=yt[yi][:], in0=yt[yi][:], scalar=r_ap,
                                                             in1=gpost_bc[:], op0=ALU.mult, op1=ALU.mult),
                     reads=yt_res[yi] + [stt_res[s2], c_res], writes=yt_res[yi])
                xi = xrring.next()
                S.dma(xr_sem[xi], I("dma_start", out=xr[xi][:], in_=x_d[tok0 + t * 128:tok0 + (t + 1) * 128, :]),
                      writes=[xr_res[xi]])
                S.op("pool", I("tensor_tensor", out=xr[xi][:], in0=yt[yi][:], in1=xr[xi][:], op=ALU.add),
                     reads=yt_res[yi] + [xr_res[xi]], writes=[xr_res[xi]])
                S.dma(xr_sem[xi], I("dma_start", out=x1s_d[tok0 + t * 128:tok0 + (t + 1) * 128, :],
                                                         in_=xr[xi][:]),
                      reads=[xr_res[xi]])

            for g in range(4):
                p1ab(g)
            for hp in range(4):
                for rb in range(4):
                    attn_rb([hp], rb)
            for t in range(16):
                p1e(t)

        S.final_wait("sp", xr_res)
        S.flush()
        ph1.close()

        W1 = sb("W1", (128, 8, 4096), BF16)
        W2 = sb("W2", (128, 32, 1024), BF16)
        gpre2_bc = sb("gpre2_bc_s", (128, D), F32)
        gpost2_bc = sb("gpost2_bc_s", (128, D), F32)
        w1_res = [[Res(True), Res(True)] for _ in range(8)]
        w2_res = [[Res(True), Res(True)] for _ in range(8)]
        c2_res = Res(const=True)
        wst = [sb("wst%d" % i, (128, 2048), F32) for i in range(3)]
        wst_res = [Res(), Res(), Res()]
        wst_sem = [S.new_sem("wst0"), S.new_sem("wst1"), S.new_sem("wst2")]
        wring = Ring([0, 1, 2])
        c2sem = S.new_sem("c2sem")
        S.dma(c2sem, I("dma_start", out=gpre2_bc[:], in_=gpre2_bc_d[:, :]), writes=[c2_res])
        S.dma(c2sem, I("dma_start", out=gpost2_bc[:], in_=gpost2_bc_d[:, :]), writes=[c2_res])

        NX2 = 2
        x2 = [sb("x2_%d" % i, (128, D), F32) for i in range(NX2)]
        x2_res = [Res() for _ in range(NX2)]
        x2_sem = [S.new_sem("x2_%d" % i) for i in range(NX2)]
        x2ring = Ring(list(range(NX2)))
        Tt = [sb("Tt%d" % i, (128, D), F32) for i in range(2)]
        Tt_res = [Res(), Res()]
        Tt_sem = [S.new_sem("Tt0"), S.new_sem("Tt1")]
        Tring = Ring([0, 1])
        hb2 = [sb("hb2_%d" % i, (128, D), BF16) for i in range(4)]
        hb2_res = [Res() for _ in range(4)]
        hb2ring = Ring([0, 1, 2, 3])
        h2T = [sb("h2T%d" % i, (128, 8, 256), BF16) for i in range(2)]
        h2T_res = [[Res(), Res()] for _ in range(2)]
        h2ring = Ring([0, 1])
        NR = 4
        rtmp = [sb("rtmp%d" % i, (128, 256), F32) for i in range(2)]
        rtmp_res = [Res(), Res()]
        rtring = Ring([0, 1])
        rT = [sb("rT%d" % i, (128, 256), BF16) for i in range(NR)]
        rT_res = [Res() for _ in range(NR)]
        rTring = Ring(list(range(NR)))
        stt2 = sb("stt2", (128, NS, 16), F32)
        stt2_res = [Res() for _ in range(NS)]
        s2ring = Ring(list(range(NS)))
        f1ring = Ring([(0, 0), (1, 0), (6, 0)])
        ringT = Ring([7])
        f1_res = {(0, 0): pf_res[0], (1, 0): pf_res[1], (6, 0): pf_res[6]}

        for i in range(8):
            for half in range(2):
                wi = wring.next()
                S.dma(wst_sem[wi], I("dma_start",
                    out=wst[wi][:].rearrange("p (k c) -> p k c", k=4),
                    in_=w1_d[half * 512:(half + 1) * 512, i * 512:(i + 1) * 512].rearrange("(k p) c -> p k c", p=128)),
                    writes=[wst_res[wi]])
                plain_cast(cast_engs.next(), W1[:, half * 4:(half + 1) * 4, i * 512:(i + 1) * 512],
                           wst[wi][:].rearrange("p (k c) -> p k c", k=4), reads=[wst_res[wi]], writes=[w1_res[i][half]])
            for half in range(2):
                wi = wring.next()
                r0 = i * 512 + half * 256
                S.dma(wst_sem[wi], I("dma_start",
                    out=wst[wi][:].rearrange("p (k c) -> p k c", k=2),
                    in_=w2_d[r0:r0 + 256, :].rearrange("(k p) c -> p k c", p=128)),
                    writes=[wst_res[wi]])
                plain_cast(cast_engs.next(), W2[:, i * 4 + half * 2:i * 4 + half * 2 + 2, :],
                           wst[wi][:].rearrange("p (k c) -> p k c", k=2), reads=[wst_res[wi]], writes=[w2_res[i][half]])

        def rstd2(ssq_ap, res, n, slot, col):
            v_ap = stt2[:, slot, col:col + 1]
            r_ap = stt2[:, slot, col + 1:col + 2]
            S.op("dve", I("tensor_scalar", out=v_ap, in0=ssq_ap, scalar1=1.0 / n, scalar2=EPS,
                                                  op0=ALU.mult, op1=ALU.add), reads=[res], writes=[stt2_res[slot]])
            S.op("pool", I("tensor_tensor", out=r_ap, in0=v_ap, in1=neghalf[:, 0:1], op=ALU.pow),
                 reads=[stt2_res[slot], cs_res], writes=[stt2_res[slot]])
            return r_ap

        NG2 = NTILE // 2
        NXA = 4
        xa = [sb("xa%d" % i, (128, D), F32) for i in range(NXA)]
        xa_res = [Res() for _ in range(NXA)]
        xa_sem = [S.new_sem("xa%d" % i) for i in range(NXA)]
        xaring = Ring(list(range(NXA)))
        prep_state = {}

        def prep_load(G):
            hi2 = h2ring.next()
            xs = []
            for tt in range(2):
                row0 = (2 * G + tt) * 128
                xi = xaring.next()
                S.dma(xa_sem[xi], I("dma_start", out=xa[xi][:], in_=x1s_d[row0:row0 + 128, :]),
                      writes=[xa_res[xi]])
                xs.append(xi)
            prep_state[G] = dict(hi2=hi2, xs=xs, his=[])

        def prep_norm(G):
            ps = prep_state[G]
            for tt in range(2):
                xi = ps["xs"][tt]
                ss = s2ring.next()
                jk, jr = junk()
                S.op("act", I("activation", out=jk[:], in_=xa[xi][:], func=AF.Square,
                              accum_out=stt2[:, ss, 0:1]),
                     reads=[xa_res[xi]], writes=[stt2_res[ss], jr])
                r_ap = rstd2(stt2[:, ss, 0:1], stt2_res[ss], float(D), ss, 1)
                hi = hb2ring.next()
                S.op("dve", I("scalar_tensor_tensor", out=hb2[hi][:], in0=xa[xi][:], scalar=r_ap, in1=gpre2_bc[:],
                              op0=ALU.mult, op1=ALU.mult),
                     reads=[xa_res[xi], stt2_res[ss], c2_res], writes=[hb2_res[hi]])
                ps["his"].append(hi)

        def prep_tr(G):
            ps = prep_state[G]
            hi2 = ps["hi2"]
            for tt in range(2):
                hi = ps["his"][tt]
                tb = ringT.next()
                S.op("pe", [I("transpose", out=pb[tb][:, kc * 128:(kc + 1) * 128],
                              in_=hb2[hi][:, kc * 128:(kc + 1) * 128], identity=ident_bf[:]) for kc in range(8)],
                     reads=[hb2_res[hi], cs_res], writes=[pb_res[tb]])
                S.op("act", I("activation", out=h2T[hi2][:, :, tt * 128:(tt + 1) * 128],
                              in_=pb[tb].rearrange("p (k c) -> p k c", k=8), func=AF.Copy),
                     reads=[pb_res[tb]], writes=[h2T_res[hi2][tt]])

        prep_load(0)
        prep_norm(0)
        prep_tr(0)
        for G in range(NG2):
            hi2 = prep_state[G]["hi2"]
            if G + 1 < NG2:
                prep_load(G + 1)
            acc = [[ringB.next() for c in range(2)] for tt in range(2)]

            def ff1(j):
                fb, fo = f1ring.next()
                fr = f1_res[(fb, fo)]
                S.op("pe", [I("matmul", out=pf[fb][:, fo:fo + 256], lhsT=W1[:, kc, j * 128:(j + 1) * 128],
                                                      rhs=h2T[hi2][:, kc, :], start=(kc == 0), stop=(kc == 7))
                            for kc in range(8)],
                     reads=h2T_res[hi2] + w1_res[j // 4], writes=[fr])
                ri = rtring.next()
                S.op("act", I("activation", out=rtmp[ri][:], in_=pf[fb][:, fo:fo + 256], func=AF.Relu),
                     reads=[fr], writes=[rtmp_res[ri]])
                qi = rTring.next()
                S.op("dve", I("tensor_tensor", out=rT[qi][:], in0=rtmp[ri][:], in1=pf[fb][:, fo:fo + 256],
                                                      op=ALU.mult),
                     reads=[rtmp_res[ri], fr], writes=[rT_res[qi]])
                return qi

            def ff2(j, qi):
                fns = []
                for tt in range(2):
                    for c in range(2):
                        fns.append(I("matmul",
                            out=pf[acc[tt][c]][:], lhsT=rT[qi][:, tt * 128:(tt + 1) * 128],
                            rhs=W2[:, j, c * 512:(c + 1) * 512], start=(j == 0), stop=(j == 31)))
                S.op("pe", fns, reads=[rT_res[qi], w2_res[j // 4][(j % 4) // 2]],
                     writes=[pf_res[acc[tt][c]] for tt in range(2) for c in range(2)])

            LAG = 2
            pend = []
            for j in range(32):
                pend.append((j, ff1(j)))
                if len(pend) > LAG:
                    ff2(*pend.pop(0))
                if G + 1 < NG2 and j == 4:
                    prep_norm(G + 1)
                if G + 1 < NG2 and j == 16:
                    prep_tr(G + 1)
            while pend:
                ff2(*pend.pop(0))

            for tt in range(2):
                row0 = (2 * G + tt) * 128
                ss = s2ring.next()
                for c in range(2):
                    jk, jr = junk()
                    S.op("act", I("activation",
                        out=jk[:, 0:512], in_=pf[acc[tt][c]][:], func=AF.Square,
                        accum_out=stt2[:, ss, c:c + 1]),
                        reads=[pf_res[acc[tt][c]]], writes=[stt2_res[ss], jr])
                S.op("dve", I("tensor_tensor", out=stt2[:, ss, 2:3], in0=stt2[:, ss, 0:1],
                                                             in1=stt2[:, ss, 1:2], op=ALU.add),
                     reads=[stt2_res[ss]], writes=[stt2_res[ss]])
                r_ap = rstd2(stt2[:, ss, 2:3], stt2_res[ss], float(D), ss, 3)
                ti = Tring.next()
                for c in range(2):
                    csl = slice(c * 512, (c + 1) * 512)
                    S.op("dve", I("scalar_tensor_tensor",
                        out=Tt[ti][:, csl], in0=pf[acc[tt][c]][:], scalar=r_ap, in1=gpost2_bc[:, csl],
                        op0=ALU.mult, op1=ALU.mult),
                        reads=[pf_res[acc[tt][c]], stt2_res[ss], c2_res], writes=[Tt_res[ti]])
                xi = x2ring.next()
                S.dma(x2_sem[xi], I("dma_start", out=x2[xi][:], in_=x1s_d[row0:row0 + 128, :]),
                      writes=[x2_res[xi]])
                S.op("pool", I("tensor_tensor", out=Tt[ti][:], in0=Tt[ti][:], in1=x2[xi][:],
                                                                     op=ALU.add),
                     reads=[Tt_res[ti], x2_res[xi]], writes=[Tt_res[ti]])
                S.dma(Tt_sem[ti], I("dma_start", out=out_d[row0:row0 + 128, :], in_=Tt[ti][:]),
                      reads=[Tt_res[ti]])
        S.final_wait("sp", Tt_res)
        S.flush()
    return nc


def _prep_shared(inp):
    f = np.float32
    c = {}
    c["w_in"] = np.ascontiguousarray(inp["w_in"][0], dtype=f)
    c["w_out"] = np.ascontiguousarray(inp["w_out"][0], dtype=f)
    c["w_ff1"] = np.ascontiguousarray(inp["w_ff1"][0], dtype=f)
    c["w_ff2"] = np.ascontiguousarray(inp["w_ff2"][0], dtype=f)
    c["gpre_pp"] = np.ascontiguousarray(inp["norm_mix_pre"][0].reshape(8, 128).T, dtype=f)
    gmix = np.concatenate([inp["g_out_na"][0], inp["g_out_sg"][0]])
    c["gmix_pp"] = np.ascontiguousarray(gmix.reshape(8, 128).T, dtype=f)
    c["lng_pp"] = np.ascontiguousarray(inp["sg_ln_g"][0].reshape(4, 128).T, dtype=f)
    c["gpost_bc"] = np.ascontiguousarray(np.broadcast_to(inp["norm_mix_post"][0][None, :], (128, D)), dtype=f)
    c["gpre2_bc"] = np.ascontiguousarray(np.broadcast_to(inp["norm_ffn_pre"][0][None, :], (128, D)), dtype=f)
    c["gpost2_bc"] = np.ascontiguousarray(np.broadcast_to(inp["norm_ffn_post"][0][None, :], (128, D)), dtype=f)
    rpb = np.asarray(inp["na_rpb"][0], dtype=f)
    cols = np.arange(GRID_W)
    dc_idx = np.clip(cols[None, :] - cols[:, None], -15, 15) + 15
    col_bias = rpb[:, :, dc_idx]
    relb = np.transpose(col_bias, (0, 2, 1, 3)).reshape(4, 2 * 64, 15 * 64)
    c["relb"] = np.ascontiguousarray(relb, dtype=f)
    col_start = np.clip(cols - 8, 0, GRID_W - 16)
    inwin = (cols[None, :] >= col_start[:, None]) & (cols[None, :] < col_start[:, None] + 16)
    m = np.where(inwin, 0.0, NEG).astype(f)
    m = np.broadcast_to(m[None, :, None, :], (2, 64, 15, 64)).reshape(128, 960)
    c["mask"] = np.ascontiguousarray(m, dtype=f)
    ws = np.asarray(inp["sg_w_s"][0], dtype=f)
    c["wsT"] = np.ascontiguousarray(np.transpose(ws, (2, 0, 1)), dtype=f)
    c["lnb_bc"] = np.ascontiguousarray(np.broadcast_to(inp["sg_ln_b"][0][None, :], (128, 512)), dtype=f)
    bs = np.asarray(inp["sg_b_s"][0], dtype=f)
    bsb = np.broadcast_to(bs.reshape(4, 2, 1, 128), (4, 2, 64, 128))
    c["bs_bc"] = np.ascontiguousarray(np.transpose(bsb.reshape(4, 128, 128), (1, 0, 2)), dtype=f)
    c["ident"] = np.eye(128, dtype=f)
    return c


_NC_CACHE = {}


def kernel(**inputs):
    x = np.asarray(inputs["x"], dtype=np.float32)
    B = x.shape[0]
    per = B // NCORES
    if per not in _NC_CACHE:
        _NC_CACHE[per] = build(nseq=per)
    nc = _NC_CACHE[per]
    shared = _prep_shared(inputs)
    in_maps = []
    for c in range(NCORES):
        m = dict(shared)
        m["x"] = np.ascontiguousarray(x[c * per:(c + 1) * per].reshape(per * SEQ, D))
        in_maps.append(m)
    res = run_bass_kernel_spmd(nc, in_maps, core_ids=list(range(NCORES)))
    outs = [np.asarray(r["out"], dtype=np.float32).reshape(per, SEQ, D) for r in res.results]
    return np.concatenate(outs, axis=0)
```

```python
import numpy as np
from contextlib import ExitStack
import concourse.bass as bass
import concourse.mybir as mybir
from concourse.bass_utils import run_bass_kernel_spmd
from concourse.alu_op_type import AluOpType as ALU

AF = mybir.ActivationFunctionType
F32, BF16 = mybir.dt.float32, mybir.dt.bfloat16

NCORES = 8
D = 1024
SEQ = 2048
NSEQ = 4
GRID_W = 64
ROWS = 32
EPS = 1e-6
NEG = -30000.0
ENGS = ("pe", "act", "dve", "pool", "sp")
CFG = dict(ringS=[0, 1, 4], ringPT=[5, 6, 7], ringO=[2, 3], ringA=[0, 1, 4, 5], ringT=[6, 7], ringTiny=[2, 3], sem_lat=350.0, prio="cp", prio2="prog")


class Res:
    __slots__ = ("w", "r", "const")

    def __init__(self, const=False):
        self.w = None
        self.r = []
        self.const = const


class Op:
    __slots__ = ("eng", "instrs", "deps", "odeps", "idx", "seg", "kind", "semkey", "val",
                 "occ", "lat", "n", "succ", "start", "finish", "cp", "pr")


def _free_elems(ap):
    try:
        sh = ap.shape
        n = 1
        for d in sh[1:]:
            n *= int(d)
        return n
    except Exception:
        return 512


def _estimate(eng, instrs):
    occ = 0.0
    for name, kw in instrs:
        if name == "matmul":
            occ += max(0.45 * _free_elems(kw["rhs"]) + 5, 0.8 * _free_elems(kw["lhsT"]) + 5)
        elif name == "transpose":
            occ += 107
        elif name == "dma_start":
            occ += 120
        elif eng == "act":
            occ += 220 + 0.85 * _free_elems(kw["in_"]) + (90 if kw.get("accum_out") is not None else 0)
        elif eng == "dve":
            key = "in_" if "in_" in kw else ("in0" if "in0" in kw else "ap")
            n = _free_elems(kw[key])
            occ += 110 + (8.0 if name == "reciprocal" else 1.05) * n
        elif eng == "pool":
            key = "in_" if "in_" in kw else ("in0" if "in0" in kw else "ap")
            occ += 260 + 1.9 * _free_elems(kw[key])
        else:
            occ += 100
    return occ


class Sched:
    SEM_LAT = 350.0

    def __init__(self, nc, st):
        self.nc = nc
        self.st = st
        self.semh = {}
        self.cnt = {}
        self.waited = {e: {} for e in ENGS}
        self.ops = []
        self.seg = 0
        self.nidx = 0
        self.last_dma = {}
        self.marks = []
        for e in ENGS:
            self.new_sem("e_" + e)

    def new_sem(self, key):
        self.semh[key] = self.st.enter_context(self.nc.semaphore(key))
        self.cnt[key] = 0
        return key

    def _new_op(self, eng, instrs, kind, reads, writes):
        o = Op()
        o.eng, o.instrs, o.kind = eng, instrs, kind
        o.idx = self.nidx
        self.nidx += 1
        o.seg = self.seg
        o.semkey = None
        o.val = None
        o.odeps = []
        deps = {}
        for r in reads:
            if r.w is not None:
                deps[id(r.w)] = r.w
        for w in writes:
            if w.w is not None:
                deps[id(w.w)] = w.w
            for x in w.r:
                deps[id(x)] = x
        o.deps = list(deps.values())
        for r in reads:
            if not r.const:
                r.r.append(o)
        for w in writes:
            w.w = o
            w.r = []
        self.ops.append(o)
        return o

    def op(self, eng, fns, reads=(), writes=()):
        if isinstance(fns, tuple):
            fns = [fns]
        o = self._new_op(eng, list(fns), "compute", reads, writes)
        o.semkey = "e_" + eng
        o.occ = _estimate(eng, o.instrs)
        o.lat = o.occ + CFG["sem_lat"]
        return o

    def dma(self, semkey, fn, reads=(), writes=(), eng="sp"):
        o = self._new_op(eng, [fn], "dma", reads, writes)
        o.semkey = semkey
        prev = self.last_dma.get(semkey)
        if prev is not None:
            o.odeps.append(prev)
        self.last_dma[semkey] = o
        nbytes = 4 * 128 * _free_elems(fn[1]["out"])
        o.occ = 120.0
        o.lat = 6000.0 + nbytes / 100.0
        return o

    def final_wait(self, eng, res_list):
        o = self._new_op(eng, [], "wait", (), ())
        deps = {}
        for r in res_list:
            if r.w is not None:
                deps[id(r.w)] = r.w
            for x in r.r:
                deps[id(x)] = x
        o.deps = list(deps.values())
        o.occ = 10.0
        o.lat = 10.0
        return o

    def _schedule(self, ops):
        import heapq
        seg = self.seg
        for o in ops:
            o.n = 0
            o.succ = []
            o.start = 0.0
            o.finish = 0.0
        for o in ops:
            for d in o.deps:
                if d.seg == seg:
                    d.succ.append(o)
                    o.n += 1
            for d in o.odeps:
                if d.seg == seg:
                    d.succ.append(o)
                    o.n += 1
        mode = CFG.get("prio%d" % self.seg, CFG.get("prio", "prog"))
        for o in reversed(ops):
            c = 0.0
            for s_ in o.succ:
                if s_.cp > c:
                    c = s_.cp
            o.cp = c + o.lat
        if mode == "prog":
            for o in ops:
                o.pr = o.idx
        elif mode == "cp":
            for o in ops:
                o.pr = -o.cp
        else:
            w = CFG.get("mixw", 1.0)
            for o in ops:
                o.pr = o.idx * CFG.get("idx_ns", 300.0) - w * o.cp
        free = {e: 0.0 for e in ENGS}
        future = {e: [] for e in ENGS}
        ready = {e: [] for e in ENGS}
        order = {e: [] for e in ENGS}

        def push(o):
            rt = 0.0
            for d in o.deps:
                if d.seg == seg and d.finish > rt:
                    rt = d.finish
            for d in o.odeps:
                if d.seg == seg and d.start > rt:
                    rt = d.start
            heapq.heappush(future[o.eng], (rt, o.idx, o))

        for o in ops:
            if o.n == 0:
                push(o)
        remaining = len(ops)
        while remaining:
            best = None
            for e in ENGS:
                f, r = future[e], ready[e]
                fe = free[e]
                while f and f[0][0] <= fe:
                    rt, idx, o = heapq.heappop(f)
                    heapq.heappush(r, (o.pr, idx, o))
                if r:
                    cand = (fe, r[0][0], e, 0)
                elif f:
                    cand = (f[0][0], f[0][1], e, 1)
                else:
                    continue
                if best is None or cand < best:
                    best = cand
            start, _, e, kind = best
            o = heapq.heappop(ready[e])[2] if kind == 0 else heapq.heappop(future[e])[2]
            o.start = start
            o.finish = start + o.lat
            free[e] = start + o.occ
            order[e].append(o)
            remaining -= 1
            for s_ in o.succ:
                s_.n -= 1
                if s_.n == 0:
                    push(s_)
        self.sim_span = max(free.values())
        return order

    def flush(self, name=None):
        ops = self.ops
        self.ops = []
        order = self._schedule(ops)
        seg = self.seg
        for e in ENGS:
            for o in order[e]:
                if o.kind == "compute":
                    self.cnt[o.semkey] += 1
                    o.val = self.cnt[o.semkey]
                elif o.kind == "dma":
                    self.cnt[o.semkey] += 16
                    o.val = self.cnt[o.semkey]
        queues = {}
        for e in ENGS:
            q = []
            wd = self.waited[e]
            for o in order[e]:
                need = {}
                for d in o.deps:
                    if d.seg != seg and d.kind != "dma":
                        continue
                    if d.val is None:
                        continue
                    if need.get(d.semkey, 0) < d.val:
                        need[d.semkey] = d.val
                for k, v in need.items():
                    if wd.get(k, 0) >= v:
                        continue
                    wd[k] = v
                    q.append(("wait", self.semh[k], v))
                if o.kind == "compute":
                    h = self.semh[o.semkey]
                    for ins in o.instrs[:-1]:
                        q.append(("ins", ins, None, 0))
                    q.append(("ins", o.instrs[-1], h, 1))
                elif o.kind == "dma":
                    q.append(("ins", o.instrs[0], self.semh[o.semkey], 16))
            queues[e] = q
        self.seg += 1
        with self.nc.Block() as blk:
            for ename, attr in (("pe", "tensor"), ("act", "scalar"), ("dve", "vector"),
                                ("pool", "gpsimd"), ("sp", "sync")):
                q = queues[ename]

                def body(e, q=q):
                    for it in q:
                        if it[0] == "wait":
                            e.wait_ge(it[1], it[2])
                        else:
                            (n_, kw_), h, inc = it[1], it[2], it[3]
                            r = getattr(e, n_)(**kw_)
                            if h is not None:
                                r.then_inc(h, inc)
                getattr(blk, attr)(body)


def I(name, **kw):
    return (name, kw)


class Ring:
    def __init__(self, items):
        self.items = items
        self.i = 0

    def next(self):
        it = self.items[self.i % len(self.items)]
        self.i += 1
        return it


def build(nseq=NSEQ, debug=False):
    nc = bass.Bass("TRN2", target_bir_lowering=False, dynamic_dma_scratch_size=1024)
    NT = nseq * SEQ
    NTILE = NT // 128

    def din(name, shape):
        return nc.dram_tensor(name, list(shape), F32, kind="ExternalInput").ap()

    x_d = din("x", (NT, D))
    win_d = din("w_in", (D, 2560))
    wout_d = din("w_out", (D, D))
    w1_d = din("w_ff1", (D, 4096))
    w2_d = din("w_ff2", (4096, D))
    gpre_pp_d = din("gpre_pp", (128, 8))
    gmix_pp_d = din("gmix_pp", (128, 8))
    lng_pp_d = din("lng_pp", (128, 4))
    gpost_bc_d = din("gpost_bc", (128, D))
    gpre2_bc_d = din("gpre2_bc", (128, D))
    gpost2_bc_d = din("gpost2_bc", (128, D))
    relb_d = din("relb", (4, 128, 960))
    mask_d = din("mask", (128, 960))
    wsT_d = din("wsT", (128, 8, 128))
    lnb_bc_d = din("lnb_bc", (128, 512))
    bs_bc_d = din("bs_bc", (128, 4, 128))
    ident_d = din("ident", (128, 128))
    x1s_d = nc.dram_tensor("x1s", [NT, D], F32).ap()
    out_d = nc.dram_tensor("out", [NT, D], F32, kind="ExternalOutput").ap()

    with ExitStack() as st:
        S = Sched(nc, st)

        def sb(name, shape, dt):
            return st.enter_context(nc.sbuf_tensor(name, list(shape), dt))

        pf = [st.enter_context(nc.psum_tensor("pf%d" % i, [128, 512], F32)) for i in range(8)]
        pf_res = [Res() for _ in range(8)]
        pb = {i: pf[i][:].bitcast(BF16) for i in range(8)}
        pb_res = pf_res
        ringA = Ring(CFG["ringA"])
        ringO = Ring(CFG["ringO"])
        ringTiny = Ring(CFG["ringTiny"])
        ringW = Ring([0, 1, 4, 5, 6, 7])
        ringS = Ring(CFG["ringS"])
        ringPT = Ring(CFG["ringPT"])
        ringB = Ring([2, 3, 4, 5])
        ringT = Ring(CFG["ringT"])

        ident_bf = sb("ident_bf", (128, 128), BF16)
        ones_bf = sb("ones_bf", (128, 2), BF16)
        neghalf = sb("neghalf", (128, 2), F32)
        junk_l = [sb("junk_act%d" % i, (128, 1024), BF16) for i in range(1)]
        junk_res = [Res() for _ in range(1)]
        junk_ring = Ring([0])

        def junk():
            i = junk_ring.next()
            return junk_l[i], junk_res[i]
        c_res = Res(const=True)

        ph1 = ExitStack()
        st.enter_context(ph1)

        def sb1(name, shape, dt):
            return ph1.enter_context(nc.sbuf_tensor(name, list(shape), dt))

        Win = sb1("Win", (128, 8, 2560), BF16)
        Wout = sb1("Wout", (128, 8, 1024), BF16)
        WsT = sb1("WsT", (128, 8, 128), BF16)
        gpost_bc = sb1("gpost_bc_s", (128, D), F32)
        Bias = sb1("Bias_s", (128, 4, 960), BF16)
        bias2 = sb1("bias2_s", (128, 4, 128), F32)
        gpp = sb1("gpp", (128, 24), F32)

        hT_l = [sb1("hT0", (128, 8, 512), BF16)]
        QT = sb1("QT", (128, 4, SEQ), BF16)
        KTb = [sb1("KT%d" % i, (128, SEQ), BF16) for i in range(4)]
        V = sb1("V", (128, 16, 512), BF16)
        guT = sb1("guT", (128, 4, SEQ), BF16)
        ssqS = sb1("ssqS", (128, 16), F32)

        hT_resl = [[Res() for _ in range(4)] for _ in range(2)]
        QT_res = [[Res() for _ in range(4)] for _ in range(4)]
        KT_res = [[Res() for _ in range(4)] for _ in range(4)]
        at0_res = [Res() for _ in range(4)]
        V_res = [Res() for _ in range(16)]
        gu_res = [[Res() for _ in range(16)] for _ in range(4)]
        ssqS_res = [Res() for _ in range(16)]

        def attn_buf(hp):
            return QT[:, hp, :], QT_res[hp]

        NX = 3
        xin = [sb1("xin%d" % i, (128, D), F32) for i in range(3)]
        xin_res = [Res() for _ in range(NX)]
        xin_sem = [S.new_sem("xin%d" % i) for i in range(NX)]
        xring = Ring(list(range(NX)))
        NXR = 3
        xr = [sb1("xr%d" % i, (128, D), F32) for i in range(NXR)]
        xr_res = [Res() for _ in range(NXR)]
        xr_sem = [S.new_sem("xr%d" % i) for i in range(NXR)]
        xrring = Ring(list(range(NXR)))
        NY = 2
        yt = [sb1("yt%d" % i, (128, D), F32) for i in range(NY)]
        yt_res = [[Res(), Res()] for _ in range(NY)]
        yring = Ring(list(range(NY)))
        hb = [sb1("hb%d" % i, (128, D), BF16) for i in range(2)]
        hb_res = [Res() for _ in range(3)]
        hbring = Ring([0, 1, 2])
        NP = 6
        Pp = [sb1("Pp%d" % i, (128, 640), BF16) for i in range(3)]
        Pp_res = [Res() for _ in range(NP)]
        Ppring = Ring(list(range(NP)))
        PTs = [sb1("PTs%d" % i, (128, 640), BF16) for i in range(2)]
        PTs_res = [Res() for _ in range(4)]
        PTring = Ring([0, 1, 2, 3])
        NBDB = 3
        BDall = sb1("BDall", (128, 4 * NBDB, 128), BF16)
        BD_res = [Res() for _ in range(NBDB)]
        BDring = Ring(list(range(NBDB)))
        gv = [sb1("gv%d" % i, (128, 512), F32) for i in range(2)]
        gv_res = [Res() for _ in range(3)]
        gvring = Ring([0, 1, 2])
        nrm = [sb1("nrm%d" % i, (128, 512), BF16) for i in range(2)]
        nrm_res = [Res() for _ in range(2)]
        nrmring = Ring([0, 1])
        t1 = sb1("t1", (128, 4, 128), F32)
        t1_res = Res()
        sq = [sb1("sq%d" % i, (128, 4, 128), BF16) for i in range(2)]
        sq_res = [Res() for _ in range(2)]
        sqring = Ring([0, 1])
        NS = 16
        stt = sb1("stt", (128, NS, 16), F32)
        stt_res = [Res() for _ in range(NS)]
        sring = Ring(list(range(NS)))

        csem = S.new_sem("csem")
        stg_cm = ExitStack()
        stg = [stg_cm.enter_context(nc.sbuf_tensor("stg%d" % i, [128, 2560], F32)) for i in range(2)]
        stg_res = [Res(), Res()]
        stg_sem = [S.new_sem("stg0"), S.new_sem("stg1")]
        stgring = Ring([0, 1])
        cast_engs = Ring(["act", "dve", "pool"])

        S.dma(csem, I("dma_start", out=gpp[:, 0:8], in_=gpre_pp_d[:, :]), writes=[c_res])
        S.dma(csem, I("dma_start", out=gpp[:, 8:16], in_=gmix_pp_d[:, :]), writes=[c_res])
        S.dma(csem, I("dma_start", out=gpp[:, 16:20], in_=lng_pp_d[:, :]), writes=[c_res])
        S.dma(csem, I("dma_start", out=gpost_bc[:], in_=gpost_bc_d[:, :]), writes=[c_res])
        cs_res = Res(const=True)
        S.op("dve", [I("memset", ap=ones_bf[:], constant=1.0),
                     I("memset", ap=neghalf[:], constant=-0.5)], writes=[cs_res])
        S.op("pool", I("memset", ap=BDall[:], constant=0.0), writes=BD_res)
        for i in range(3):
            S.op("pool", I("memset", ap=Pp[i][:], constant=0.0), writes=[Pp_res[i]])

        def scaled_cast(eng, out_ap, in_ap, sc_ap, reads, writes):
            if eng == "act":
                S.op("act", I("activation", out=out_ap, in_=in_ap, func=AF.Copy, scale=sc_ap),
                     reads=reads, writes=writes)
            elif eng == "dve":
                S.op("dve", I("tensor_scalar", out=out_ap, in0=in_ap, scalar1=sc_ap, scalar2=None,
                                                      op0=ALU.mult), reads=reads, writes=writes)
            else:
                S.op("pool", I("tensor_scalar", out=out_ap, in0=in_ap, scalar1=sc_ap, scalar2=1.0,
                                                       op0=ALU.mult, op1=ALU.mult), reads=reads, writes=writes)

        def plain_cast(eng, out_ap, in_ap, reads, writes):
            if eng == "act":
                S.op("act", I("activation", out=out_ap, in_=in_ap, func=AF.Copy),
                     reads=reads, writes=writes)
            elif eng == "dve":
                S.op("dve", I("tensor_copy", out=out_ap, in_=in_ap), reads=reads, writes=writes)
            else:
                S.op("pool", I("tensor_copy", out=out_ap, in_=in_ap), reads=reads, writes=writes)

        for kc in range(8):
            si = stgring.next()
            S.dma(stg_sem[si], I("dma_start",
                out=stg[si][:, 0:2560], in_=win_d[kc * 128:(kc + 1) * 128, :]), writes=[stg_res[si]])
            for hlf in range(2):
                scaled_cast(cast_engs.next(), Win[:, kc, hlf * 1280:(hlf + 1) * 1280],
                            stg[si][:, hlf * 1280:(hlf + 1) * 1280], gpp[:, kc:kc + 1],
                            reads=[stg_res[si], c_res], writes=[c_res] if False else [Res()])
        for kc in range(8):
            si = stgring.next()
            S.dma(stg_sem[si], I("dma_start",
                out=stg[si][:, 0:1024], in_=wout_d[kc * 128:(kc + 1) * 128, :]), writes=[stg_res[si]])
            scaled_cast(cast_engs.next(), Wout[:, kc, :], stg[si][:, 0:1024], gpp[:, 8 + kc:9 + kc],
                        reads=[stg_res[si], c_res], writes=[Res()])
        si_m = stgring.next()
        S.dma(stg_sem[si_m], I("dma_start", out=stg[si_m][:, 0:960], in_=mask_d[:, :]),
              writes=[stg_res[si_m]])
        si_b = stgring.next()
        for hp in range(4):
            S.dma(stg_sem[si_b], I("dma_start", out=stg[si_b][:, 1000:1960], in_=relb_d[hp, :, :]),
                  writes=[stg_res[si_b]])
            S.op("dve", I("tensor_tensor", out=Bias[:, hp, :], in0=stg[si_b][:, 1000:1960],
                                                         in1=stg[si_m][:, 0:960], op=ALU.add),
                 reads=[stg_res[si_b], stg_res[si_m]], writes=[Res()])
        si = stgring.next()
        S.dma(stg_sem[si], I("dma_start", out=stg[si][:, 0:128], in_=ident_d[:, :]),
              writes=[stg_res[si]])
        S.op("dve", I("tensor_copy", out=ident_bf[:], in_=stg[si][:, 0:128]),
             reads=[stg_res[si]], writes=[cs_res])
        S.dma(stg_sem[si], I("dma_start", out=stg[si][:, 128:1152], in_=wsT_d.rearrange("p g q -> p (g q)")),
              writes=[stg_res[si]])
        S.op("dve", I("tensor_copy", out=WsT[:].rearrange("p g q -> p (g q)"), in_=stg[si][:, 128:1152]),
             reads=[stg_res[si]], writes=[cs_res])
        S.dma(stg_sem[si], I("dma_start", out=stg[si][:, 1152:1664], in_=lnb_bc_d[:, :]),
              writes=[stg_res[si]])
        S.dma(stg_sem[si], I("dma_start", out=stg[si][:, 1664:2176], in_=bs_bc_d.rearrange("p g q -> p (g q)")),
              writes=[stg_res[si]])
        for gp in range(4):
            for gg in range(2):
                g = 2 * gp + gg
                bk = ringA.next()
                S.op("pe", I("matmul",
                    out=pf[bk][:, 0:128], lhsT=stg[si][:, 1152 + gp * 128:1152 + (gp + 1) * 128],
                    rhs=stg[si][:, 128 + g * 128:128 + (g + 1) * 128], start=True, stop=True),
                    reads=[stg_res[si]], writes=[pf_res[bk]])
                S.op("dve", I("tensor_tensor",
                    out=bias2[gg * 64:(gg + 1) * 64, gp, :], in0=pf[bk][gg * 64:(gg + 1) * 64, 0:128],
                    in1=stg[si][gg * 64:(gg + 1) * 64, 1664 + gp * 128:1664 + (gp + 1) * 128], op=ALU.add),
                    reads=[pf_res[bk], stg_res[si]], writes=[cs_res])
        S.flush()
        stg_cm.close()
        hT_l.append(sb1("hT1", (128, 8, 512), BF16))
        hb.append(sb1("hb2", (128, D), BF16))
        Pp.append(sb1("Pp3", (128, 640), BF16))
        Pp.append(sb1("Pp4", (128, 640), BF16))
        Pp.append(sb1("Pp5", (128, 640), BF16))
        PTs.append(sb1("PTs3", (128, 640), BF16))
        PTs.append(sb1("PTs2", (128, 640), BF16))
        gv.append(sb1("gv2", (128, 512), F32))
        for i in (3, 4, 5):
            S.op("pool", I("memset", ap=Pp[i][:], constant=0.0), writes=[Pp_res[i]])
        hTring = Ring([0, 1])

        def rstd_from_ssq(ssq_ap, ssq_res, n, sres_slot, col):
            v_ap = stt[:, sres_slot, col:col + 1]
            r_ap = stt[:, sres_slot, col + 1:col + 2]
            S.op("dve", I("tensor_scalar", out=v_ap, in0=ssq_ap, scalar1=1.0 / n, scalar2=EPS,
                                                  op0=ALU.mult, op1=ALU.add),
                 reads=[ssq_res], writes=[stt_res[sres_slot]])
            S.op("pool", I("tensor_tensor", out=r_ap, in0=v_ap, in1=neghalf[:, 0:1], op=ALU.pow),
                 reads=[stt_res[sres_slot], cs_res], writes=[stt_res[sres_slot]])
            return r_ap

        def load_tile(dram_ap, row0):
            xi = xring.next()
            S.dma(xin_sem[xi], I("dma_start", out=xin[xi][:], in_=dram_ap[row0:row0 + 128, :]),
                  writes=[xin_res[xi]])
            return xi

        def norm_transpose(xi, dstT, dst_res, col0, gbc=None):
            ss = sring.next()
            jk, jr = junk()
            S.op("act", I("activation", out=jk[:], in_=xin[xi][:], func=AF.Square,
                                               accum_out=stt[:, ss, 0:1]),
                 reads=[xin_res[xi]], writes=[stt_res[ss], jr])
            r_ap = rstd_from_ssq(stt[:, ss, 0:1], stt_res[ss], float(D), ss, 1)
            hi = hbring.next()
            if gbc is None:
                S.op("dve", I("tensor_scalar", out=hb[hi][:], in0=xin[xi][:], scalar1=r_ap, scalar2=None,
                                                      op0=ALU.mult),
                     reads=[xin_res[xi], stt_res[ss]], writes=[hb_res[hi]])
            else:
                S.op("dve", I("scalar_tensor_tensor", out=hb[hi][:], in0=xin[xi][:], scalar=r_ap,
                                                             in1=gbc[:], op0=ALU.mult, op1=ALU.mult),
                     reads=[xin_res[xi], stt_res[ss], c_res], writes=[hb_res[hi]])
            tb = ringT.next()
            S.op("pe", [I("transpose", out=pb[tb][:, kc * 128:(kc + 1) * 128],
                                                     in_=hb[hi][:, kc * 128:(kc + 1) * 128], identity=ident_bf[:])
                        for kc in range(8)],
                 reads=[hb_res[hi], cs_res], writes=[pb_res[tb]])
            S.op("act", I("activation", out=dstT[:, :, col0:col0 + 128],
                                               in_=pb[tb].rearrange("p (k c) -> p k c", k=8), func=AF.Copy),
                 reads=[pb_res[tb]], writes=[dst_res])

        for s in range(nseq):
            tok0 = s * SEQ
            def p1ab(g):
                S.marks.append(("s%d P1ab g%d" % (s, g), S.nidx))
                hbuf = hTring.next()
                hT = hT_l[hbuf]
                hT_res = hT_resl[hbuf]
                for tt in range(4):
                    xi = load_tile(x_d, tok0 + (4 * g + tt) * 128)
                    norm_transpose(xi, hT, hT_res[tt], tt * 128)
                gsl = slice(g * 512, (g + 1) * 512)

                def proj_fm(col0, evac):
                    bk = ringA.next()
                    S.op("pe", [I("matmul", out=pf[bk][:], lhsT=Win[:, kc, col0:col0 + 128],
                                                          rhs=hT[:, kc, :], start=(kc == 0), stop=(kc == 7))
                                for kc in range(8)],
                         reads=hT_res + [c_res], writes=[pf_res[bk]])
                    evac(bk)

                def proj_tm(tt, col0, evac):
                    bk = ringA.next()
                    S.op("pe", [I("matmul", out=pf[bk][:], lhsT=hT[:, kc, tt * 128:(tt + 1) * 128],
                                                          rhs=Win[:, kc, col0:col0 + 512], start=(kc == 0),
                                                          stop=(kc == 7))
                                for kc in range(8)],
                         reads=[hT_res[tt], c_res], writes=[pf_res[bk]])
                    evac(bk)

                for c in range(4):
                    def ev_u(bk, c=c):
                        S.op("act", I("activation", out=guT[:, c, gsl], in_=pf[bk][:],
                                                           func=AF.Gelu_apprx_tanh),
                             reads=[pf_res[bk]], writes=[gu_res[c][4 * g + k] for k in range(4)])
                    proj_fm(1536 + c * 128, ev_u)
                for tt in range(4):
                    t = 4 * g + tt

                    def ev_v(bk, t=t):
                        S.op("dve", I("tensor_copy", out=V[:, t, :], in_=pf[bk][:]),
                             reads=[pf_res[bk]], writes=[V_res[t]])
                    proj_tm(tt, 1024, ev_v)

                    def ev_sg(bk, t=t):
                        gi = gvring.next()
                        S.op("act", I("activation", out=gv[gi][:], in_=pf[bk][:], func=AF.Gelu_apprx_tanh),
                             reads=[pf_res[bk]], writes=[gv_res[gi]])
                        ss = sring.next()
                        S.op("dve", I("bn_stats", out=stt[:, ss, 0:6], in_=gv[gi][:]),
                             reads=[gv_res[gi]], writes=[stt_res[ss]])
                        S.op("dve", I("bn_aggr", out=stt[:, ss, 6:8], in_=stt[:, ss, 0:6]),
                             reads=[stt_res[ss]], writes=[stt_res[ss]])
                        S.op("dve", I("tensor_scalar", out=stt[:, ss, 8:9], in0=stt[:, ss, 7:8], scalar1=EPS,
                                                              scalar2=None, op0=ALU.add),
                             reads=[stt_res[ss]], writes=[stt_res[ss]])
                        S.op("pool", I("tensor_tensor", out=stt[:, ss, 9:10], in0=stt[:, ss, 8:9],
                                                               in1=neghalf[:, 0:1], op=ALU.pow),
                             reads=[stt_res[ss], cs_res], writes=[stt_res[ss]])
                        ni = nrmring.next()
                        S.op("dve", I("tensor_scalar", out=nrm[ni][:], in0=gv[gi][:], scalar1=stt[:, ss, 6:7],
                                                              scalar2=stt[:, ss, 9:10], op0=ALU.subtract,
                                                              op1=ALU.mult),
                             reads=[gv_res[gi], stt_res[ss]], writes=[nrm_res[ni]])
                        b2 = ringA.next()
                        S.op("pe", [I("matmul",
                            out=pf[b2][(gq % 2) * 64:(gq % 2 + 1) * 64, (gq // 2) * 128:(gq // 2 + 1) * 128],
                            lhsT=nrm[ni][:, gq * 64:(gq + 1) * 64], rhs=WsT[:, gq, :], start=True, stop=True)
                            for gq in range(8)],
                            reads=[nrm_res[ni], cs_res], writes=[pf_res[b2]])
                        for gp in range(4):
                            S.op("dve", I("scalar_tensor_tensor",
                                out=t1[:, gp, :], in0=pf[b2][:, gp * 128:(gp + 1) * 128],
                                scalar=gpp[:, 16 + gp:17 + gp], in1=bias2[:, gp, :], op0=ALU.mult, op1=ALU.add),
                                reads=[pf_res[b2], c_res, cs_res], writes=[t1_res])
                        tsl = slice(t * 128, (t + 1) * 128)
                        gur = [gu_res[c][t] for c in range(4)]
                        S.op("pool", I("tensor_tensor", out=guT[:, :, tsl], in0=t1[:], in1=guT[:, :, tsl],
                                                               op=ALU.mult),
                             reads=[t1_res] + gur, writes=gur)
                        qi = sqring.next()
                        S.op("pool", I("tensor_tensor", out=sq[qi][:], in0=guT[:, :, tsl], in1=guT[:, :, tsl],
                                                               op=ALU.mult),
                             reads=gur, writes=[sq_res[qi]])
                        b3 = ringTiny.next()
                        S.op("pe", [I("matmul", out=pf[b3][:, 0:1], lhsT=sq[qi][:, gp, :],
                                                              rhs=ones_bf[:, 0:1], start=(gp == 0), stop=(gp == 3))
                                    for gp in range(4)],
                             reads=[sq_res[qi], cs_res], writes=[pf_res[b3]])
                        S.op("dve", I("tensor_copy", out=ssqS[:, t:t + 1], in_=pf[b3][:, 0:1]),
                             reads=[pf_res[b3]], writes=[ssqS_res[t]])
                    proj_tm(tt, 2048, ev_sg)
                for hp in range(4):
                    def ev_k(bk, hp=hp):
                        S.op("act", I("activation", out=KTb[hp][:, gsl], in_=pf[bk][:], func=AF.Copy),
                             reads=[pf_res[bk]], writes=[KT_res[hp][g]])
                    proj_fm(512 + hp * 128, ev_k)
                for hp in range(4):
                    def ev_q(bk, hp=hp):
                        S.op("dve", I("tensor_scalar", out=QT[:, hp, gsl], in0=pf[bk][:], scalar1=0.125,
                                                              scalar2=None, op0=ALU.mult),
                             reads=[pf_res[bk]], writes=[QT_res[hp][g]])
                    proj_fm(hp * 128, ev_q)

            def attn_rb(hps, rb):
                for hp in hps:
                    if rb == 0:
                        S.marks.append(("s%d attn hp%d" % (s, hp), S.nidx))
                    abuf, ares = attn_buf(hp)
                    ob = ringO.next()
                    for r in range(rb * 8, rb * 8 + 8):
                        rs = min(max(r - 4, 0), ROWS - 8)
                        dr0 = rs - r + 7
                        g_q = r // 8
                        if r % 4 == 0:
                            bb = BDring.next()
                            S.op("pool", [I("tensor_copy", out=BDall[0:64, bb * 4:bb * 4 + 4, 0:64],
                                            in_=QT[0:64, hp, r * 64:(r + 4) * 64].rearrange("p (r q) -> p r q", r=4)),
                                          I("tensor_copy", out=BDall[64:128, bb * 4:bb * 4 + 4, 64:128],
                                            in_=QT[64:128, hp, r * 64:(r + 4) * 64].rearrange("p (r q) -> p r q", r=4))],
                                 reads=[QT_res[hp][g_q]], writes=[BD_res[bb]])
                        bslot = bb * 4 + (r % 4)
                        kgs = sorted(set([(rs * 64) // 512, (rs * 64 + 511) // 512]))
                        sbk = ringS.next()
                        S.op("pe", I("matmul", out=pf[sbk][:], lhsT=BDall[:, bslot, :],
                                     rhs=KTb[hp][:, rs * 64:rs * 64 + 512], start=True, stop=True),
                             reads=[BD_res[bb]] + [KT_res[hp][k] for k in kgs],
                             writes=[pf_res[sbk]])
                        S.op("dve", I("tensor_tensor", out=pf[sbk][:], in0=pf[sbk][:],
                                      in1=Bias[:, hp, dr0 * 64:dr0 * 64 + 512], op=ALU.add),
                             reads=[pf_res[sbk], c_res], writes=[pf_res[sbk]])
                        ss = sring.next()
                        S.op("dve", I("tensor_reduce", out=stt[:, ss, 0:1], in_=pf[sbk][:],
                                                              axis=mybir.AxisListType.X, op=ALU.max, negate=True),
                             reads=[pf_res[sbk]], writes=[stt_res[ss]])
                        pi = Ppring.next()
                        S.op("act", I("activation", out=Pp[pi][:, 64:576], in_=pf[sbk][:], func=AF.Exp,
                                                           bias=stt[:, ss, 0:1], scale=1.0,
                                                           accum_out=stt[:, ss, 1:2]),
                             reads=[pf_res[sbk], stt_res[ss]], writes=[Pp_res[pi], stt_res[ss]])
                        S.op("dve", I("reciprocal", out=stt[:, ss, 2:3], in_=stt[:, ss, 1:2]),
                             reads=[stt_res[ss]], writes=[stt_res[ss]])
                        S.op("pool", I("tensor_scalar", out=Pp[pi][:, 64:576], in0=Pp[pi][:, 64:576],
                                                               scalar1=stt[:, ss, 2:3], scalar2=1.0,
                                                               op0=ALU.mult, op1=ALU.mult),
                             reads=[stt_res[ss], Pp_res[pi]], writes=[Pp_res[pi]])
                        if rs % 2 == 0:
                            nch, c0, t0 = 4, 64, rs // 2
                        else:
                            nch, c0, t0 = 5, 0, (rs - 1) // 2
                        tb = ringPT.next()
                        S.op("pe", [I("transpose", out=pb[tb][:, c * 128:(c + 1) * 128],
                                                               in_=Pp[pi][:, c0 + c * 128:c0 + (c + 1) * 128],
                                                               identity=ident_bf[:])
                                    for c in range(nch)],
                             reads=[Pp_res[pi], cs_res], writes=[pb_res[tb]])
                        ti = PTring.next()
                        S.op("act", I("activation", out=PTs[ti][:, 0:nch * 128], in_=pb[tb][:, 0:nch * 128],
                                                           func=AF.Copy),
                             reads=[pb_res[tb]], writes=[PTs_res[ti]])
                        oc = (r % 8) * 64
                        fns = []
                        for hh in range(2):
                            for c in range(nch):
                                fns.append(I("matmul",
                                    out=pf[ob][hh * 64:(hh + 1) * 64, oc:oc + 64],
                                    lhsT=V[:, t0 + c, (2 * hp + hh) * 64:(2 * hp + hh + 1) * 64],
                                    rhs=PTs[ti][:, c * 128 + hh * 64:c * 128 + (hh + 1) * 64],
                                    start=(c == 0), stop=(c == nch - 1)))
                        S.op("pe", fns, reads=[PTs_res[ti]] + [V_res[t0 + c] for c in range(nch)],
                             writes=[pf_res[ob]])
                    S.op("dve", I("tensor_copy", out=abuf[:, rb * 512:(rb + 1) * 512], in_=pf[ob][:]),
                         reads=[pf_res[ob]], writes=[ares[rb]])

            def p1e(t):
                if t % 4 == 0:
                    S.marks.append(("s%d P1e t%d" % (s, t), S.nidx))
                tsl = slice(t * 128, (t + 1) * 128)
                g_t = t // 4
                qi = sqring.next()
                for hp in range(4):
                    abuf, ares = attn_buf(hp)
                    S.op("act", I("activation", out=sq[qi][:, hp, :], in_=abuf[:, tsl],
                                                                         func=AF.Square),
                         reads=[ares[g_t]], writes=[sq_res[qi]])
                b3 = ringTiny.next()
                S.op("pe", [I("matmul", out=pf[b3][:, 0:1], lhsT=sq[qi][:, hp, :], rhs=ones_bf[:, 0:1],
                                                      start=(hp == 0), stop=(hp == 3)) for hp in range(4)],
                     reads=[sq_res[qi], cs_res], writes=[pf_res[b3]])
                ss = sring.next()
                S.op("dve", I("tensor_scalar", out=stt[:, ss, 0:1], in0=pf[b3][:, 0:1], scalar1=1.0 / 512,
                                                      scalar2=EPS, op0=ALU.mult, op1=ALU.add),
                     reads=[pf_res[b3]], writes=[stt_res[ss]])
                S.op("dve", I("tensor_scalar", out=stt[:, ss, 1:2], in0=ssqS[:, t:t + 1], scalar1=1.0 / 512,
                                                      scalar2=EPS, op0=ALU.mult, op1=ALU.add),
                     reads=[ssqS_res[t]], writes=[stt_res[ss]])
                S.op("pool", I("tensor_tensor", out=stt[:, ss, 2:4], in0=stt[:, ss, 0:2], in1=neghalf[:, 0:2],
                                                       op=ALU.pow),
                     reads=[stt_res[ss], cs_res], writes=[stt_res[ss]])
                yi = yring.next()
                for c in range(2):
                    csl = slice(c * 512, (c + 1) * 512)
                    bA = ringW.next()
                    fa = []
                    for hp in range(4):
                        abuf, ares = attn_buf(hp)
                        fa.append(I("matmul", out=pf[bA][:], lhsT=abuf[:, tsl],
                                                                       rhs=Wout[:, hp, csl], start=(hp == 0),
                                                                       stop=(hp == 3)))
                    S.op("pe", fa, reads=[attn_buf(hp)[1][g_t] for hp in range(4)] + [c_res],
                         writes=[pf_res[bA]])
                    bB = ringW.next()
                    S.op("pe", [I("matmul", out=pf[bB][:], lhsT=guT[:, gp, tsl],
                                                          rhs=Wout[:, 4 + gp, csl], start=(gp == 0), stop=(gp == 3))
                                for gp in range(4)],
                         reads=[gu_res[gp][t] for gp in range(4)] + [c_res], writes=[pf_res[bB]])
                    S.op("act", I("activation", out=yt[yi][:, csl], in_=pf[bA][:], func=AF.Copy,
                                                            scale=stt[:, ss, 2:3]),
                         reads=[pf_res[bA], stt_res[ss]], writes=[yt_res[yi][c]])
                    S.op("dve", I("scalar_tensor_tensor", out=yt[yi][:, csl], in0=pf[bB][:],
                                                                      scalar=stt[:, ss, 3:4], in1=yt[yi][:, csl],
                                                                      op0=ALU.mult, op1=ALU.add),
                         reads=[pf_res[bB], stt_res[ss], yt_res[yi][c]], writes=[yt_res[yi][c]])
                s2 = sring.next()
                jk, jr = junk()
                S.op("act", I("activation", out=jk[:], in_=yt[yi][:], func=AF.Square,
                                                   accum_out=stt[:, s2, 0:1]),
                     reads=yt_res[yi], writes=[stt_res[s2], jr])
                r_ap = rstd_from_ssq(stt[:, s2, 0:1], stt_res[s2], float(D), s2, 1)
                S.op("dve", I("scalar_tensor_tensor", out=yt[yi][:], in0=yt[yi][:], scalar=r_ap,
                                                             in1=gpost_bc[:], op0=ALU.mult, op1=ALU.mult),
                     reads=yt_res[yi] + [stt_res[s2], c_res], writes=yt_res[yi])
                xi = xrring.next()
                S.dma(xr_sem[xi], I("dma_start", out=xr[xi][:], in_=x_d[tok0 + t * 128:tok0 + (t + 1) * 128, :]),
                      writes=[xr_res[xi]])
                S.op("pool", I("tensor_tensor", out=xr[xi][:], in0=yt[yi][:], in1=xr[xi][:], op=ALU.add),
                     reads=yt_res[yi] + [xr_res[xi]], writes=[xr_res[xi]])
                S.dma(xr_sem[xi], I("dma_start", out=x1s_d[tok0 + t * 128:tok0 + (t + 1) * 128, :],
                                                         in_=xr[xi][:]),
                      reads=[xr_res[xi]])

            for g in range(4):
                p1ab(g)
            for hp in range(4):
                for rb in range(4):
                    attn_rb([hp], rb)
            for t in range(16):
                p1e(t)

        S.final_wait("sp", xr_res)
        S.flush()
        ph1.close()

        W1 = sb("W1", (128, 8, 4096), BF16)
        W2 = sb("W2", (128, 32, 1024), BF16)
        gpre2_bc = sb("gpre2_bc_s", (128, D), F32)
        gpost2_bc = sb("gpost2_bc_s", (128, D), F32)
        w1_res = [[Res(True), Res(True)] for _ in range(8)]
        w2_res = [[Res(True), Res(True)] for _ in range(8)]
        c2_res = Res(const=True)
        wst = [sb("wst%d" % i, (128, 2048), F32) for i in range(3)]
        wst_res = [Res(), Res(), Res()]
        wst_sem = [S.new_sem("wst0"), S.new_sem("wst1"), S.new_sem("wst2")]
        wring = Ring([0, 1, 2])
        c2sem = S.new_sem("c2sem")
        S.dma(c2sem, I("dma_start", out=gpre2_bc[:], in_=gpre2_bc_d[:, :]), writes=[c2_res])
        S.dma(c2sem, I("dma_start", out=gpost2_bc[:], in_=gpost2_bc_d[:, :]), writes=[c2_res])

        NX2 = 2
        x2 = [sb("x2_%d" % i, (128, D), F32) for i in range(NX2)]
        x2_res = [Res() for _ in range(NX2)]
        x2_sem = [S.new_sem("x2_%d" % i) for i in range(NX2)]
        x2ring = Ring(list(range(NX2)))
        Tt = [sb("Tt%d" % i, (128, D), F32) for i in range(2)]
        Tt_res = [Res(), Res()]
        Tt_sem = [S.new_sem("Tt0"), S.new_sem("Tt1")]
        Tring = Ring([0, 1])
        hb2 = [sb("hb2_%d" % i, (128, D), BF16) for i in range(4)]
        hb2_res = [Res() for _ in range(4)]
        hb2ring = Ring([0, 1, 2, 3])
        h2T = [sb("h2T%d" % i, (128, 8, 256), BF16) for i in range(2)]
        h2T_res = [[Res(), Res()] for _ in range(2)]
        h2ring = Ring([0, 1])
        NR = 4
        rtmp = [sb("rtmp%d" % i, (128, 256), F32) for i in range(2)]
        rtmp_res = [Res(), Res()]
        rtring = Ring([0, 1])
        rT = [sb("rT%d" % i, (128, 256), BF16) for i in range(NR)]
        rT_res = [Res() for _ in range(NR)]
        rTring = Ring(list(range(NR)))
        stt2 = sb("stt2", (128, NS, 16), F32)
        stt2_res = [Res() for _ in range(NS)]
        s2ring = Ring(list(range(NS)))
        f1ring = Ring([(0, 0), (1, 0), (6, 0)])
        ringT = Ring([7])
        f1_res = {(0, 0): pf_res[0], (1, 0): pf_res[1], (6, 0): pf_res[6]}

        for i in range(8):
            for half in range(2):
                wi = wring.next()
                S.dma(wst_sem[wi], I("dma_start",
                    out=wst[wi][:].rearrange("p (k c) -> p k c", k=4),
                    in_=w1_d[half * 512:(half + 1) * 512, i * 512:(i + 1) * 512].rearrange("(k p) c -> p k c", p=128)),
                    writes=[wst_res[wi]])
                plain_cast(cast_engs.next(), W1[:, half * 4:(half + 1) * 4, i * 512:(i + 1) * 512],
                           wst[wi][:].rearrange("p (k c) -> p k c", k=4), reads=[wst_res[wi]], writes=[w1_res[i][half]])
            for half in range(2):
                wi = wring.next()
                r0 = i * 512 + half * 256
                S.dma(wst_sem[wi], I("dma_start",
                    out=wst[wi][:].rearrange("p (k c) -> p k c", k=2),
                    in_=w2_d[r0:r0 + 256, :].rearrange("(k p) c -> p k c", p=128)),
                    writes=[wst_res[wi]])
                plain_cast(cast_engs.next(), W2[:, i * 4 + half * 2:i * 4 + half * 2 + 2, :],
                           wst[wi][:].rearrange("p (k c) -> p k c", k=2), reads=[wst_res[wi]], writes=[w2_res[i][half]])

        def rstd2(ssq_ap, res, n, slot, col):
            v_ap = stt2[:, slot, col:col + 1]
            r_ap = stt2[:, slot, col + 1:col + 2]
            S.op("dve", I("tensor_scalar", out=v_ap, in0=ssq_ap, scalar1=1.0 / n, scalar2=EPS,
                                                  op0=ALU.mult, op1=ALU.add), reads=[res], writes=[stt2_res[slot]])
            S.op("pool", I("tensor_tensor", out=r_ap, in0=v_ap, in1=neghalf[:, 0:1], op=ALU.pow),
                 reads=[stt2_res[slot], cs_res], writes=[stt2_res[slot]])
            return r_ap

        NG2 = NTILE // 2
        NXA = 4
        xa = [sb("xa%d" % i, (128, D), F32) for i in range(NXA)]
        xa_res = [Res() for _ in range(NXA)]
        xa_sem = [S.new_sem("xa%d" % i) for i in range(NXA)]
        xaring = Ring(list(range(NXA)))
        prep_state = {}

        def prep_load(G):
            hi2 = h2ring.next()
            xs = []
            for tt in range(2):
                row0 = (2 * G + tt) * 128
                xi = xaring.next()
                S.dma(xa_sem[xi], I("dma_start", out=xa[xi][:], in_=x1s_d[row0:row0 + 128, :]),
                      writes=[xa_res[xi]])
                xs.append(xi)
            prep_state[G] = dict(hi2=hi2, xs=xs, his=[])

        def prep_norm(G):
            ps = prep_state[G]
            for tt in range(2):
                xi = ps["xs"][tt]
                ss = s2ring.next()
                jk, jr = junk()
                S.op("act", I("activation", out=jk[:], in_=xa[xi][:], func=AF.Square,
                              accum_out=stt2[:, ss, 0:1]),
                     reads=[xa_res[xi]], writes=[stt2_res[ss], jr])
                r_ap = rstd2(stt2[:, ss, 0:1], stt2_res[ss], float(D), ss, 1)
                hi = hb2ring.next()
                S.op("dve", I("scalar_tensor_tensor", out=hb2[hi][:], in0=xa[xi][:], scalar=r_ap, in1=gpre2_bc[:],
                              op0=ALU.mult, op1=ALU.mult),
                     reads=[xa_res[xi], stt2_res[ss], c2_res], writes=[hb2_res[hi]])
                ps["his"].append(hi)

        def prep_tr(G):
            ps = prep_state[G]
            hi2 = ps["hi2"]
            for tt in range(2):
                hi = ps["his"][tt]
                tb = ringT.next()
                S.op("pe", [I("transpose", out=pb[tb][:, kc * 128:(kc + 1) * 128],
                              in_=hb2[hi][:, kc * 128:(kc + 1) * 128], identity=ident_bf[:]) for kc in range(8)],
                     reads=[hb2_res[hi], cs_res], writes=[pb_res[tb]])
                S.op("act", I("activation", out=h2T[hi2][:, :, tt * 128:(tt + 1) * 128],
                              in_=pb[tb].rearrange("p (k c) -> p k c", k=8), func=AF.Copy),
                     reads=[pb_res[tb]], writes=[h2T_res[hi2][tt]])

        prep_load(0)
        prep_norm(0)
        prep_tr(0)
        for G in range(NG2):
            hi2 = prep_state[G]["hi2"]
            if G + 1 < NG2:
                prep_load(G + 1)
            acc = [[ringB.next() for c in range(2)] for tt in range(2)]

            def ff1(j):
                fb, fo = f1ring.next()
                fr = f1_res[(fb, fo)]
                S.op("pe", [I("matmul", out=pf[fb][:, fo:fo + 256], lhsT=W1[:, kc, j * 128:(j + 1) * 128],
                                                      rhs=h2T[hi2][:, kc, :], start=(kc == 0), stop=(kc == 7))
                            for kc in range(8)],
                     reads=h2T_res[hi2] + w1_res[j // 4], writes=[fr])
                ri = rtring.next()
                S.op("act", I("activation", out=rtmp[ri][:], in_=pf[fb][:, fo:fo + 256], func=AF.Relu),
                     reads=[fr], writes=[rtmp_res[ri]])
                qi = rTring.next()
                S.op("dve", I("tensor_tensor", out=rT[qi][:], in0=rtmp[ri][:], in1=pf[fb][:, fo:fo + 256],
                                                      op=ALU.mult),
                     reads=[rtmp_res[ri], fr], writes=[rT_res[qi]])
                return qi

            def ff2(j, qi):
                fns = []
                for tt in range(2):
                    for c in range(2):
                        fns.append(I("matmul",
                            out=pf[acc[tt][c]][:], lhsT=rT[qi][:, tt * 128:(tt + 1) * 128],
                            rhs=W2[:, j, c * 512:(c + 1) * 512], start=(j == 0), stop=(j == 31)))
                S.op("pe", fns, reads=[rT_res[qi], w2_res[j // 4][(j % 4) // 2]],
                     writes=[pf_res[acc[tt][c]] for tt in range(2) for c in range(2)])

            LAG = 2
            pend = []
            for j in range(32):
                pend.append((j, ff1(j)))
                if len(pend) > LAG:
                    ff2(*pend.pop(0))
                if G + 1 < NG2 and j == 1:
                    prep_norm(G + 1)
                if G + 1 < NG2 and j == 22:
                    prep_tr(G + 1)
            while pend:
                ff2(*pend.pop(0))

            for tt in range(2):
                row0 = (2 * G + tt) * 128
                ss = s2ring.next()
                for c in range(2):
                    jk, jr = junk()
                    S.op("act", I("activation",
                        out=jk[:, 0:512], in_=pf[acc[tt][c]][:], func=AF.Square,
                        accum_out=stt2[:, ss, c:c + 1]),
                        reads=[pf_res[acc[tt][c]]], writes=[stt2_res[ss], jr])
                S.op("dve", I("tensor_tensor", out=stt2[:, ss, 2:3], in0=stt2[:, ss, 0:1],
                                                             in1=stt2[:, ss, 1:2], op=ALU.add),
                     reads=[stt2_res[ss]], writes=[stt2_res[ss]])
                r_ap = rstd2(stt2[:, ss, 2:3], stt2_res[ss], float(D), ss, 3)
                ti = Tring.next()
                for c in range(2):
                    csl = slice(c * 512, (c + 1) * 512)
                    S.op("dve", I("scalar_tensor_tensor",
                        out=Tt[ti][:, csl], in0=pf[acc[tt][c]][:], scalar=r_ap, in1=gpost2_bc[:, csl],
                        op0=ALU.mult, op1=ALU.mult),
                        reads=[pf_res[acc[tt][c]], stt2_res[ss], c2_res], writes=[Tt_res[ti]])
                xi = x2ring.next()
                S.dma(x2_sem[xi], I("dma_start", out=x2[xi][:], in_=x1s_d[row0:row0 + 128, :]),
                      writes=[x2_res[xi]])
                S.op("pool", I("tensor_tensor", out=Tt[ti][:], in0=Tt[ti][:], in1=x2[xi][:],
                                                                     op=ALU.add),
                     reads=[Tt_res[ti], x2_res[xi]], writes=[Tt_res[ti]])
                S.dma(Tt_sem[ti], I("dma_start", out=out_d[row0:row0 + 128, :], in_=Tt[ti][:]),
                      reads=[Tt_res[ti]])
        S.final_wait("sp", Tt_res)
        S.flush()
    return nc


def _prep_shared(inp):
    f = np.float32
    c = {}
    c["w_in"] = np.ascontiguousarray(inp["w_in"][0], dtype=f)
    c["w_out"] = np.ascontiguousarray(inp["w_out"][0], dtype=f)
    c["w_ff1"] = np.ascontiguousarray(inp["w_ff1"][0], dtype=f)
    c["w_ff2"] = np.ascontiguousarray(inp["w_ff2"][0], dtype=f)
    c["gpre_pp"] = np.ascontiguousarray(inp["norm_mix_pre"][0].reshape(8, 128).T, dtype=f)
    gmix = np.concatenate([inp["g_out_na"][0], inp["g_out_sg"][0]])
    c["gmix_pp"] = np.ascontiguousarray(gmix.reshape(8, 128).T, dtype=f)
    c["lng_pp"] = np.ascontiguousarray(inp["sg_ln_g"][0].reshape(4, 128).T, dtype=f)
    c["gpost_bc"] = np.ascontiguousarray(np.broadcast_to(inp["norm_mix_post"][0][None, :], (128, D)), dtype=f)
    c["gpre2_bc"] = np.ascontiguousarray(np.broadcast_to(inp["norm_ffn_pre"][0][None, :], (128, D)), dtype=f)
    c["gpost2_bc"] = np.ascontiguousarray(np.broadcast_to(inp["norm_ffn_post"][0][None, :], (128, D)), dtype=f)
    rpb = np.asarray(inp["na_rpb"][0], dtype=f)
    cols = np.arange(GRID_W)
    dc_idx = np.clip(cols[None, :] - cols[:, None], -15, 15) + 15
    col_bias = rpb[:, :, dc_idx]
    relb = np.transpose(col_bias, (0, 2, 1, 3)).reshape(4, 2 * 64, 15 * 64)
    c["relb"] = np.ascontiguousarray(relb, dtype=f)
    col_start = np.clip(cols - 8, 0, GRID_W - 16)
    inwin = (cols[None, :] >= col_start[:, None]) & (cols[None, :] < col_start[:, None] + 16)
    m = np.where(inwin, 0.0, NEG).astype(f)
    m = np.broadcast_to(m[None, :, None, :], (2, 64, 15, 64)).reshape(128, 960)
    c["mask"] = np.ascontiguousarray(m, dtype=f)
    ws = np.asarray(inp["sg_w_s"][0], dtype=f)
    c["wsT"] = np.ascontiguousarray(np.transpose(ws, (2, 0, 1)), dtype=f)
    c["lnb_bc"] = np.ascontiguousarray(np.broadcast_to(inp["sg_ln_b"][0][None, :], (128, 512)), dtype=f)
    bs = np.asarray(inp["sg_b_s"][0], dtype=f)
    bsb = np.broadcast_to(bs.reshape(4, 2, 1, 128), (4, 2, 64, 128))
    c["bs_bc"] = np.ascontiguousarray(np.transpose(bsb.reshape(4, 128, 128), (1, 0, 2)), dtype=f)
    c["ident"] = np.eye(128, dtype=f)
    return c


_NC_CACHE = {}


def kernel(**inputs):
    x = np.asarray(inputs["x"], dtype=np.float32)
    B = x.shape[0]
    per = B // NCORES
    if per not in _NC_CACHE:
        _NC_CACHE[per] = build(nseq=per)
    nc = _NC_CACHE[per]
    shared = _prep_shared(inputs)
    in_maps = []
    for c in range(NCORES):
        m = dict(shared)
        m["x"] = np.ascontiguousarray(x[c * per:(c + 1) * per].reshape(per * SEQ, D))
        in_maps.append(m)
    res = run_bass_kernel_spmd(nc, in_maps, core_ids=list(range(NCORES)))
    outs = [np.asarray(r["out"], dtype=np.float32).reshape(per, SEQ, D) for r in res.results]
    return np.concatenate(outs, axis=0)
```

```python
import numpy as np
from contextlib import ExitStack
import concourse.bass as bass
import concourse.mybir as mybir
from concourse.bass_utils import run_bass_kernel_spmd
from concourse.alu_op_type import AluOpType as ALU

AF = mybir.ActivationFunctionType
F32, BF16 = mybir.dt.float32, mybir.dt.bfloat16

NCORES = 8
D = 1024
SEQ = 2048
NSEQ = 4
GRID_W = 64
ROWS = 32
EPS = 1e-6
NEG = -30000.0
ENGS = ("pe", "act", "dve", "pool", "sp")
CFG = dict(ringS=[0, 1, 4], ringPT=[5, 6, 7], ringO=[2, 3], ringA=[0, 1, 4, 5], ringT=[6, 7], ringTiny=[2, 3], sem_lat=350.0, prio="cp", prio2="prog")


class Res:
    __slots__ = ("w", "r", "const")

    def __init__(self, const=False):
        self.w = None
        self.r = []
        self.const = const


class Op:
    __slots__ = ("eng", "instrs", "deps", "odeps", "idx", "seg", "kind", "semkey", "val",
                 "occ", "lat", "n", "succ", "start", "finish", "cp", "pr")


def _free_elems(ap):
    try:
        sh = ap.shape
        n = 1
        for d in sh[1:]:
            n *= int(d)
        return n
    except Exception:
        return 512


def _estimate(eng, instrs):
    occ = 0.0
    for name, kw in instrs:
        if name == "matmul":
            occ += max(0.45 * _free_elems(kw["rhs"]) + 5, 0.8 * _free_elems(kw["lhsT"]) + 5)
        elif name == "transpose":
            occ += 107
        elif name == "dma_start":
            occ += 120
        elif eng == "act":
            occ += 220 + 0.85 * _free_elems(kw["in_"]) + (90 if kw.get("accum_out") is not None else 0)
        elif eng == "dve":
            key = "in_" if "in_" in kw else ("in0" if "in0" in kw else "ap")
            n = _free_elems(kw[key])
            occ += 110 + (8.0 if name == "reciprocal" else 1.05) * n
        elif eng == "pool":
            key = "in_" if "in_" in kw else ("in0" if "in0" in kw else "ap")
            occ += 260 + 1.9 * _free_elems(kw[key])
        else:
            occ += 100
    return occ


class Sched:
    SEM_LAT = 350.0

    def __init__(self, nc, st):
        self.nc = nc
        self.st = st
        self.semh = {}
        self.cnt = {}
        self.waited = {e: {} for e in ENGS}
        self.ops = []
        self.seg = 0
        self.nidx = 0
        self.last_dma = {}
        self.marks = []
        for e in ENGS:
            self.new_sem("e_" + e)

    def new_sem(self, key):
        self.semh[key] = self.st.enter_context(self.nc.semaphore(key))
        self.cnt[key] = 0
        return key

    def _new_op(self, eng, instrs, kind, reads, writes):
        o = Op()
        o.eng, o.instrs, o.kind = eng, instrs, kind
        o.idx = self.nidx
        self.nidx += 1
        o.seg = self.seg
        o.semkey = None
        o.val = None
        o.odeps = []
        deps = {}
        for r in reads:
            if r.w is not None:
                deps[id(r.w)] = r.w
        for w in writes:
            if w.w is not None:
                deps[id(w.w)] = w.w
            for x in w.r:
                deps[id(x)] = x
        o.deps = list(deps.values())
        for r in reads:
            if not r.const:
                r.r.append(o)
        for w in writes:
            w.w = o
            w.r = []
        self.ops.append(o)
        return o

    def op(self, eng, fns, reads=(), writes=()):
        if isinstance(fns, tuple):
            fns = [fns]
        o = self._new_op(eng, list(fns), "compute", reads, writes)
        o.semkey = "e_" + eng
        o.occ = _estimate(eng, o.instrs)
        o.lat = o.occ + CFG["sem_lat"]
        return o

    def dma(self, semkey, fn, reads=(), writes=(), eng="sp"):
        o = self._new_op(eng, [fn], "dma", reads, writes)
        o.semkey = semkey
        prev = self.last_dma.get(semkey)
        if prev is not None:
            o.odeps.append(prev)
        self.last_dma[semkey] = o
        nbytes = 4 * 128 * _free_elems(fn[1]["out"])
        o.occ = 120.0
        o.lat = 6000.0 + nbytes / 100.0
        return o

    def final_wait(self, eng, res_list):
        o = self._new_op(eng, [], "wait", (), ())
        deps = {}
        for r in res_list:
            if r.w is not None:
                deps[id(r.w)] = r.w
            for x in r.r:
                deps[id(x)] = x
        o.deps = list(deps.values())
        o.occ = 10.0
        o.lat = 10.0
        return o

    def _schedule(self, ops):
        import heapq
        seg = self.seg
        for o in ops:
            o.n = 0
            o.succ = []
            o.start = 0.0
            o.finish = 0.0
        for o in ops:
            for d in o.deps:
                if d.seg == seg:
                    d.succ.append(o)
                    o.n += 1
            for d in o.odeps:
                if d.seg == seg:
                    d.succ.append(o)
                    o.n += 1
        mode = CFG.get("prio%d" % self.seg, CFG.get("prio", "prog"))
        for o in reversed(ops):
            c = 0.0
            for s_ in o.succ:
                if s_.cp > c:
                    c = s_.cp
            o.cp = c + o.lat
        if mode == "prog":
            for o in ops:
                o.pr = o.idx
        elif mode == "cp":
            for o in ops:
                o.pr = -o.cp
        else:
            w = CFG.get("mixw", 1.0)
            for o in ops:
                o.pr = o.idx * CFG.get("idx_ns", 300.0) - w * o.cp
        free = {e: 0.0 for e in ENGS}
        future = {e: [] for e in ENGS}
        ready = {e: [] for e in ENGS}
        order = {e: [] for e in ENGS}

        def push(o):
            rt = 0.0
            for d in o.deps:
                if d.seg == seg and d.finish > rt:
                    rt = d.finish
            for d in o.odeps:
                if d.seg == seg and d.start > rt:
                    rt = d.start
            heapq.heappush(future[o.eng], (rt, o.idx, o))

        for o in ops:
            if o.n == 0:
                push(o)
        remaining = len(ops)
        while remaining:
            best = None
            for e in ENGS:
                f, r = future[e], ready[e]
                fe = free[e]
                while f and f[0][0] <= fe:
                    rt, idx, o = heapq.heappop(f)
                    heapq.heappush(r, (o.pr, idx, o))
                if r:
                    cand = (fe, r[0][0], e, 0)
                elif f:
                    cand = (f[0][0], f[0][1], e, 1)
                else:
                    continue
                if best is None or cand < best:
                    best = cand
            start, _, e, kind = best
            o = heapq.heappop(ready[e])[2] if kind == 0 else heapq.heappop(future[e])[2]
            o.start = start
            o.finish = start + o.lat
            free[e] = start + o.occ
            order[e].append(o)
            remaining -= 1
            for s_ in o.succ:
                s_.n -= 1
                if s_.n == 0:
                    push(s_)
        self.sim_span = max(free.values())
        return order

    def flush(self, name=None):
        ops = self.ops
        self.ops = []
        order = self._schedule(ops)
        seg = self.seg
        for e in ENGS:
            for o in order[e]:
                if o.kind == "compute":
                    self.cnt[o.semkey] += 1
                    o.val = self.cnt[o.semkey]
                elif o.kind == "dma":
                    self.cnt[o.semkey] += 16
                    o.val = self.cnt[o.semkey]
        queues = {}
        for e in ENGS:
            q = []
            wd = self.waited[e]
            for o in order[e]:
                need = {}
                for d in o.deps:
                    if d.seg != seg and d.kind != "dma":
                        continue
                    if d.val is None:
                        continue
                    if need.get(d.semkey, 0) < d.val:
                        need[d.semkey] = d.val
                for k, v in need.items():
                    if wd.get(k, 0) >= v:
                        continue
                    wd[k] = v
                    q.append(("wait", self.semh[k], v))
                if o.kind == "compute":
                    h = self.semh[o.semkey]
                    for ins in o.instrs[:-1]:
                        q.append(("ins", ins, None, 0))
                    q.append(("ins", o.instrs[-1], h, 1))
                elif o.kind == "dma":
                    q.append(("ins", o.instrs[0], self.semh[o.semkey], 16))
            queues[e] = q
        self.seg += 1
        with self.nc.Block() as blk:
            for ename, attr in (("pe", "tensor"), ("act", "scalar"), ("dve", "vector"),
                                ("pool", "gpsimd"), ("sp", "sync")):
                q = queues[ename]

                def body(e, q=q):
                    for it in q:
                        if it[0] == "wait":
                            e.wait_ge(it[1], it[2])
                        else:
                            (n_, kw_), h, inc = it[1], it[2], it[3]
                            r = getattr(e, n_)(**kw_)
                            if h is not None:
                                r.then_inc(h, inc)
                getattr(blk, attr)(body)


def I(name, **kw):
    return (name, kw)


class Ring:
    def __init__(self, items):
        self.items = items
        self.i = 0

    def next(self):
        it = self.items[self.i % len(self.items)]
        self.i += 1
        return it


def build(nseq=NSEQ, debug=False):
    nc = bass.Bass("TRN2", target_bir_lowering=False, dynamic_dma_scratch_size=1024)
    NT = nseq * SEQ
    NTILE = NT // 128

    def din(name, shape):
        return nc.dram_tensor(name, list(shape), F32, kind="ExternalInput").ap()

    x_d = din("x", (NT, D))
    win_d = din("w_in", (D, 2560))
    wout_d = din("w_out", (D, D))
    w1_d = din("w_ff1", (D, 4096))
    w2_d = din("w_ff2", (4096, D))
    gpre_pp_d = din("gpre_pp", (128, 8))
    gmix_pp_d = din("gmix_pp", (128, 8))
    lng_pp_d = din("lng_pp", (128, 4))
    gpost_bc_d = din("gpost_bc", (128, D))
    gpre2_bc_d = din("gpre2_bc", (128, D))
    gpost2_bc_d = din("gpost2_bc", (128, D))
    relb_d = din("relb", (4, 128, 960))
    mask_d = din("mask", (128, 960))
    wsT_d = din("wsT", (128, 8, 128))
    lnb_bc_d = din("lnb_bc", (128, 512))
    bs_bc_d = din("bs_bc", (128, 4, 128))
    ident_d = din("ident", (128, 128))
    x1s_d = nc.dram_tensor("x1s", [NT, D], F32).ap()
    out_d = nc.dram_tensor("out", [NT, D], F32, kind="ExternalOutput").ap()

    with ExitStack() as st:
        S = Sched(nc, st)

        def sb(name, shape, dt):
            return st.enter_context(nc.sbuf_tensor(name, list(shape), dt))

        pf = [st.enter_context(nc.psum_tensor("pf%d" % i, [128, 512], F32)) for i in range(8)]
        pf_res = [Res() for _ in range(8)]
        pb = {i: pf[i][:].bitcast(BF16) for i in range(8)}
        pb_res = pf_res
        ringA = Ring(CFG["ringA"])
        ringO = Ring(CFG["ringO"])
        ringTiny = Ring(CFG["ringTiny"])
        ringW = Ring([0, 1, 4, 5, 6, 7])
        ringS = Ring(CFG["ringS"])
        ringPT = Ring(CFG["ringPT"])
        ringB = Ring([2, 3, 4, 5])
        ringT = Ring(CFG["ringT"])

        ident_bf = sb("ident_bf", (128, 128), BF16)
        ones_bf = sb("ones_bf", (128, 2), BF16)
        neghalf = sb("neghalf", (128, 2), F32)
        junk_l = [sb("junk_act%d" % i, (128, 1024), BF16) for i in range(1)]
        junk_res = [Res() for _ in range(1)]
        junk_ring = Ring([0])

        def junk():
            i = junk_ring.next()
            return junk_l[i], junk_res[i]
        c_res = Res(const=True)

        ph1 = ExitStack()
        st.enter_context(ph1)

        def sb1(name, shape, dt):
            return ph1.enter_context(nc.sbuf_tensor(name, list(shape), dt))

        Win = sb1("Win", (128, 8, 2560), BF16)
        Wout = sb1("Wout", (128, 8, 1024), BF16)
        WsT = sb1("WsT", (128, 8, 128), BF16)
        gpost_bc = sb1("gpost_bc_s", (128, D), F32)
        Bias = sb1("Bias_s", (128, 4, 960), BF16)
        bias2 = sb1("bias2_s", (128, 4, 128), F32)
        gpp = sb1("gpp", (128, 24), F32)

        hT_l = [sb1("hT0", (128, 8, 512), BF16)]
        QT = sb1("QT", (128, 4, SEQ), BF16)
        KTb = [sb1("KT%d" % i, (128, SEQ), BF16) for i in range(4)]
        V = sb1("V", (128, 16, 512), BF16)
        guT = sb1("guT", (128, 4, SEQ), BF16)
        ssqS = sb1("ssqS", (128, 16), F32)

        hT_resl = [[Res() for _ in range(4)] for _ in range(2)]
        QT_res = [[Res() for _ in range(4)] for _ in range(4)]
        KT_res = [[Res() for _ in range(4)] for _ in range(4)]
        at0_res = [Res() for _ in range(4)]
        V_res = [Res() for _ in range(16)]
        gu_res = [[Res() for _ in range(16)] for _ in range(4)]
        ssqS_res = [Res() for _ in range(16)]

        def attn_buf(hp):
            return QT[:, hp, :], QT_res[hp]

        NX = 3
        xin = [sb1("xin%d" % i, (128, D), F32) for i in range(3)]
        xin_res = [Res() for _ in range(NX)]
        xin_sem = [S.new_sem("xin%d" % i) for i in range(NX)]
        xring = Ring(list(range(NX)))
        NXR = 3
        xr = [sb1("xr%d" % i, (128, D), F32) for i in range(NXR)]
        xr_res = [Res() for _ in range(NXR)]
        xr_sem = [S.new_sem("xr%d" % i) for i in range(NXR)]
        xrring = Ring(list(range(NXR)))
        NY = 2
        yt = [sb1("yt%d" % i, (128, D), F32) for i in range(NY)]
        yt_res = [[Res(), Res()] for _ in range(NY)]
        yring = Ring(list(range(NY)))
        hb = [sb1("hb%d" % i, (128, D), BF16) for i in range(2)]
        hb_res = [Res() for _ in range(3)]
        hbring = Ring([0, 1, 2])
        NP = 6
        Pp = [sb1("Pp%d" % i, (128, 640), BF16) for i in range(3)]
        Pp_res = [Res() for _ in range(NP)]
        Ppring = Ring(list(range(NP)))
        PTs = [sb1("PTs%d" % i, (128, 640), BF16) for i in range(2)]
        PTs_res = [Res() for _ in range(4)]
        PTring = Ring([0, 1, 2, 3])
        NBDB = 3
        BDall = sb1("BDall", (128, 4 * NBDB, 128), BF16)
        BD_res = [Res() for _ in range(NBDB)]
        BDring = Ring(list(range(NBDB)))
        gv = [sb1("gv%d" % i, (128, 512), F32) for i in range(2)]
        gv_res = [Res() for _ in range(3)]
        gvring = Ring([0, 1, 2])
        nrm = [sb1("nrm%d" % i, (128, 512), BF16) for i in range(2)]
        nrm_res = [Res() for _ in range(2)]
        nrmring = Ring([0, 1])
        t1 = sb1("t1", (128, 4, 128), F32)
        t1_res = Res()
        sq = [sb1("sq%d" % i, (128, 4, 128), BF16) for i in range(2)]
        sq_res = [Res() for _ in range(2)]
        sqring = Ring([0, 1])
        NS = 16
        stt = sb1("stt", (128, NS, 16), F32)
        stt_res = [Res() for _ in range(NS)]
        sring = Ring(list(range(NS)))

        csem = S.new_sem("csem")
        stg_cm = ExitStack()
        stg = [stg_cm.enter_context(nc.sbuf_tensor("stg%d" % i, [128, 2560], F32)) for i in range(2)]
        stg_res = [Res(), Res()]
        stg_sem = [S.new_sem("stg0"), S.new_sem("stg1")]
        stgring = Ring([0, 1])
        cast_engs = Ring(["act", "dve", "pool"])

        S.dma(csem, I("dma_start", out=gpp[:, 0:8], in_=gpre_pp_d[:, :]), writes=[c_res])
        S.dma(csem, I("dma_start", out=gpp[:, 8:16], in_=gmix_pp_d[:, :]), writes=[c_res])
        S.dma(csem, I("dma_start", out=gpp[:, 16:20], in_=lng_pp_d[:, :]), writes=[c_res])
        S.dma(csem, I("dma_start", out=gpost_bc[:], in_=gpost_bc_d[:, :]), writes=[c_res])
        cs_res = Res(const=True)
        S.op("dve", [I("memset", ap=ones_bf[:], constant=1.0),
                     I("memset", ap=neghalf[:], constant=-0.5)], writes=[cs_res])
        S.op("pool", I("memset", ap=BDall[:], constant=0.0), writes=BD_res)
        for i in range(3):
            S.op("pool", I("memset", ap=Pp[i][:], constant=0.0), writes=[Pp_res[i]])

        def scaled_cast(eng, out_ap, in_ap, sc_ap, reads, writes):
            if eng == "act":
                S.op("act", I("activation", out=out_ap, in_=in_ap, func=AF.Copy, scale=sc_ap),
                     reads=reads, writes=writes)
            elif eng == "dve":
                S.op("dve", I("tensor_scalar", out=out_ap, in0=in_ap, scalar1=sc_ap, scalar2=None,
                                                      op0=ALU.mult), reads=reads, writes=writes)
            else:
                S.op("pool", I("tensor_scalar", out=out_ap, in0=in_ap, scalar1=sc_ap, scalar2=1.0,
                                                       op0=ALU.mult, op1=ALU.mult), reads=reads, writes=writes)

        def plain_cast(eng, out_ap, in_ap, reads, writes):
            if eng == "act":
                S.op("act", I("activation", out=out_ap, in_=in_ap, func=AF.Copy),
                     reads=reads, writes=writes)
            elif eng == "dve":
                S.op("dve", I("tensor_copy", out=out_ap, in_=in_ap), reads=reads, writes=writes)
            else:
                S.op("pool", I("tensor_copy", out=out_ap, in_=in_ap), reads=reads, writes=writes)

        for kc in range(8):
            si = stgring.next()
            S.dma(stg_sem[si], I("dma_start",
                out=stg[si][:, 0:2560], in_=win_d[kc * 128:(kc + 1) * 128, :]), writes=[stg_res[si]])
            for hlf in range(2):
                scaled_cast(cast_engs.next(), Win[:, kc, hlf * 1280:(hlf + 1) * 1280],
                            stg[si][:, hlf * 1280:(hlf + 1) * 1280], gpp[:, kc:kc + 1],
                            reads=[stg_res[si], c_res], writes=[c_res] if False else [Res()])
        for kc in range(8):
            si = stgring.next()
            S.dma(stg_sem[si], I("dma_start",
                out=stg[si][:, 0:1024], in_=wout_d[kc * 128:(kc + 1) * 128, :]), writes=[stg_res[si]])
            scaled_cast(cast_engs.next(), Wout[:, kc, :], stg[si][:, 0:1024], gpp[:, 8 + kc:9 + kc],
                        reads=[stg_res[si], c_res], writes=[Res()])
        si_m = stgring.next()
        S.dma(stg_sem[si_m], I("dma_start", out=stg[si_m][:, 0:960], in_=mask_d[:, :]),
              writes=[stg_res[si_m]])
        si_b = stgring.next()
        for hp in range(4):
            S.dma(stg_sem[si_b], I("dma_start", out=stg[si_b][:, 1000:1960], in_=relb_d[hp, :, :]),
                  writes=[stg_res[si_b]])
            S.op("dve", I("tensor_tensor", out=Bias[:, hp, :], in0=stg[si_b][:, 1000:1960],
                                                         in1=stg[si_m][:, 0:960], op=ALU.add),
                 reads=[stg_res[si_b], stg_res[si_m]], writes=[Res()])
        si = stgring.next()
        S.dma(stg_sem[si], I("dma_start", out=stg[si][:, 0:128], in_=ident_d[:, :]),
              writes=[stg_res[si]])
        S.op("dve", I("tensor_copy", out=ident_bf[:], in_=stg[si][:, 0:128]),
             reads=[stg_res[si]], writes=[cs_res])
        S.dma(stg_sem[si], I("dma_start", out=stg[si][:, 128:1152], in_=wsT_d.rearrange("p g q -> p (g q)")),
              writes=[stg_res[si]])
        S.op("dve", I("tensor_copy", out=WsT[:].rearrange("p g q -> p (g q)"), in_=stg[si][:, 128:1152]),
             reads=[stg_res[si]], writes=[cs_res])
        S.dma(stg_sem[si], I("dma_start", out=stg[si][:, 1152:1664], in_=lnb_bc_d[:, :]),
              writes=[stg_res[si]])
        S.dma(stg_sem[si], I("dma_start", out=stg[si][:, 1664:2176], in_=bs_bc_d.rearrange("p g q -> p (g q)")),
              writes=[stg_res[si]])
        for gp in range(4):
            for gg in range(2):
                g = 2 * gp + gg
                bk = ringA.next()
                S.op("pe", I("matmul",
                    out=pf[bk][:, 0:128], lhsT=stg[si][:, 1152 + gp * 128:1152 + (gp + 1) * 128],
                    rhs=stg[si][:, 128 + g * 128:128 + (g + 1) * 128], start=True, stop=True),
                    reads=[stg_res[si]], writes=[pf_res[bk]])
                S.op("dve", I("tensor_tensor",
                    out=bias2[gg * 64:(gg + 1) * 64, gp, :], in0=pf[bk][gg * 64:(gg + 1) * 64, 0:128],
                    in1=stg[si][gg * 64:(gg + 1) * 64, 1664 + gp * 128:1664 + (gp + 1) * 128], op=ALU.add),
                    reads=[pf_res[bk], stg_res[si]], writes=[cs_res])
        S.flush()
        stg_cm.close()
        hT_l.append(sb1("hT1", (128, 8, 512), BF16))
        hb.append(sb1("hb2", (128, D), BF16))
        Pp.append(sb1("Pp3", (128, 640), BF16))
        Pp.append(sb1("Pp4", (128, 640), BF16))
        Pp.append(sb1("Pp5", (128, 640), BF16))
        PTs.append(sb1("PTs3", (128, 640), BF16))
        PTs.append(sb1("PTs2", (128, 640), BF16))
        gv.append(sb1("gv2", (128, 512), F32))
        for i in (3, 4, 5):
            S.op("pool", I("memset", ap=Pp[i][:], constant=0.0), writes=[Pp_res[i]])
        hTring = Ring([0, 1])

        def rstd_from_ssq(ssq_ap, ssq_res, n, sres_slot, col):
            v_ap = stt[:, sres_slot, col:col + 1]
            r_ap = stt[:, sres_slot, col + 1:col + 2]
            S.op("dve", I("tensor_scalar", out=v_ap, in0=ssq_ap, scalar1=1.0 / n, scalar2=EPS,
                                                  op0=ALU.mult, op1=ALU.add),
                 reads=[ssq_res], writes=[stt_res[sres_slot]])
            S.op("pool", I("tensor_tensor", out=r_ap, in0=v_ap, in1=neghalf[:, 0:1], op=ALU.pow),
                 reads=[stt_res[sres_slot], cs_res], writes=[stt_res[sres_slot]])
            return r_ap

        def load_tile(dram_ap, row0):
            xi = xring.next()
            S.dma(xin_sem[xi], I("dma_start", out=xin[xi][:], in_=dram_ap[row0:row0 + 128, :]),
                  writes=[xin_res[xi]])
            return xi

        def norm_transpose(xi, dstT, dst_res, col0, gbc=None):
            ss = sring.next()
            jk, jr = junk()
            S.op("act", I("activation", out=jk[:], in_=xin[xi][:], func=AF.Square,
                                               accum_out=stt[:, ss, 0:1]),
                 reads=[xin_res[xi]], writes=[stt_res[ss], jr])
            r_ap = rstd_from_ssq(stt[:, ss, 0:1], stt_res[ss], float(D), ss, 1)
            hi = hbring.next()
            if gbc is None:
                S.op("dve", I("tensor_scalar", out=hb[hi][:], in0=xin[xi][:], scalar1=r_ap, scalar2=None,
                                                      op0=ALU.mult),
                     reads=[xin_res[xi], stt_res[ss]], writes=[hb_res[hi]])
            else:
                S.op("dve", I("scalar_tensor_tensor", out=hb[hi][:], in0=xin[xi][:], scalar=r_ap,
                                                             in1=gbc[:], op0=ALU.mult, op1=ALU.mult),
                     reads=[xin_res[xi], stt_res[ss], c_res], writes=[hb_res[hi]])
            tb = ringT.next()
            S.op("pe", [I("transpose", out=pb[tb][:, kc * 128:(kc + 1) * 128],
                                                     in_=hb[hi][:, kc * 128:(kc + 1) * 128], identity=ident_bf[:])
                        for kc in range(8)],
                 reads=[hb_res[hi], cs_res], writes=[pb_res[tb]])
            S.op("act", I("activation", out=dstT[:, :, col0:col0 + 128],
                                               in_=pb[tb].rearrange("p (k c) -> p k c", k=8), func=AF.Copy),
                 reads=[pb_res[tb]], writes=[dst_res])

        for s in range(nseq):
            tok0 = s * SEQ
            def p1ab(g):
                S.marks.append(("s%d P1ab g%d" % (s, g), S.nidx))
                hbuf = hTring.next()
                hT = hT_l[hbuf]
                hT_res = hT_resl[hbuf]
                for tt in range(4):
                    xi = load_tile(x_d, tok0 + (4 * g + tt) * 128)
                    norm_transpose(xi, hT, hT_res[tt], tt * 128)
                gsl = slice(g * 512, (g + 1) * 512)

                def proj_fm(col0, evac):
                    bk = ringA.next()
                    S.op("pe", [I("matmul", out=pf[bk][:], lhsT=Win[:, kc, col0:col0 + 128],
                                                          rhs=hT[:, kc, :], start=(kc == 0), stop=(kc == 7))
                                for kc in range(8)],
                         reads=hT_res + [c_res], writes=[pf_res[bk]])
                    evac(bk)

                def proj_tm(tt, col0, evac):
                    bk = ringA.next()
                    S.op("pe", [I("matmul", out=pf[bk][:], lhsT=hT[:, kc, tt * 128:(tt + 1) * 128],
                                                          rhs=Win[:, kc, col0:col0 + 512], start=(kc == 0),
                                                          stop=(kc == 7))
                                for kc in range(8)],
                         reads=[hT_res[tt], c_res], writes=[pf_res[bk]])
                    evac(bk)

                for c in range(4):
                    def ev_u(bk, c=c):
                        S.op("act", I("activation", out=guT[:, c, gsl], in_=pf[bk][:],
                                                           func=AF.Gelu_apprx_tanh),
                             reads=[pf_res[bk]], writes=[gu_res[c][4 * g + k] for k in range(4)])
                    proj_fm(1536 + c * 128, ev_u)
                for tt in range(4):
                    t = 4 * g + tt

                    def ev_v(bk, t=t):
                        S.op("dve", I("tensor_copy", out=V[:, t, :], in_=pf[bk][:]),
                             reads=[pf_res[bk]], writes=[V_res[t]])
                    proj_tm(tt, 1024, ev_v)

                    def ev_sg(bk, t=t):
                        gi = gvring.next()
                        S.op("act", I("activation", out=gv[gi][:], in_=pf[bk][:], func=AF.Gelu_apprx_tanh),
                             reads=[pf_res[bk]], writes=[gv_res[gi]])
                        ss = sring.next()
                        S.op("dve", I("bn_stats", out=stt[:, ss, 0:6], in_=gv[gi][:]),
                             reads=[gv_res[gi]], writes=[stt_res[ss]])
                        S.op("dve", I("bn_aggr", out=stt[:, ss, 6:8], in_=stt[:, ss, 0:6]),
                             reads=[stt_res[ss]], writes=[stt_res[ss]])
                        S.op("dve", I("tensor_scalar", out=stt[:, ss, 8:9], in0=stt[:, ss, 7:8], scalar1=EPS,
                                                              scalar2=None, op0=ALU.add),
                             reads=[stt_res[ss]], writes=[stt_res[ss]])
                        S.op("pool", I("tensor_tensor", out=stt[:, ss, 9:10], in0=stt[:, ss, 8:9],
                                                               in1=neghalf[:, 0:1], op=ALU.pow),
                             reads=[stt_res[ss], cs_res], writes=[stt_res[ss]])
                        ni = nrmring.next()
                        S.op("dve", I("tensor_scalar", out=nrm[ni][:], in0=gv[gi][:], scalar1=stt[:, ss, 6:7],
                                                              scalar2=stt[:, ss, 9:10], op0=ALU.subtract,
                                                              op1=ALU.mult),
                             reads=[gv_res[gi], stt_res[ss]], writes=[nrm_res[ni]])
                        b2 = ringA.next()
                        S.op("pe", [I("matmul",
                            out=pf[b2][(gq % 2) * 64:(gq % 2 + 1) * 64, (gq // 2) * 128:(gq // 2 + 1) * 128],
                            lhsT=nrm[ni][:, gq * 64:(gq + 1) * 64], rhs=WsT[:, gq, :], start=True, stop=True)
                            for gq in range(8)],
                            reads=[nrm_res[ni], cs_res], writes=[pf_res[b2]])
                        for gp in range(4):
                            S.op("dve", I("scalar_tensor_tensor",
                                out=t1[:, gp, :], in0=pf[b2][:, gp * 128:(gp + 1) * 128],
                                scalar=gpp[:, 16 + gp:17 + gp], in1=bias2[:, gp, :], op0=ALU.mult, op1=ALU.add),
                                reads=[pf_res[b2], c_res, cs_res], writes=[t1_res])
                        tsl = slice(t * 128, (t + 1) * 128)
                        gur = [gu_res[c][t] for c in range(4)]
                        S.op("pool", I("tensor_tensor", out=guT[:, :, tsl], in0=t1[:], in1=guT[:, :, tsl],
                                                               op=ALU.mult),
                             reads=[t1_res] + gur, writes=gur)
                        qi = sqring.next()
                        S.op("pool", I("tensor_tensor", out=sq[qi][:], in0=guT[:, :, tsl], in1=guT[:, :, tsl],
                                                               op=ALU.mult),
                             reads=gur, writes=[sq_res[qi]])
                        b3 = ringTiny.next()
                        S.op("pe", [I("matmul", out=pf[b3][:, 0:1], lhsT=sq[qi][:, gp, :],
                                                              rhs=ones_bf[:, 0:1], start=(gp == 0), stop=(gp == 3))
                                    for gp in range(4)],
                             reads=[sq_res[qi], cs_res], writes=[pf_res[b3]])
                        S.op("dve", I("tensor_copy", out=ssqS[:, t:t + 1], in_=pf[b3][:, 0:1]),
                             reads=[pf_res[b3]], writes=[ssqS_res[t]])
                    proj_tm(tt, 2048, ev_sg)
                for hp in range(4):
                    def ev_k(bk, hp=hp):
                        S.op("act", I("activation", out=KTb[hp][:, gsl], in_=pf[bk][:], func=AF.Copy),
                             reads=[pf_res[bk]], writes=[KT_res[hp][g]])
                    proj_fm(512 + hp * 128, ev_k)
                for hp in range(4):
                    def ev_q(bk, hp=hp):
                        S.op("dve", I("tensor_scalar", out=QT[:, hp, gsl], in0=pf[bk][:], scalar1=0.125,
                                                              scalar2=None, op0=ALU.mult),
                             reads=[pf_res[bk]], writes=[QT_res[hp][g]])
                    proj_fm(hp * 128, ev_q)

            def attn_rb(hps, rb):
                for hp in hps:
                    if rb == 0:
                        S.marks.append(("s%d attn hp%d" % (s, hp), S.nidx))
                    abuf, ares = attn_buf(hp)
                    ob = ringO.next()
                    for r in range(rb * 8, rb * 8 + 8):
                        rs = min(max(r - 4, 0), ROWS - 8)
                        dr0 = rs - r + 7
                        g_q = r // 8
                        if r % 4 == 0:
                            bb = BDring.next()
                            S.op("pool", [I("tensor_copy", out=BDall[0:64, bb * 4:bb * 4 + 4, 0:64],
                                            in_=QT[0:64, hp, r * 64:(r + 4) * 64].rearrange("p (r q) -> p r q", r=4)),
                                          I("tensor_copy", out=BDall[64:128, bb * 4:bb * 4 + 4, 64:128],
                                            in_=QT[64:128, hp, r * 64:(r + 4) * 64].rearrange("p (r q) -> p r q", r=4))],
                                 reads=[QT_res[hp][g_q]], writes=[BD_res[bb]])
                        bslot = bb * 4 + (r % 4)
                        kgs = sorted(set([(rs * 64) // 512, (rs * 64 + 511) // 512]))
                        sbk = ringS.next()
                        S.op("pe", I("matmul", out=pf[sbk][:], lhsT=BDall[:, bslot, :],
                                     rhs=KTb[hp][:, rs * 64:rs * 64 + 512], start=True, stop=True),
                             reads=[BD_res[bb]] + [KT_res[hp][k] for k in kgs],
                             writes=[pf_res[sbk]])
                        S.op("dve", I("tensor_tensor", out=pf[sbk][:], in0=pf[sbk][:],
                                      in1=Bias[:, hp, dr0 * 64:dr0 * 64 + 512], op=ALU.add),
                             reads=[pf_res[sbk], c_res], writes=[pf_res[sbk]])
                        ss = sring.next()
                        S.op("dve", I("tensor_reduce", out=stt[:, ss, 0:1], in_=pf[sbk][:],
                                                              axis=mybir.AxisListType.X, op=ALU.max, negate=True),
                             reads=[pf_res[sbk]], writes=[stt_res[ss]])
                        pi = Ppring.next()
                        S.op("act", I("activation", out=Pp[pi][:, 64:576], in_=pf[sbk][:], func=AF.Exp,
                                                           bias=stt[:, ss, 0:1], scale=1.0,
                                                           accum_out=stt[:, ss, 1:2]),
                             reads=[pf_res[sbk], stt_res[ss]], writes=[Pp_res[pi], stt_res[ss]])
                        S.op("dve", I("reciprocal", out=stt[:, ss, 2:3], in_=stt[:, ss, 1:2]),
                             reads=[stt_res[ss]], writes=[stt_res[ss]])
                        S.op("pool", I("tensor_scalar", out=Pp[pi][:, 64:576], in0=Pp[pi][:, 64:576],
                                                               scalar1=stt[:, ss, 2:3], scalar2=1.0,
                                                               op0=ALU.mult, op1=ALU.mult),
                             reads=[stt_res[ss], Pp_res[pi]], writes=[Pp_res[pi]])
                        if rs % 2 == 0:
                            nch, c0, t0 = 4, 64, rs // 2
                        else:
                            nch, c0, t0 = 5, 0, (rs - 1) // 2
                        tb = ringPT.next()
                        S.op("pe", [I("transpose", out=pb[tb][:, c * 128:(c + 1) * 128],
                                                               in_=Pp[pi][:, c0 + c * 128:c0 + (c + 1) * 128],
                                                               identity=ident_bf[:])
                                    for c in range(nch)],
                             reads=[Pp_res[pi], cs_res], writes=[pb_res[tb]])
                        ti = PTring.next()
                        S.op("act", I("activation", out=PTs[ti][:, 0:nch * 128], in_=pb[tb][:, 0:nch * 128],
                                                           func=AF.Copy),
                             reads=[pb_res[tb]], writes=[PTs_res[ti]])
                        oc = (r % 8) * 64
                        fns = []
                        for hh in range(2):
                            for c in range(nch):
                                fns.append(I("matmul",
                                    out=pf[ob][hh * 64:(hh + 1) * 64, oc:oc + 64],
                                    lhsT=V[:, t0 + c, (2 * hp + hh) * 64:(2 * hp + hh + 1) * 64],
                                    rhs=PTs[ti][:, c * 128 + hh * 64:c * 128 + (hh + 1) * 64],
                                    start=(c == 0), stop=(c == nch - 1)))
                        S.op("pe", fns, reads=[PTs_res[ti]] + [V_res[t0 + c] for c in range(nch)],
                             writes=[pf_res[ob]])
                    S.op("dve", I("tensor_copy", out=abuf[:, rb * 512:(rb + 1) * 512], in_=pf[ob][:]),
                         reads=[pf_res[ob]], writes=[ares[rb]])

            def p1e(t):
                if t % 4 == 0:
                    S.marks.append(("s%d P1e t%d" % (s, t), S.nidx))
                tsl = slice(t * 128, (t + 1) * 128)
                g_t = t // 4
                qi = sqring.next()
                for hp in range(4):
                    abuf, ares = attn_buf(hp)
                    S.op("act", I("activation", out=sq[qi][:, hp, :], in_=abuf[:, tsl],
                                                                         func=AF.Square),
                         reads=[ares[g_t]], writes=[sq_res[qi]])
                b3 = ringTiny.next()
                S.op("pe", [I("matmul", out=pf[b3][:, 0:1], lhsT=sq[qi][:, hp, :], rhs=ones_bf[:, 0:1],
                                                      start=(hp == 0), stop=(hp == 3)) for hp in range(4)],
                     reads=[sq_res[qi], cs_res], writes=[pf_res[b3]])
                ss = sring.next()
                S.op("dve", I("tensor_scalar", out=stt[:, ss, 0:1], in0=pf[b3][:, 0:1], scalar1=1.0 / 512,
                                                      scalar2=EPS, op0=ALU.mult, op1=ALU.add),
                     reads=[pf_res[b3]], writes=[stt_res[ss]])
                S.op("dve", I("tensor_scalar", out=stt[:, ss, 1:2], in0=ssqS[:, t:t + 1], scalar1=1.0 / 512,
                                                      scalar2=EPS, op0=ALU.mult, op1=ALU.add),
                     reads=[ssqS_res[t]], writes=[stt_res[ss]])
                S.op("pool", I("tensor_tensor", out=stt[:, ss, 2:4], in0=stt[:, ss, 0:2], in1=neghalf[:, 0:2],
                                                       op=ALU.pow),
                     reads=[stt_res[ss], cs_res], writes=[stt_res[ss]])
                yi = yring.next()
                for c in range(2):
                    csl = slice(c * 512, (c + 1) * 512)
                    bA = ringW.next()
                    fa = []
                    for hp in range(4):
                        abuf, ares = attn_buf(hp)
                        fa.append(I("matmul", out=pf[bA][:], lhsT=abuf[:, tsl],
                                                                       rhs=Wout[:, hp, csl], start=(hp == 0),
                                                                       stop=(hp == 3)))
                    S.op("pe", fa, reads=[attn_buf(hp)[1][g_t] for hp in range(4)] + [c_res],
                         writes=[pf_res[bA]])
                    bB = ringW.next()
                    S.op("pe", [I("matmul", out=pf[bB][:], lhsT=guT[:, gp, tsl],
                                                          rhs=Wout[:, 4 + gp, csl], start=(gp == 0), stop=(gp == 3))
                                for gp in range(4)],
                         reads=[gu_res[gp][t] for gp in range(4)] + [c_res], writes=[pf_res[bB]])
                    S.op("act", I("activation", out=yt[yi][:, csl], in_=pf[bA][:], func=AF.Copy,
                                                            scale=stt[:, ss, 2:3]),
                         reads=[pf_res[bA], stt_res[ss]], writes=[yt_res[yi][c]])
                    S.op("dve", I("scalar_tensor_tensor", out=yt[yi][:, csl], in0=pf[bB][:],
                                                                      scalar=stt[:, ss, 3:4], in1=yt[yi][:, csl],
                                                                      op0=ALU.mult, op1=ALU.add),
                         reads=[pf_res[bB], stt_res[ss], yt_res[yi][c]], writes=[yt_res[yi][c]])
                s2 = sring.next()
                jk, jr = junk()
                S.op("act", I("activation", out=jk[:], in_=yt[yi][:], func=AF.Square,
                                                   accum_out=stt[:, s2, 0:1]),
                     reads=yt_res[yi], writes=[stt_res[s2], jr])
                r_ap = rstd_from_ssq(stt[:, s2, 0:1], stt_res[s2], float(D), s2, 1)
                S.op("dve", I("scalar_tensor_tensor", out=yt[yi][:], in0=yt[yi][:], scalar=r_ap,
                                                             in1=gpost_bc[:], op0=ALU.mult, op1=ALU.mult),
                     reads=yt_res[yi] + [stt_res[s2], c_res], writes=yt_res[yi])
                xi = xrring.next()
                S.dma(xr_sem[xi], I("dma_start", out=xr[xi][:], in_=x_d[tok0 + t * 128:tok0 + (t + 1) * 128, :]),
                      writes=[xr_res[xi]])
                S.op("pool", I("tensor_tensor", out=xr[xi][:], in0=yt[yi][:], in1=xr[xi][:], op=ALU.add),
                     reads=yt_res[yi] + [xr_res[xi]], writes=[xr_res[xi]])
                S.dma(xr_sem[xi], I("dma_start", out=x1s_d[tok0 + t * 128:tok0 + (t + 1) * 128, :],
                                                         in_=xr[xi][:]),
                      reads=[xr_res[xi]])

            for g in range(4):
                p1ab(g)
            for hp in range(4):
                for rb in range(4):
                    attn_rb([hp], rb)
            for t in range(16):
                p1e(t)

        S.final_wait("sp", xr_res)
        S.flush()
        ph1.close()

        W1 = sb("W1", (128, 8, 4096), BF16)
        W2 = sb("W2", (128, 32, 1024), BF16)
        gpre2_bc = sb("gpre2_bc_s", (128, D), F32)
        gpost2_bc = sb("gpost2_bc_s", (128, D), F32)
        w1_res = [[Res(True), Res(True)] for _ in range(8)]
        w2_res = [[Res(True), Res(True)] for _ in range(8)]
        c2_res = Res(const=True)
        wst = [sb("wst%d" % i, (128, 2048), F32) for i in range(3)]
        wst_res = [Res(), Res(), Res()]
        wst_sem = [S.new_sem("wst0"), S.new_sem("wst1"), S.new_sem("wst2")]
        wring = Ring([0, 1, 2])
        c2sem = S.new_sem("c2sem")
        S.dma(c2sem, I("dma_start", out=gpre2_bc[:], in_=gpre2_bc_d[:, :]), writes=[c2_res])
        S.dma(c2sem, I("dma_start", out=gpost2_bc[:], in_=gpost2_bc_d[:, :]), writes=[c2_res])

        NX2 = 2
        x2 = [sb("x2_%d" % i, (128, D), F32) for i in range(NX2)]
        x2_res = [Res() for _ in range(NX2)]
        x2_sem = [S.new_sem("x2_%d" % i) for i in range(NX2)]
        x2ring = Ring(list(range(NX2)))
        Tt = [sb("Tt%d" % i, (128, D), F32) for i in range(2)]
        Tt_resh = [[Res(), Res()], [Res(), Res()]]
        Tt_sem = [S.new_sem("Tt0"), S.new_sem("Tt1")]
        Tring = Ring([0, 1])
        hb2 = [sb("hb2_%d" % i, (128, D), BF16) for i in range(4)]
        hb2_res = [Res() for _ in range(4)]
        hb2ring = Ring([0, 1, 2, 3])
        h2T = [sb("h2T%d" % i, (128, 8, 256), BF16) for i in range(2)]
        h2T_res = [[Res(), Res()] for _ in range(2)]
        h2ring = Ring([0, 1])
        NR = 4
        rtmp = [sb("rtmp%d" % i, (128, 256), F32) for i in range(2)]
        rtmp_res = [Res(), Res()]
        rtring = Ring([0, 1])
        rT = [sb("rT%d" % i, (128, 256), BF16) for i in range(NR)]
        rT_res = [Res() for _ in range(NR)]
        rTring = Ring(list(range(NR)))
        stt2 = sb("stt2", (128, NS, 16), F32)
        stt2_res = [Res() for _ in range(NS)]
        s2ring = Ring(list(range(NS)))
        f1ring = Ring([(0, 0), (1, 0), (6, 0)])
        ringT = Ring([7])
        f1_res = {(0, 0): pf_res[0], (1, 0): pf_res[1], (6, 0): pf_res[6]}

        for i in range(8):
            for half in range(2):
                wi = wring.next()
                S.dma(wst_sem[wi], I("dma_start",
                    out=wst[wi][:].rearrange("p (k c) -> p k c", k=4),
                    in_=w1_d[half * 512:(half + 1) * 512, i * 512:(i + 1) * 512].rearrange("(k p) c -> p k c", p=128)),
                    writes=[wst_res[wi]])
                plain_cast(cast_engs.next(), W1[:, half * 4:(half + 1) * 4, i * 512:(i + 1) * 512],
                           wst[wi][:].rearrange("p (k c) -> p k c", k=4), reads=[wst_res[wi]], writes=[w1_res[i][half]])
            for half in range(2):
                wi = wring.next()
                r0 = i * 512 + half * 256
                S.dma(wst_sem[wi], I("dma_start",
                    out=wst[wi][:].rearrange("p (k c) -> p k c", k=2),
                    in_=w2_d[r0:r0 + 256, :].rearrange("(k p) c -> p k c", p=128)),
                    writes=[wst_res[wi]])
                plain_cast(cast_engs.next(), W2[:, i * 4 + half * 2:i * 4 + half * 2 + 2, :],
                           wst[wi][:].rearrange("p (k c) -> p k c", k=2), reads=[wst_res[wi]], writes=[w2_res[i][half]])

        def rstd2(ssq_ap, res, n, slot, col):
            v_ap = stt2[:, slot, col:col + 1]
            r_ap = stt2[:, slot, col + 1:col + 2]
            S.op("dve", I("tensor_scalar", out=v_ap, in0=ssq_ap, scalar1=1.0 / n, scalar2=EPS,
                                                  op0=ALU.mult, op1=ALU.add), reads=[res], writes=[stt2_res[slot]])
            S.op("pool", I("tensor_tensor", out=r_ap, in0=v_ap, in1=neghalf[:, 0:1], op=ALU.pow),
                 reads=[stt2_res[slot], cs_res], writes=[stt2_res[slot]])
            return r_ap

        NG2 = NTILE // 2
        NXA = 4
        xa = [sb("xa%d" % i, (128, D), F32) for i in range(NXA)]
        xa_res = [Res() for _ in range(NXA)]
        xa_sem = [S.new_sem("xa%d" % i) for i in range(NXA)]
        xaring = Ring(list(range(NXA)))
        prep_state = {}

        def prep_load(G):
            hi2 = h2ring.next()
            xs = []
            for tt in range(2):
                row0 = (2 * G + tt) * 128
                xi = xaring.next()
                S.dma(xa_sem[xi], I("dma_start", out=xa[xi][:], in_=x1s_d[row0:row0 + 128, :]),
                      writes=[xa_res[xi]])
                xs.append(xi)
            prep_state[G] = dict(hi2=hi2, xs=xs, his=[])

        def prep_norm(G):
            ps = prep_state[G]
            for tt in range(2):
                xi = ps["xs"][tt]
                ss = s2ring.next()
                jk, jr = junk()
                S.op("act", I("activation", out=jk[:], in_=xa[xi][:], func=AF.Square,
                              accum_out=stt2[:, ss, 0:1]),
                     reads=[xa_res[xi]], writes=[stt2_res[ss], jr])
                r_ap = rstd2(stt2[:, ss, 0:1], stt2_res[ss], float(D), ss, 1)
                hi = hb2ring.next()
                S.op("dve", I("scalar_tensor_tensor", out=hb2[hi][:], in0=xa[xi][:], scalar=r_ap, in1=gpre2_bc[:],
                              op0=ALU.mult, op1=ALU.mult),
                     reads=[xa_res[xi], stt2_res[ss], c2_res], writes=[hb2_res[hi]])
                ps["his"].append(hi)

        def prep_tr(G):
            ps = prep_state[G]
            hi2 = ps["hi2"]
            for tt in range(2):
                hi = ps["his"][tt]
                tb = ringT.next()
                top = S.op("pe", [I("transpose", out=pb[tb][:, kc * 128:(kc + 1) * 128],
                                    in_=hb2[hi][:, kc * 128:(kc + 1) * 128], identity=ident_bf[:]) for kc in range(8)],
                           reads=[hb2_res[hi], cs_res], writes=[pb_res[tb]])
                if ff_anchor[0] is not None:
                    top.odeps.append(ff_anchor[0])
                S.op("act", I("activation", out=h2T[hi2][:, :, tt * 128:(tt + 1) * 128],
                              in_=pb[tb].rearrange("p (k c) -> p k c", k=8), func=AF.Copy),
                     reads=[pb_res[tb]], writes=[h2T_res[hi2][tt]])

        ff_anchor = [None]
        prep_load(0)
        prep_norm(0)
        prep_tr(0)
        for G in range(NG2):
            hi2 = prep_state[G]["hi2"]
            if G + 1 < NG2:
                prep_load(G + 1)
            acc = [[ringB.next() for c in range(2)] for tt in range(2)]

            def ff1(j):
                fb, fo = f1ring.next()
                fr = f1_res[(fb, fo)]
                ff_anchor[0] = S.op("pe", [I("matmul", out=pf[fb][:, fo:fo + 256],
                                             lhsT=W1[:, kc, j * 128:(j + 1) * 128], rhs=h2T[hi2][:, kc, :],
                                             start=(kc == 0), stop=(kc == 7)) for kc in range(8)],
                                    reads=h2T_res[hi2] + w1_res[j // 4], writes=[fr])
                ri = rtring.next()
                S.op("act", I("activation", out=rtmp[ri][:], in_=pf[fb][:, fo:fo + 256], func=AF.Relu),
                     reads=[fr], writes=[rtmp_res[ri]])
                qi = rTring.next()
                S.op("dve", I("tensor_tensor", out=rT[qi][:], in0=rtmp[ri][:], in1=pf[fb][:, fo:fo + 256],
                                                      op=ALU.mult),
                     reads=[rtmp_res[ri], fr], writes=[rT_res[qi]])
                return qi

            def ff2(j, qi):
                fns = []
                for tt in range(2):
                    for c in range(2):
                        fns.append(I("matmul",
                            out=pf[acc[tt][c]][:], lhsT=rT[qi][:, tt * 128:(tt + 1) * 128],
                            rhs=W2[:, j, c * 512:(c + 1) * 512], start=(j == 0), stop=(j == 31)))
                S.op("pe", fns, reads=[rT_res[qi], w2_res[j // 4][(j % 4) // 2]],
                     writes=[pf_res[acc[tt][c]] for tt in range(2) for c in range(2)])

            LAG = 2
            pend = []
            for j in range(32):
                pend.append((j, ff1(j)))
                if len(pend) > LAG:
                    ff2(*pend.pop(0))
                if G + 1 < NG2 and j == 1:
                    prep_norm(G + 1)
                if G + 1 < NG2 and j == 22:
                    prep_tr(G + 1)
            while pend:
                ff2(*pend.pop(0))

            tis = [Tring.next() for tt in range(2)]
            for tt in range(2):
                ti = tis[tt]
                for c in range(2):
                    csl = slice(c * 512, (c + 1) * 512)
                    if c == 0:
                        S.op("act", I("activation", out=Tt[ti][:, csl], in_=pf[acc[tt][c]][:], func=AF.Copy),
                             reads=[pf_res[acc[tt][c]]], writes=[Tt_resh[ti][c]])
                    else:
                        S.op("dve", I("tensor_copy", out=Tt[ti][:, csl], in_=pf[acc[tt][c]][:]),
                             reads=[pf_res[acc[tt][c]]], writes=[Tt_resh[ti][c]])
            for tt in range(2):
                row0 = (2 * G + tt) * 128
                ti = tis[tt]
                ss = s2ring.next()
                for c in range(2):
                    csl = slice(c * 512, (c + 1) * 512)
                    jk, jr = junk()
                    S.op("act", I("activation", out=jk[:, 0:512], in_=Tt[ti][:, csl], func=AF.Square,
                                  accum_out=stt2[:, ss, c:c + 1]),
                         reads=[Tt_resh[ti][c]], writes=[stt2_res[ss], jr])
                S.op("dve", I("tensor_tensor", out=stt2[:, ss, 2:3], in0=stt2[:, ss, 0:1],
                              in1=stt2[:, ss, 1:2], op=ALU.add),
                     reads=[stt2_res[ss]], writes=[stt2_res[ss]])
                r_ap = rstd2(stt2[:, ss, 2:3], stt2_res[ss], float(D), ss, 3)
                for c in range(2):
                    csl = slice(c * 512, (c + 1) * 512)
                    S.op("dve", I("scalar_tensor_tensor", out=Tt[ti][:, csl], in0=Tt[ti][:, csl], scalar=r_ap,
                                  in1=gpost2_bc[:, csl], op0=ALU.mult, op1=ALU.mult),
                         reads=[Tt_resh[ti][c], stt2_res[ss], c2_res], writes=[Tt_resh[ti][c]])
                xi = x2ring.next()
                S.dma(x2_sem[xi], I("dma_start", out=x2[xi][:], in_=x1s_d[row0:row0 + 128, :]),
                      writes=[x2_res[xi]])
                S.op("pool", I("tensor_tensor", out=Tt[ti][:], in0=Tt[ti][:], in1=x2[xi][:], op=ALU.add),
                     reads=Tt_resh[ti] + [x2_res[xi]], writes=Tt_resh[ti])
                S.dma(Tt_sem[ti], I("dma_start", out=out_d[row0:row0 + 128, :], in_=Tt[ti][:]),
                      reads=Tt_resh[ti])
        S.final_wait("sp", Tt_resh[0] + Tt_resh[1])
        S.flush()
    return nc


def _prep_shared(inp):
    f = np.float32
    c = {}
    c["w_in"] = np.ascontiguousarray(inp["w_in"][0], dtype=f)
    c["w_out"] = np.ascontiguousarray(inp["w_out"][0], dtype=f)
    c["w_ff1"] = np.ascontiguousarray(inp["w_ff1"][0], dtype=f)
    c["w_ff2"] = np.ascontiguousarray(inp["w_ff2"][0], dtype=f)
    c["gpre_pp"] = np.ascontiguousarray(inp["norm_mix_pre"][0].reshape(8, 128).T, dtype=f)
    gmix = np.concatenate([inp["g_out_na"][0], inp["g_out_sg"][0]])
    c["gmix_pp"] = np.ascontiguousarray(gmix.reshape(8, 128).T, dtype=f)
    c["lng_pp"] = np.ascontiguousarray(inp["sg_ln_g"][0].reshape(4, 128).T, dtype=f)
    c["gpost_bc"] = np.ascontiguousarray(np.broadcast_to(inp["norm_mix_post"][0][None, :], (128, D)), dtype=f)
    c["gpre2_bc"] = np.ascontiguousarray(np.broadcast_to(inp["norm_ffn_pre"][0][None, :], (128, D)), dtype=f)
    c["gpost2_bc"] = np.ascontiguousarray(np.broadcast_to(inp["norm_ffn_post"][0][None, :], (128, D)), dtype=f)
    rpb = np.asarray(inp["na_rpb"][0], dtype=f)
    cols = np.arange(GRID_W)
    dc_idx = np.clip(cols[None, :] - cols[:, None], -15, 15) + 15
    col_bias = rpb[:, :, dc_idx]
    relb = np.transpose(col_bias, (0, 2, 1, 3)).reshape(4, 2 * 64, 15 * 64)
    c["relb"] = np.ascontiguousarray(relb, dtype=f)
    col_start = np.clip(cols - 8, 0, GRID_W - 16)
    inwin = (cols[None, :] >= col_start[:, None]) & (cols[None, :] < col_start[:, None] + 16)
    m = np.where(inwin, 0.0, NEG).astype(f)
    m = np.broadcast_to(m[None, :, None, :], (2, 64, 15, 64)).reshape(128, 960)
    c["mask"] = np.ascontiguousarray(m, dtype=f)
    ws = np.asarray(inp["sg_w_s"][0], dtype=f)
    c["wsT"] = np.ascontiguousarray(np.transpose(ws, (2, 0, 1)), dtype=f)
    c["lnb_bc"] = np.ascontiguousarray(np.broadcast_to(inp["sg_ln_b"][0][None, :], (128, 512)), dtype=f)
    bs = np.asarray(inp["sg_b_s"][0], dtype=f)
    bsb = np.broadcast_to(bs.reshape(4, 2, 1, 128), (4, 2, 64, 128))
    c["bs_bc"] = np.ascontiguousarray(np.transpose(bsb.reshape(4, 128, 128), (1, 0, 2)), dtype=f)
    c["ident"] = np.eye(128, dtype=f)
    return c


_NC_CACHE = {}


def kernel(**inputs):
    x = np.asarray(inputs["x"], dtype=np.float32)
    B = x.shape[0]
    per = B // NCORES
    if per not in _NC_CACHE:
        _NC_CACHE[per] = build(nseq=per)
    nc = _NC_CACHE[per]
    shared = _prep_shared(inputs)
    in_maps = []
    for c in range(NCORES):
        m = dict(shared)
        m["x"] = np.ascontiguousarray(x[c * per:(c + 1) * per].reshape(per * SEQ, D))
        in_maps.append(m)
    res = run_bass_kernel_spmd(nc, in_maps, core_ids=list(range(NCORES)))
    outs = [np.asarray(r["out"], dtype=np.float32).reshape(per, SEQ, D) for r in res.results]
    return np.concatenate(outs, axis=0)
```

```python
import numpy as np
from contextlib import ExitStack
import concourse.bass as bass
import concourse.mybir as mybir
from concourse.bass_utils import run_bass_kernel_spmd
from concourse.alu_op_type import AluOpType as ALU

AF = mybir.ActivationFunctionType
F32, BF16 = mybir.dt.float32, mybir.dt.bfloat16

NCORES = 8
D = 1024
SEQ = 2048
NSEQ = 4
GRID_W = 64
ROWS = 32
EPS = 1e-6
NEG = -30000.0
ENGS = ("pe", "act", "dve", "pool", "sp")
CFG = dict(ringS=[0, 1, 4], ringPT=[5, 6, 7], ringO=[2, 3], ringA=[0, 1, 4, 5], ringT=[6, 7], ringTiny=[2, 3], sem_lat=350.0, prio="cp", prio2="prog")


class Res:
    __slots__ = ("w", "r", "const")

    def __init__(self, const=False):
        self.w = None
        self.r = []
        self.const = const


class Op:
    __slots__ = ("eng", "instrs", "deps", "odeps", "idx", "seg", "kind", "semkey", "val",
                 "occ", "lat", "n", "succ", "start", "finish", "cp", "pr")


def _free_elems(ap):
    try:
        sh = ap.shape
        n = 1
        for d in sh[1:]:
            n *= int(d)
        return n
    except Exception:
        return 512


def _estimate(eng, instrs):
    occ = 0.0
    for name, kw in instrs:
        if name == "matmul":
            occ += max(0.45 * _free_elems(kw["rhs"]) + 5, 0.8 * _free_elems(kw["lhsT"]) + 5)
        elif name == "transpose":
            occ += 107
        elif name == "dma_start":
            occ += 120
        elif eng == "act":
            occ += 220 + 0.85 * _free_elems(kw["in_"]) + (90 if kw.get("accum_out") is not None else 0)
        elif eng == "dve":
            key = "in_" if "in_" in kw else ("in0" if "in0" in kw else "ap")
            n = _free_elems(kw[key])
            occ += 110 + (8.0 if name == "reciprocal" else 1.05) * n
        elif eng == "pool":
            key = "in_" if "in_" in kw else ("in0" if "in0" in kw else "ap")
            occ += 260 + 1.9 * _free_elems(kw[key])
        else:
            occ += 100
    return occ


class Sched:
    SEM_LAT = 350.0

    def __init__(self, nc, st):
        self.nc = nc
        self.st = st
        self.semh = {}
        self.cnt = {}
        self.waited = {e: {} for e in ENGS}
        self.ops = []
        self.seg = 0
        self.nidx = 0
        self.last_dma = {}
        self.marks = []
        for e in ENGS:
            self.new_sem("e_" + e)

    def new_sem(self, key):
        self.semh[key] = self.st.enter_context(self.nc.semaphore(key))
        self.cnt[key] = 0
        return key

    def _new_op(self, eng, instrs, kind, reads, writes):
        o = Op()
        o.eng, o.instrs, o.kind = eng, instrs, kind
        o.idx = self.nidx
        self.nidx += 1
        o.seg = self.seg
        o.semkey = None
        o.val = None
        o.odeps = []
        deps = {}
        for r in reads:
            if r.w is not None:
                deps[id(r.w)] = r.w
        for w in writes:
            if w.w is not None:
                deps[id(w.w)] = w.w
            for x in w.r:
                deps[id(x)] = x
        o.deps = list(deps.values())
        for r in reads:
            if not r.const:
                r.r.append(o)
        for w in writes:
            w.w = o
            w.r = []
        self.ops.append(o)
        return o

    def op(self, eng, fns, reads=(), writes=()):
        if isinstance(fns, tuple):
            fns = [fns]
        o = self._new_op(eng, list(fns), "compute", reads, writes)
        o.semkey = "e_" + eng
        o.occ = _estimate(eng, o.instrs)
        o.lat = o.occ + CFG["sem_lat"]
        return o

    def dma(self, semkey, fn, reads=(), writes=(), eng="sp"):
        o = self._new_op(eng, [fn], "dma", reads, writes)
        o.semkey = semkey
        prev = self.last_dma.get(semkey)
        if prev is not None:
            o.odeps.append(prev)
        self.last_dma[semkey] = o
        nbytes = 4 * 128 * _free_elems(fn[1]["out"])
        o.occ = 120.0
        o.lat = 6000.0 + nbytes / 100.0
        return o

    def final_wait(self, eng, res_list):
        o = self._new_op(eng, [], "wait", (), ())
        deps = {}
        for r in res_list:
            if r.w is not None:
                deps[id(r.w)] = r.w
            for x in r.r:
                deps[id(x)] = x
        o.deps = list(deps.values())
        o.occ = 10.0
        o.lat = 10.0
        return o

    def _schedule(self, ops):
        import heapq
        seg = self.seg
        for o in ops:
            o.n = 0
            o.succ = []
            o.start = 0.0
            o.finish = 0.0
        for o in ops:
            for d in o.deps:
                if d.seg == seg:
                    d.succ.append(o)
                    o.n += 1
            for d in o.odeps:
                if d.seg == seg:
                    d.succ.append(o)
                    o.n += 1
        mode = CFG.get("prio%d" % self.seg, CFG.get("prio", "prog"))
        for o in reversed(ops):
            c = 0.0
            for s_ in o.succ:
                if s_.cp > c:
                    c = s_.cp
            o.cp = c + o.lat
        if mode == "prog":
            for o in ops:
                o.pr = o.idx
        elif mode == "cp":
            for o in ops:
                o.pr = -o.cp
        else:
            w = CFG.get("mixw", 1.0)
            for o in ops:
                o.pr = o.idx * CFG.get("idx_ns", 300.0) - w * o.cp
        free = {e: 0.0 for e in ENGS}
        future = {e: [] for e in ENGS}
        ready = {e: [] for e in ENGS}
        order = {e: [] for e in ENGS}

        def push(o):
            rt = 0.0
            for d in o.deps:
                if d.seg == seg and d.finish > rt:
                    rt = d.finish
            for d in o.odeps:
                if d.seg == seg and d.start > rt:
                    rt = d.start
            heapq.heappush(future[o.eng], (rt, o.idx, o))

        for o in ops:
            if o.n == 0:
                push(o)
        remaining = len(ops)
        while remaining:
            best = None
            for e in ENGS:
                f, r = future[e], ready[e]
                fe = free[e]
                while f and f[0][0] <= fe:
                    rt, idx, o = heapq.heappop(f)
                    heapq.heappush(r, (o.pr, idx, o))
                if r:
                    cand = (fe, r[0][0], e, 0)
                elif f:
                    cand = (f[0][0], f[0][1], e, 1)
                else:
                    continue
                if best is None or cand < best:
                    best = cand
            start, _, e, kind = best
            o = heapq.heappop(ready[e])[2] if kind == 0 else heapq.heappop(future[e])[2]
            o.start = start
            o.finish = start + o.lat
            free[e] = start + o.occ
            order[e].append(o)
            remaining -= 1
            for s_ in o.succ:
                s_.n -= 1
                if s_.n == 0:
                    push(s_)
        self.sim_span = max(free.values())
        return order

    def flush(self, name=None):
        ops = self.ops
        self.ops = []
        order = self._schedule(ops)
        seg = self.seg
        for e in ENGS:
            for o in order[e]:
                if o.kind == "compute":
                    self.cnt[o.semkey] += 1
                    o.val = self.cnt[o.semkey]
                elif o.kind == "dma":
                    self.cnt[o.semkey] += 16
                    o.val = self.cnt[o.semkey]
        queues = {}
        for e in ENGS:
            q = []
            wd = self.waited[e]
            for o in order[e]:
                need = {}
                for d in o.deps:
                    if d.seg != seg and d.kind != "dma":
                        continue
                    if d.val is None:
                        continue
                    if need.get(d.semkey, 0) < d.val:
                        need[d.semkey] = d.val
                for k, v in need.items():
                    if wd.get(k, 0) >= v:
                        continue
                    wd[k] = v
                    q.append(("wait", self.semh[k], v))
                if o.kind == "compute":
                    h = self.semh[o.semkey]
                    for ins in o.instrs[:-1]:
                        q.append(("ins", ins, None, 0))
                    q.append(("ins", o.instrs[-1], h, 1))
                elif o.kind == "dma":
                    q.append(("ins", o.instrs[0], self.semh[o.semkey], 16))
            queues[e] = q
        self.seg += 1
        with self.nc.Block() as blk:
            for ename, attr in (("pe", "tensor"), ("act", "scalar"), ("dve", "vector"),
                                ("pool", "gpsimd"), ("sp", "sync")):
                q = queues[ename]

                def body(e, q=q):
                    for it in q:
                        if it[0] == "wait":
                            e.wait_ge(it[1], it[2])
                        else:
                            (n_, kw_), h, inc = it[1], it[2], it[3]
                            r = getattr(e, n_)(**kw_)
                            if h is not None:
                                r.then_inc(h, inc)
                getattr(blk, attr)(body)


def I(name, **kw):
    return (name, kw)


class Ring:
    def __init__(self, items):
        self.items = items
        self.i = 0

    def next(self):
        it = self.items[self.i % len(self.items)]
        self.i += 1
        return it


def build(nseq=NSEQ, debug=False):
    nc = bass.Bass("TRN2", target_bir_lowering=False, dynamic_dma_scratch_size=1024)
    NT = nseq * SEQ
    NTILE = NT // 128

    def din(name, shape):
        return nc.dram_tensor(name, list(shape), F32, kind="ExternalInput").ap()

    x_d = din("x", (NT, D))
    win_d = din("w_in", (D, 2560))
    wout_d = din("w_out", (D, D))
    w1_d = din("w_ff1", (D, 4096))
    w2_d = din("w_ff2", (4096, D))
    gpre_pp_d = din("gpre_pp", (128, 8))
    gmix_pp_d = din("gmix_pp", (128, 8))
    lng_pp_d = din("lng_pp", (128, 4))
    gpost_bc_d = din("gpost_bc", (128, D))
    gpre2_bc_d = din("gpre2_bc", (128, D))
    gpost2_bc_d = din("gpost2_bc", (128, D))
    relb_d = din("relb", (4, 128, 960))
    mask_d = din("mask", (128, 960))
    wsT_d = din("wsT", (128, 8, 128))
    lnb_bc_d = din("lnb_bc", (128, 512))
    bs_bc_d = din("bs_bc", (128, 4, 128))
    ident_d = din("ident", (128, 128))
    x1s_d = nc.dram_tensor("x1s", [NT, D], F32).ap()
    out_d = nc.dram_tensor("out", [NT, D], F32, kind="ExternalOutput").ap()

    with ExitStack() as st:
        S = Sched(nc, st)

        def sb(name, shape, dt):
            return st.enter_context(nc.sbuf_tensor(name, list(shape), dt))

        pf = [st.enter_context(nc.psum_tensor("pf%d" % i, [128, 512], F32)) for i in range(8)]
        pf_res = [Res() for _ in range(8)]
        pb = {i: pf[i][:].bitcast(BF16) for i in range(8)}
        pb_res = pf_res
        ringA = Ring(CFG["ringA"])
        ringO = Ring(CFG["ringO"])
        ringTiny = Ring(CFG["ringTiny"])
        ringW = Ring([0, 1, 4, 5, 6, 7])
        ringS = Ring(CFG["ringS"])
        ringPT = Ring(CFG["ringPT"])
        ringB = Ring([2, 3, 4, 5])
        ringT = Ring(CFG["ringT"])

        ident_bf = sb("ident_bf", (128, 128), BF16)
        ones_bf = sb("ones_bf", (128, 2), BF16)
        neghalf = sb("neghalf", (128, 2), F32)
        junk_l = [sb("junk_act%d" % i, (128, 1024), BF16) for i in range(1)]
        junk_res = [Res() for _ in range(1)]
        junk_ring = Ring([0])

        def junk():
            i = junk_ring.next()
            return junk_l[i], junk_res[i]
        c_res = Res(const=True)

        ph1 = ExitStack()
        st.enter_context(ph1)

        def sb1(name, shape, dt):
            return ph1.enter_context(nc.sbuf_tensor(name, list(shape), dt))

        Win = sb1("Win", (128, 8, 2560), BF16)
        Wout = sb1("Wout", (128, 8, 1024), BF16)
        WsT = sb1("WsT", (128, 8, 128), BF16)
        gpost_bc = sb1("gpost_bc_s", (128, D), F32)
        Bias = sb1("Bias_s", (128, 4, 960), BF16)
        bias2 = sb1("bias2_s", (128, 4, 128), F32)
        gpp = sb1("gpp", (128, 24), F32)

        hT_l = [sb1("hT0", (128, 8, 512), BF16)]
        QT = sb1("QT", (128, 4, SEQ), BF16)
        KTb = [sb1("KT%d" % i, (128, SEQ), BF16) for i in range(4)]
        V = sb1("V", (128, 16, 512), BF16)
        guT = sb1("guT", (128, 4, SEQ), BF16)
        ssqS = sb1("ssqS", (128, 16), F32)

        hT_resl = [[Res() for _ in range(4)] for _ in range(2)]
        QT_res = [[Res() for _ in range(4)] for _ in range(4)]
        KT_res = [[Res() for _ in range(4)] for _ in range(4)]
        at0_res = [Res() for _ in range(4)]
        V_res = [Res() for _ in range(16)]
        gu_res = [[Res() for _ in range(16)] for _ in range(4)]
        ssqS_res = [Res() for _ in range(16)]

        def attn_buf(hp):
            return QT[:, hp, :], QT_res[hp]

        NX = 3
        xin = [sb1("xin%d" % i, (128, D), F32) for i in range(3)]
        xin_res = [Res() for _ in range(NX)]
        xin_sem = [S.new_sem("xin%d" % i) for i in range(NX)]
        xring = Ring(list(range(NX)))
        NXR = 3
        xr = [sb1("xr%d" % i, (128, D), F32) for i in range(NXR)]
        xr_res = [Res() for _ in range(NXR)]
        xr_sem = [S.new_sem("xr%d" % i) for i in range(NXR)]
        xrring = Ring(list(range(NXR)))
        NY = 2
        yt = [sb1("yt%d" % i, (128, D), F32) for i in range(NY)]
        yt_res = [[Res(), Res()] for _ in range(NY)]
        yring = Ring(list(range(NY)))
        hb = [sb1("hb%d" % i, (128, D), BF16) for i in range(2)]
        hb_res = [Res() for _ in range(3)]
        hbring = Ring([0, 1, 2])
        NP = 6
        Pp = [sb1("Pp%d" % i, (128, 640), BF16) for i in range(3)]
        Pp_res = [Res() for _ in range(NP)]
        Ppring = Ring(list(range(NP)))
        PTs = [sb1("PTs%d" % i, (128, 640), BF16) for i in range(2)]
        PTs_res = [Res() for _ in range(4)]
        PTring = Ring([0, 1, 2, 3])
        NBDB = 3
        BDall = sb1("BDall", (128, 4 * NBDB, 128), BF16)
        BD_res = [Res() for _ in range(NBDB)]
        BDring = Ring(list(range(NBDB)))
        gv = [sb1("gv%d" % i, (128, 512), F32) for i in range(2)]
        gv_res = [Res() for _ in range(3)]
        gvring = Ring([0, 1, 2])
        nrm = [sb1("nrm%d" % i, (128, 512), BF16) for i in range(2)]
        nrm_res = [Res() for _ in range(2)]
        nrmring = Ring([0, 1])
        t1 = sb1("t1", (128, 4, 128), F32)
        t1_res = Res()
        sq = [sb1("sq%d" % i, (128, 4, 128), BF16) for i in range(2)]
        sq_res = [Res() for _ in range(2)]
        sqring = Ring([0, 1])
        NS = 16
        stt = sb1("stt", (128, NS, 16), F32)
        stt_res = [Res() for _ in range(NS)]
        sring = Ring(list(range(NS)))

        csem = S.new_sem("csem")
        stg_cm = ExitStack()
        stg = [stg_cm.enter_context(nc.sbuf_tensor("stg%d" % i, [128, 2560], F32)) for i in range(2)]
        stg_res = [Res(), Res()]
        stg_sem = [S.new_sem("stg0"), S.new_sem("stg1")]
        stgring = Ring([0, 1])
        cast_engs = Ring(["act", "dve", "pool"])

        S.dma(csem, I("dma_start", out=gpp[:, 0:8], in_=gpre_pp_d[:, :]), writes=[c_res])
        S.dma(csem, I("dma_start", out=gpp[:, 8:16], in_=gmix_pp_d[:, :]), writes=[c_res])
        S.dma(csem, I("dma_start", out=gpp[:, 16:20], in_=lng_pp_d[:, :]), writes=[c_res])
        S.dma(csem, I("dma_start", out=gpost_bc[:], in_=gpost_bc_d[:, :]), writes=[c_res])
        cs_res = Res(const=True)
        S.op("dve", [I("memset", ap=ones_bf[:], constant=1.0),
                     I("memset", ap=neghalf[:], constant=-0.5)], writes=[cs_res])
        S.op("pool", I("memset", ap=BDall[:], constant=0.0), writes=BD_res)
        for i in range(3):
            S.op("pool", I("memset", ap=Pp[i][:], constant=0.0), writes=[Pp_res[i]])

        def scaled_cast(eng, out_ap, in_ap, sc_ap, reads, writes):
            if eng == "act":
                S.op("act", I("activation", out=out_ap, in_=in_ap, func=AF.Copy, scale=sc_ap),
                     reads=reads, writes=writes)
            elif eng == "dve":
                S.op("dve", I("tensor_scalar", out=out_ap, in0=in_ap, scalar1=sc_ap, scalar2=None,
                                                      op0=ALU.mult), reads=reads, writes=writes)
            else:
                S.op("pool", I("tensor_scalar", out=out_ap, in0=in_ap, scalar1=sc_ap, scalar2=1.0,
                                                       op0=ALU.mult, op1=ALU.mult), reads=reads, writes=writes)

        def plain_cast(eng, out_ap, in_ap, reads, writes):
            if eng == "act":
                S.op("act", I("activation", out=out_ap, in_=in_ap, func=AF.Copy),
                     reads=reads, writes=writes)
            elif eng == "dve":
                S.op("dve", I("tensor_copy", out=out_ap, in_=in_ap), reads=reads, writes=writes)
            else:
                S.op("pool", I("tensor_copy", out=out_ap, in_=in_ap), reads=reads, writes=writes)

        for kc in range(8):
            si = stgring.next()
            S.dma(stg_sem[si], I("dma_start",
                out=stg[si][:, 0:2560], in_=win_d[kc * 128:(kc + 1) * 128, :]), writes=[stg_res[si]])
            for hlf in range(2):
                scaled_cast(cast_engs.next(), Win[:, kc, hlf * 1280:(hlf + 1) * 1280],
                            stg[si][:, hlf * 1280:(hlf + 1) * 1280], gpp[:, kc:kc + 1],
                            reads=[stg_res[si], c_res], writes=[c_res] if False else [Res()])
        si = stgring.next()
        S.dma(stg_sem[si], I("dma_start", out=stg[si][:, 0:128], in_=ident_d[:, :]),
              writes=[stg_res[si]])
        S.op("dve", I("tensor_copy", out=ident_bf[:], in_=stg[si][:, 0:128]),
             reads=[stg_res[si]], writes=[cs_res])
        S.dma(stg_sem[si], I("dma_start", out=stg[si][:, 128:1152], in_=wsT_d.rearrange("p g q -> p (g q)")),
              writes=[stg_res[si]])
        S.op("dve", I("tensor_copy", out=WsT[:].rearrange("p g q -> p (g q)"), in_=stg[si][:, 128:1152]),
             reads=[stg_res[si]], writes=[cs_res])
        S.dma(stg_sem[si], I("dma_start", out=stg[si][:, 1152:1664], in_=lnb_bc_d[:, :]),
              writes=[stg_res[si]])
        S.dma(stg_sem[si], I("dma_start", out=stg[si][:, 1664:2176], in_=bs_bc_d.rearrange("p g q -> p (g q)")),
              writes=[stg_res[si]])
        for gp in range(4):
            for gg in range(2):
                g = 2 * gp + gg
                bk = ringA.next()
                S.op("pe", I("matmul",
                    out=pf[bk][:, 0:128], lhsT=stg[si][:, 1152 + gp * 128:1152 + (gp + 1) * 128],
                    rhs=stg[si][:, 128 + g * 128:128 + (g + 1) * 128], start=True, stop=True),
                    reads=[stg_res[si]], writes=[pf_res[bk]])
                S.op("dve", I("tensor_tensor",
                    out=bias2[gg * 64:(gg + 1) * 64, gp, :], in0=pf[bk][gg * 64:(gg + 1) * 64, 0:128],
                    in1=stg[si][gg * 64:(gg + 1) * 64, 1664 + gp * 128:1664 + (gp + 1) * 128], op=ALU.add),
                    reads=[pf_res[bk], stg_res[si]], writes=[cs_res])
        S.flush()
        stg_cm.close()
        hT_l.append(sb1("hT1", (128, 8, 512), BF16))
        hb.append(sb1("hb2", (128, D), BF16))
        Pp.append(sb1("Pp3", (128, 640), BF16))
        Pp.append(sb1("Pp4", (128, 640), BF16))
        Pp.append(sb1("Pp5", (128, 640), BF16))
        PTs.append(sb1("PTs3", (128, 640), BF16))
        PTs.append(sb1("PTs2", (128, 640), BF16))
        gv.append(sb1("gv2", (128, 512), F32))
        for i in (3, 4, 5):
            S.op("pool", I("memset", ap=Pp[i][:], constant=0.0), writes=[Pp_res[i]])
        hTring = Ring([0, 1])

        def rstd_from_ssq(ssq_ap, ssq_res, n, sres_slot, col):
            v_ap = stt[:, sres_slot, col:col + 1]
            r_ap = stt[:, sres_slot, col + 1:col + 2]
            S.op("dve", I("tensor_scalar", out=v_ap, in0=ssq_ap, scalar1=1.0 / n, scalar2=EPS,
                                                  op0=ALU.mult, op1=ALU.add),
                 reads=[ssq_res], writes=[stt_res[sres_slot]])
            S.op("pool", I("tensor_tensor", out=r_ap, in0=v_ap, in1=neghalf[:, 0:1], op=ALU.pow),
                 reads=[stt_res[sres_slot], cs_res], writes=[stt_res[sres_slot]])
            return r_ap

        def load_tile(dram_ap, row0):
            xi = xring.next()
            last_x_load[0] = S.dma(xin_sem[xi], I("dma_start", out=xin[xi][:], in_=dram_ap[row0:row0 + 128, :]),
                                   writes=[xin_res[xi]])
            return xi

        def norm_transpose(xi, dstT, dst_res, col0, gbc=None):
            ss = sring.next()
            jk, jr = junk()
            S.op("act", I("activation", out=jk[:], in_=xin[xi][:], func=AF.Square,
                                               accum_out=stt[:, ss, 0:1]),
                 reads=[xin_res[xi]], writes=[stt_res[ss], jr])
            r_ap = rstd_from_ssq(stt[:, ss, 0:1], stt_res[ss], float(D), ss, 1)
            hi = hbring.next()
            if gbc is None:
                S.op("dve", I("tensor_scalar", out=hb[hi][:], in0=xin[xi][:], scalar1=r_ap, scalar2=None,
                                                      op0=ALU.mult),
                     reads=[xin_res[xi], stt_res[ss]], writes=[hb_res[hi]])
            else:
                S.op("dve", I("scalar_tensor_tensor", out=hb[hi][:], in0=xin[xi][:], scalar=r_ap,
                                                             in1=gbc[:], op0=ALU.mult, op1=ALU.mult),
                     reads=[xin_res[xi], stt_res[ss], c_res], writes=[hb_res[hi]])
            tb = ringT.next()
            S.op("pe", [I("transpose", out=pb[tb][:, kc * 128:(kc + 1) * 128],
                                                     in_=hb[hi][:, kc * 128:(kc + 1) * 128], identity=ident_bf[:])
                        for kc in range(8)],
                 reads=[hb_res[hi], cs_res], writes=[pb_res[tb]])
            S.op("act", I("activation", out=dstT[:, :, col0:col0 + 128],
                                               in_=pb[tb].rearrange("p (k c) -> p k c", k=8), func=AF.Copy),
                 reads=[pb_res[tb]], writes=[dst_res])

        wout_res = [Res(const=True) for _ in range(8)]
        bias_res = [Res(const=True) for _ in range(4)]
        last_x_load = [None]

        def late_prep():
            first = True
            for kc in range(8):
                xi = kc % NXR
                o = S.dma(xr_sem[xi], I("dma_start", out=xr[xi][:], in_=wout_d[kc * 128:(kc + 1) * 128, :]),
                          writes=[xr_res[xi]])
                if first and last_x_load[0] is not None:
                    o.odeps.append(last_x_load[0])
                first = False
                scaled_cast(cast_engs.next(), Wout[:, kc, :], xr[xi][:], gpp[:, 8 + kc:9 + kc],
                            reads=[xr_res[xi], c_res], writes=[wout_res[kc]])
            xm = 2
            S.dma(xr_sem[xm], I("dma_start", out=xr[xm][:, 0:960], in_=mask_d[:, :]), writes=[xr_res[xm]])
            for hp in range(4):
                xb = hp % 2
                S.dma(xr_sem[xb], I("dma_start", out=xr[xb][:, 0:960], in_=relb_d[hp, :, :]), writes=[xr_res[xb]])
                S.op("dve", I("tensor_tensor", out=Bias[:, hp, :], in0=xr[xb][:, 0:960], in1=xr[xm][:, 0:960],
                              op=ALU.add),
                     reads=[xr_res[xb], xr_res[xm]], writes=[bias_res[hp]])

        for s in range(nseq):
            tok0 = s * SEQ
            def p1ab(g):
                S.marks.append(("s%d P1ab g%d" % (s, g), S.nidx))
                hbuf = hTring.next()
                hT = hT_l[hbuf]
                hT_res = hT_resl[hbuf]
                for tt in range(4):
                    xi = load_tile(x_d, tok0 + (4 * g + tt) * 128)
                    norm_transpose(xi, hT, hT_res[tt], tt * 128)
                gsl = slice(g * 512, (g + 1) * 512)

                def proj_fm(col0, evac):
                    bk = ringA.next()
                    S.op("pe", [I("matmul", out=pf[bk][:], lhsT=Win[:, kc, col0:col0 + 128],
                                                          rhs=hT[:, kc, :], start=(kc == 0), stop=(kc == 7))
                                for kc in range(8)],
                         reads=hT_res + [c_res], writes=[pf_res[bk]])
                    evac(bk)

                def proj_tm(tt, col0, evac):
                    bk = ringA.next()
                    S.op("pe", [I("matmul", out=pf[bk][:], lhsT=hT[:, kc, tt * 128:(tt + 1) * 128],
                                                          rhs=Win[:, kc, col0:col0 + 512], start=(kc == 0),
                                                          stop=(kc == 7))
                                for kc in range(8)],
                         reads=[hT_res[tt], c_res], writes=[pf_res[bk]])
                    evac(bk)

                for c in range(4):
                    def ev_u(bk, c=c):
                        S.op("act", I("activation", out=guT[:, c, gsl], in_=pf[bk][:],
                                                           func=AF.Gelu_apprx_tanh),
                             reads=[pf_res[bk]], writes=[gu_res[c][4 * g + k] for k in range(4)])
                    proj_fm(1536 + c * 128, ev_u)
                for tt in range(4):
                    t = 4 * g + tt

                    def ev_v(bk, t=t):
                        S.op("dve", I("tensor_copy", out=V[:, t, :], in_=pf[bk][:]),
                             reads=[pf_res[bk]], writes=[V_res[t]])
                    proj_tm(tt, 1024, ev_v)

                    def ev_sg(bk, t=t):
                        gi = gvring.next()
                        S.op("act", I("activation", out=gv[gi][:], in_=pf[bk][:], func=AF.Gelu_apprx_tanh),
                             reads=[pf_res[bk]], writes=[gv_res[gi]])
                        ss = sring.next()
                        S.op("dve", I("bn_stats", out=stt[:, ss, 0:6], in_=gv[gi][:]),
                             reads=[gv_res[gi]], writes=[stt_res[ss]])
                        S.op("dve", I("bn_aggr", out=stt[:, ss, 6:8], in_=stt[:, ss, 0:6]),
                             reads=[stt_res[ss]], writes=[stt_res[ss]])
                        S.op("dve", I("tensor_scalar", out=stt[:, ss, 8:9], in0=stt[:, ss, 7:8], scalar1=EPS,
                                                              scalar2=None, op0=ALU.add),
                             reads=[stt_res[ss]], writes=[stt_res[ss]])
                        S.op("pool", I("tensor_tensor", out=stt[:, ss, 9:10], in0=stt[:, ss, 8:9],
                                                               in1=neghalf[:, 0:1], op=ALU.pow),
                             reads=[stt_res[ss], cs_res], writes=[stt_res[ss]])
                        ni = nrmring.next()
                        S.op("dve", I("tensor_scalar", out=nrm[ni][:], in0=gv[gi][:], scalar1=stt[:, ss, 6:7],
                                                              scalar2=stt[:, ss, 9:10], op0=ALU.subtract,
                                                              op1=ALU.mult),
                             reads=[gv_res[gi], stt_res[ss]], writes=[nrm_res[ni]])
                        b2 = ringA.next()
                        S.op("pe", [I("matmul",
                            out=pf[b2][(gq % 2) * 64:(gq % 2 + 1) * 64, (gq // 2) * 128:(gq // 2 + 1) * 128],
                            lhsT=nrm[ni][:, gq * 64:(gq + 1) * 64], rhs=WsT[:, gq, :], start=True, stop=True)
                            for gq in range(8)],
                            reads=[nrm_res[ni], cs_res], writes=[pf_res[b2]])
                        for gp in range(4):
                            S.op("dve", I("scalar_tensor_tensor",
                                out=t1[:, gp, :], in0=pf[b2][:, gp * 128:(gp + 1) * 128],
                                scalar=gpp[:, 16 + gp:17 + gp], in1=bias2[:, gp, :], op0=ALU.mult, op1=ALU.add),
                                reads=[pf_res[b2], c_res, cs_res], writes=[t1_res])
                        tsl = slice(t * 128, (t + 1) * 128)
                        gur = [gu_res[c][t] for c in range(4)]
                        S.op("pool", I("tensor_tensor", out=guT[:, :, tsl], in0=t1[:], in1=guT[:, :, tsl],
                                                               op=ALU.mult),
                             reads=[t1_res] + gur, writes=gur)
                        qi = sqring.next()
                        S.op("pool", I("tensor_tensor", out=sq[qi][:], in0=guT[:, :, tsl], in1=guT[:, :, tsl],
                                                               op=ALU.mult),
                             reads=gur, writes=[sq_res[qi]])
                        b3 = ringTiny.next()
                        S.op("pe", [I("matmul", out=pf[b3][:, 0:1], lhsT=sq[qi][:, gp, :],
                                                              rhs=ones_bf[:, 0:1], start=(gp == 0), stop=(gp == 3))
                                    for gp in range(4)],
                             reads=[sq_res[qi], cs_res], writes=[pf_res[b3]])
                        S.op("dve", I("tensor_copy", out=ssqS[:, t:t + 1], in_=pf[b3][:, 0:1]),
                             reads=[pf_res[b3]], writes=[ssqS_res[t]])
                    proj_tm(tt, 2048, ev_sg)
                for hp in range(4):
                    def ev_k(bk, hp=hp):
                        S.op("act", I("activation", out=KTb[hp][:, gsl], in_=pf[bk][:], func=AF.Copy),
                             reads=[pf_res[bk]], writes=[KT_res[hp][g]])
                    proj_fm(512 + hp * 128, ev_k)
                for hp in range(4):
                    def ev_q(bk, hp=hp):
                        S.op("dve", I("tensor_scalar", out=QT[:, hp, gsl], in0=pf[bk][:], scalar1=0.125,
                                                              scalar2=None, op0=ALU.mult),
                             reads=[pf_res[bk]], writes=[QT_res[hp][g]])
                    proj_fm(hp * 128, ev_q)

            def attn_rb(hps, rb):
                for hp in hps:
                    if rb == 0:
                        S.marks.append(("s%d attn hp%d" % (s, hp), S.nidx))
                    abuf, ares = attn_buf(hp)
                    ob = ringO.next()
                    for r in range(rb * 8, rb * 8 + 8):
                        rs = min(max(r - 4, 0), ROWS - 8)
                        dr0 = rs - r + 7
                        g_q = r // 8
                        if r % 4 == 0:
                            bb = BDring.next()
                            S.op("pool", [I("tensor_copy", out=BDall[0:64, bb * 4:bb * 4 + 4, 0:64],
                                            in_=QT[0:64, hp, r * 64:(r + 4) * 64].rearrange("p (r q) -> p r q", r=4)),
                                          I("tensor_copy", out=BDall[64:128, bb * 4:bb * 4 + 4, 64:128],
                                            in_=QT[64:128, hp, r * 64:(r + 4) * 64].rearrange("p (r q) -> p r q", r=4))],
                                 reads=[QT_res[hp][g_q]], writes=[BD_res[bb]])
                        bslot = bb * 4 + (r % 4)
                        kgs = sorted(set([(rs * 64) // 512, (rs * 64 + 511) // 512]))
                        sbk = ringS.next()
                        S.op("pe", I("matmul", out=pf[sbk][:], lhsT=BDall[:, bslot, :],
                                     rhs=KTb[hp][:, rs * 64:rs * 64 + 512], start=True, stop=True),
                             reads=[BD_res[bb]] + [KT_res[hp][k] for k in kgs],
                             writes=[pf_res[sbk]])
                        S.op("dve", I("tensor_tensor", out=pf[sbk][:], in0=pf[sbk][:],
                                      in1=Bias[:, hp, dr0 * 64:dr0 * 64 + 512], op=ALU.add),
                             reads=[pf_res[sbk], bias_res[hp]], writes=[pf_res[sbk]])
                        ss = sring.next()
                        S.op("dve", I("tensor_reduce", out=stt[:, ss, 0:1], in_=pf[sbk][:],
                                                              axis=mybir.AxisListType.X, op=ALU.max, negate=True),
                             reads=[pf_res[sbk]], writes=[stt_res[ss]])
                        pi = Ppring.next()
                        S.op("act", I("activation", out=Pp[pi][:, 64:576], in_=pf[sbk][:], func=AF.Exp,
                                                           bias=stt[:, ss, 0:1], scale=1.0,
                                                           accum_out=stt[:, ss, 1:2]),
                             reads=[pf_res[sbk], stt_res[ss]], writes=[Pp_res[pi], stt_res[ss]])
                        S.op("dve", I("reciprocal", out=stt[:, ss, 2:3], in_=stt[:, ss, 1:2]),
                             reads=[stt_res[ss]], writes=[stt_res[ss]])
                        S.op("pool", I("tensor_scalar", out=Pp[pi][:, 64:576], in0=Pp[pi][:, 64:576],
                                                               scalar1=stt[:, ss, 2:3], scalar2=1.0,
                                                               op0=ALU.mult, op1=ALU.mult),
                             reads=[stt_res[ss], Pp_res[pi]], writes=[Pp_res[pi]])
                        if rs % 2 == 0:
                            nch, c0, t0 = 4, 64, rs // 2
                        else:
                            nch, c0, t0 = 5, 0, (rs - 1) // 2
                        tb = ringPT.next()
                        S.op("pe", [I("transpose", out=pb[tb][:, c * 128:(c + 1) * 128],
                                                               in_=Pp[pi][:, c0 + c * 128:c0 + (c + 1) * 128],
                                                               identity=ident_bf[:])
                                    for c in range(nch)],
                             reads=[Pp_res[pi], cs_res], writes=[pb_res[tb]])
                        ti = PTring.next()
                        S.op("act", I("activation", out=PTs[ti][:, 0:nch * 128], in_=pb[tb][:, 0:nch * 128],
                                                           func=AF.Copy),
                             reads=[pb_res[tb]], writes=[PTs_res[ti]])
                        oc = (r % 8) * 64
                        fns = []
                        for hh in range(2):
                            for c in range(nch):
                                fns.append(I("matmul",
                                    out=pf[ob][hh * 64:(hh + 1) * 64, oc:oc + 64],
                                    lhsT=V[:, t0 + c, (2 * hp + hh) * 64:(2 * hp + hh + 1) * 64],
                                    rhs=PTs[ti][:, c * 128 + hh * 64:c * 128 + (hh + 1) * 64],
                                    start=(c == 0), stop=(c == nch - 1)))
                        S.op("pe", fns, reads=[PTs_res[ti]] + [V_res[t0 + c] for c in range(nch)],
                             writes=[pf_res[ob]])
                    S.op("dve", I("tensor_copy", out=abuf[:, rb * 512:(rb + 1) * 512], in_=pf[ob][:]),
                         reads=[pf_res[ob]], writes=[ares[rb]])

            def p1e(t):
                if t % 4 == 0:
                    S.marks.append(("s%d P1e t%d" % (s, t), S.nidx))
                tsl = slice(t * 128, (t + 1) * 128)
                g_t = t // 4
                qi = sqring.next()
                for hp in range(4):
                    abuf, ares = attn_buf(hp)
                    S.op("act", I("activation", out=sq[qi][:, hp, :], in_=abuf[:, tsl],
                                                                         func=AF.Square),
                         reads=[ares[g_t]], writes=[sq_res[qi]])
                b3 = ringTiny.next()
                S.op("pe", [I("matmul", out=pf[b3][:, 0:1], lhsT=sq[qi][:, hp, :], rhs=ones_bf[:, 0:1],
                                                      start=(hp == 0), stop=(hp == 3)) for hp in range(4)],
                     reads=[sq_res[qi], cs_res], writes=[pf_res[b3]])
                ss = sring.next()
                S.op("dve", I("tensor_scalar", out=stt[:, ss, 0:1], in0=pf[b3][:, 0:1], scalar1=1.0 / 512,
                                                      scalar2=EPS, op0=ALU.mult, op1=ALU.add),
                     reads=[pf_res[b3]], writes=[stt_res[ss]])
                S.op("dve", I("tensor_scalar", out=stt[:, ss, 1:2], in0=ssqS[:, t:t + 1], scalar1=1.0 / 512,
                                                      scalar2=EPS, op0=ALU.mult, op1=ALU.add),
                     reads=[ssqS_res[t]], writes=[stt_res[ss]])
                S.op("pool", I("tensor_tensor", out=stt[:, ss, 2:4], in0=stt[:, ss, 0:2], in1=neghalf[:, 0:2],
                                                       op=ALU.pow),
                     reads=[stt_res[ss], cs_res], writes=[stt_res[ss]])
                yi = yring.next()
                for c in range(2):
                    csl = slice(c * 512, (c + 1) * 512)
                    bA = ringW.next()
                    fa = []
                    for hp in range(4):
                        abuf, ares = attn_buf(hp)
                        fa.append(I("matmul", out=pf[bA][:], lhsT=abuf[:, tsl],
                                                                       rhs=Wout[:, hp, csl], start=(hp == 0),
                                                                       stop=(hp == 3)))
                    S.op("pe", fa, reads=[attn_buf(hp)[1][g_t] for hp in range(4)] + wout_res[0:4],
                         writes=[pf_res[bA]])
                    bB = ringW.next()
                    S.op("pe", [I("matmul", out=pf[bB][:], lhsT=guT[:, gp, tsl],
                                                          rhs=Wout[:, 4 + gp, csl], start=(gp == 0), stop=(gp == 3))
                                for gp in range(4)],
                         reads=[gu_res[gp][t] for gp in range(4)] + wout_res[4:8], writes=[pf_res[bB]])
                    S.op("act", I("activation", out=yt[yi][:, csl], in_=pf[bA][:], func=AF.Copy,
                                                            scale=stt[:, ss, 2:3]),
                         reads=[pf_res[bA], stt_res[ss]], writes=[yt_res[yi][c]])
                    S.op("dve", I("scalar_tensor_tensor", out=yt[yi][:, csl], in0=pf[bB][:],
                                                                      scalar=stt[:, ss, 3:4], in1=yt[yi][:, csl],
                                                                      op0=ALU.mult, op1=ALU.add),
                         reads=[pf_res[bB], stt_res[ss], yt_res[yi][c]], writes=[yt_res[yi][c]])
                s2 = sring.next()
                jk, jr = junk()
                S.op("act", I("activation", out=jk[:], in_=yt[yi][:], func=AF.Square,
                                                   accum_out=stt[:, s2, 0:1]),
                     reads=yt_res[yi], writes=[stt_res[s2], jr])
                r_ap = rstd_from_ssq(stt[:, s2, 0:1], stt_res[s2], float(D), s2, 1)
                S.op("dve", I("scalar_tensor_tensor", out=yt[yi][:], in0=yt[yi][:], scalar=r_ap,
                                                             in1=gpost_bc[:], op0=ALU.mult, op1=ALU.mult),
                     reads=yt_res[yi] + [stt_res[s2], c_res], writes=yt_res[yi])
                xi = xrring.next()
                S.dma(xr_sem[xi], I("dma_start", out=xr[xi][:], in_=x_d[tok0 + t * 128:tok0 + (t + 1) * 128, :]),
                      writes=[xr_res[xi]])
                S.op("pool", I("tensor_tensor", out=xr[xi][:], in0=yt[yi][:], in1=xr[xi][:], op=ALU.add),
                     reads=yt_res[yi] + [xr_res[xi]], writes=[xr_res[xi]])
                S.dma(xr_sem[xi], I("dma_start", out=x1s_d[tok0 + t * 128:tok0 + (t + 1) * 128, :],
                                                         in_=xr[xi][:]),
                      reads=[xr_res[xi]])

            for g in range(4):
                p1ab(g)
                if s == 0 and g == 1:
                    late_prep()
            for hp in range(4):
                for rb in range(4):
                    attn_rb([hp], rb)
            for t in range(16):
                p1e(t)

        S.final_wait("sp", xr_res)
        S.flush()
        ph1.close()

        W1 = sb("W1", (128, 8, 4096), BF16)
        W2 = sb("W2", (128, 32, 1024), BF16)
        gpre2_bc = sb("gpre2_bc_s", (128, D), F32)
        gpost2_bc = sb("gpost2_bc_s", (128, D), F32)
        w1_res = [[Res(True), Res(True)] for _ in range(8)]
        w2_res = [[Res(True), Res(True)] for _ in range(8)]
        c2_res = Res(const=True)
        wst = [sb("wst%d" % i, (128, 2048), F32) for i in range(3)]
        wst_res = [Res(), Res(), Res()]
        wst_sem = [S.new_sem("wst0"), S.new_sem("wst1"), S.new_sem("wst2")]
        wring = Ring([0, 1, 2])
        c2sem = S.new_sem("c2sem")
        S.dma(c2sem, I("dma_start", out=gpre2_bc[:], in_=gpre2_bc_d[:, :]), writes=[c2_res])
        S.dma(c2sem, I("dma_start", out=gpost2_bc[:], in_=gpost2_bc_d[:, :]), writes=[c2_res])

        NX2 = 2
        x2 = [sb("x2_%d" % i, (128, D), F32) for i in range(NX2)]
        x2_res = [Res() for _ in range(NX2)]
        x2_sem = [S.new_sem("x2_%d" % i) for i in range(NX2)]
        x2ring = Ring(list(range(NX2)))
        Tt = [sb("Tt%d" % i, (128, D), F32) for i in range(2)]
        Tt_resh = [[Res(), Res()], [Res(), Res()]]
        Tt_sem = [S.new_sem("Tt0"), S.new_sem("Tt1")]
        Tring = Ring([0, 1])
        hb2 = [sb("hb2_%d" % i, (128, D), BF16) for i in range(4)]
        hb2_res = [Res() for _ in range(4)]
        hb2ring = Ring([0, 1, 2, 3])
        h2T = [sb("h2T%d" % i, (128, 8, 256), BF16) for i in range(2)]
        h2T_res = [[Res(), Res()] for _ in range(2)]
        h2ring = Ring([0, 1])
        NR = 4
        rtmp = [sb("rtmp%d" % i, (128, 256), F32) for i in range(2)]
        rtmp_res = [Res(), Res()]
        rtring = Ring([0, 1])
        rT = [sb("rT%d" % i, (128, 256), BF16) for i in range(NR)]
        rT_res = [Res() for _ in range(NR)]
        rTring = Ring(list(range(NR)))
        stt2 = sb("stt2", (128, NS, 16), F32)
        stt2_res = [Res() for _ in range(NS)]
        s2ring = Ring(list(range(NS)))
        f1ring = Ring([(0, 0), (1, 0), (6, 0)])
        ringT = Ring([7])
        f1_res = {(0, 0): pf_res[0], (1, 0): pf_res[1], (6, 0): pf_res[6]}

        for i in range(8):
            for half in range(2):
                wi = wring.next()
                S.dma(wst_sem[wi], I("dma_start",
                    out=wst[wi][:].rearrange("p (k c) -> p k c", k=4),
                    in_=w1_d[half * 512:(half + 1) * 512, i * 512:(i + 1) * 512].rearrange("(k p) c -> p k c", p=128)),
                    writes=[wst_res[wi]])
                plain_cast(cast_engs.next(), W1[:, half * 4:(half + 1) * 4, i * 512:(i + 1) * 512],
                           wst[wi][:].rearrange("p (k c) -> p k c", k=4), reads=[wst_res[wi]], writes=[w1_res[i][half]])
            for half in range(2):
                wi = wring.next()
                r0 = i * 512 + half * 256
                S.dma(wst_sem[wi], I("dma_start",
                    out=wst[wi][:].rearrange("p (k c) -> p k c", k=2),
                    in_=w2_d[r0:r0 + 256, :].rearrange("(k p) c -> p k c", p=128)),
                    writes=[wst_res[wi]])
                plain_cast(cast_engs.next(), W2[:, i * 4 + half * 2:i * 4 + half * 2 + 2, :],
                           wst[wi][:].rearrange("p (k c) -> p k c", k=2), reads=[wst_res[wi]], writes=[w2_res[i][half]])

        def rstd2(ssq_ap, res, n, slot, col):
            v_ap = stt2[:, slot, col:col + 1]
            r_ap = stt2[:, slot, col + 1:col + 2]
            S.op("dve", I("tensor_scalar", out=v_ap, in0=ssq_ap, scalar1=1.0 / n, scalar2=EPS,
                                                  op0=ALU.mult, op1=ALU.add), reads=[res], writes=[stt2_res[slot]])
            S.op("pool", I("tensor_tensor", out=r_ap, in0=v_ap, in1=neghalf[:, 0:1], op=ALU.pow),
                 reads=[stt2_res[slot], cs_res], writes=[stt2_res[slot]])
            return r_ap

        NG2 = NTILE // 2
        NXA = 4
        xa = [sb("xa%d" % i, (128, D), F32) for i in range(NXA)]
        xa_res = [Res() for _ in range(NXA)]
        xa_sem = [S.new_sem("xa%d" % i) for i in range(NXA)]
        xaring = Ring(list(range(NXA)))
        prep_state = {}

        def prep_load(G):
            hi2 = h2ring.next()
            xs = []
            for tt in range(2):
                row0 = (2 * G + tt) * 128
                xi = xaring.next()
                S.dma(xa_sem[xi], I("dma_start", out=xa[xi][:], in_=x1s_d[row0:row0 + 128, :]),
                      writes=[xa_res[xi]])
                xs.append(xi)
            prep_state[G] = dict(hi2=hi2, xs=xs, his=[])

        def prep_norm(G):
            ps = prep_state[G]
            for tt in range(2):
                xi = ps["xs"][tt]
                ss = s2ring.next()
                jk, jr = junk()
                S.op("act", I("activation", out=jk[:], in_=xa[xi][:], func=AF.Square,
                              accum_out=stt2[:, ss, 0:1]),
                     reads=[xa_res[xi]], writes=[stt2_res[ss], jr])
                r_ap = rstd2(stt2[:, ss, 0:1], stt2_res[ss], float(D), ss, 1)
                hi = hb2ring.next()
                S.op("dve", I("scalar_tensor_tensor", out=hb2[hi][:], in0=xa[xi][:], scalar=r_ap, in1=gpre2_bc[:],
                              op0=ALU.mult, op1=ALU.mult),
                     reads=[xa_res[xi], stt2_res[ss], c2_res], writes=[hb2_res[hi]])
                ps["his"].append(hi)

        def prep_tr(G):
            ps = prep_state[G]
            hi2 = ps["hi2"]
            for tt in range(2):
                hi = ps["his"][tt]
                tb = ringT.next()
                top = S.op("pe", [I("transpose", out=pb[tb][:, kc * 128:(kc + 1) * 128],
                                    in_=hb2[hi][:, kc * 128:(kc + 1) * 128], identity=ident_bf[:]) for kc in range(8)],
                           reads=[hb2_res[hi], cs_res], writes=[pb_res[tb]])
                if ff_anchor[0] is not None:
                    top.odeps.append(ff_anchor[0])
                S.op("act", I("activation", out=h2T[hi2][:, :, tt * 128:(tt + 1) * 128],
                              in_=pb[tb].rearrange("p (k c) -> p k c", k=8), func=AF.Copy),
                     reads=[pb_res[tb]], writes=[h2T_res[hi2][tt]])

        ff_anchor = [None]
        prep_load(0)
        prep_norm(0)
        prep_tr(0)
        for G in range(NG2):
            hi2 = prep_state[G]["hi2"]
            if G + 1 < NG2:
                prep_load(G + 1)
            acc = [[ringB.next() for c in range(2)] for tt in range(2)]

            def ff1(j):
                fb, fo = f1ring.next()
                fr = f1_res[(fb, fo)]
                ff_anchor[0] = S.op("pe", [I("matmul", out=pf[fb][:, fo:fo + 256],
                                             lhsT=W1[:, kc, j * 128:(j + 1) * 128], rhs=h2T[hi2][:, kc, :],
                                             start=(kc == 0), stop=(kc == 7)) for kc in range(8)],
                                    reads=h2T_res[hi2] + w1_res[j // 4], writes=[fr])
                ri = rtring.next()
                S.op("act", I("activation", out=rtmp[ri][:], in_=pf[fb][:, fo:fo + 256], func=AF.Relu),
                     reads=[fr], writes=[rtmp_res[ri]])
                qi = rTring.next()
                S.op("dve", I("tensor_tensor", out=rT[qi][:], in0=rtmp[ri][:], in1=pf[fb][:, fo:fo + 256],
                                                      op=ALU.mult),
                     reads=[rtmp_res[ri], fr], writes=[rT_res[qi]])
                return qi

            def ff2(j, qi):
                fns = []
                for tt in range(2):
                    for c in range(2):
                        fns.append(I("matmul",
                            out=pf[acc[tt][c]][:], lhsT=rT[qi][:, tt * 128:(tt + 1) * 128],
                            rhs=W2[:, j, c * 512:(c + 1) * 512], start=(j == 0), stop=(j == 31)))
                S.op("pe", fns, reads=[rT_res[qi], w2_res[j // 4][(j % 4) // 2]],
                     writes=[pf_res[acc[tt][c]] for tt in range(2) for c in range(2)])

            LAG = 2
            pend = []
            for j in range(32):
                pend.append((j, ff1(j)))
                if len(pend) > LAG:
                    ff2(*pend.pop(0))
                if G + 1 < NG2 and j == 1:
                    prep_norm(G + 1)
                if G + 1 < NG2 and j == 22:
                    prep_tr(G + 1)
            while pend:
                ff2(*pend.pop(0))

            tis = [Tring.next() for tt in range(2)]
            for tt in range(2):
                ti = tis[tt]
                for c in range(2):
                    csl = slice(c * 512, (c + 1) * 512)
                    if c == 0:
                        S.op("act", I("activation", out=Tt[ti][:, csl], in_=pf[acc[tt][c]][:], func=AF.Copy),
                             reads=[pf_res[acc[tt][c]]], writes=[Tt_resh[ti][c]])
                    else:
                        S.op("dve", I("tensor_copy", out=Tt[ti][:, csl], in_=pf[acc[tt][c]][:]),
                             reads=[pf_res[acc[tt][c]]], writes=[Tt_resh[ti][c]])
            for tt in range(2):
                row0 = (2 * G + tt) * 128
                ti = tis[tt]
                ss = s2ring.next()
                for c in range(2):
                    csl = slice(c * 512, (c + 1) * 512)
                    jk, jr = junk()
                    S.op("act", I("activation", out=jk[:, 0:512], in_=Tt[ti][:, csl], func=AF.Square,
                                  accum_out=stt2[:, ss, c:c + 1]),
                         reads=[Tt_resh[ti][c]], writes=[stt2_res[ss], jr])
                S.op("dve", I("tensor_tensor", out=stt2[:, ss, 2:3], in0=stt2[:, ss, 0:1],
                              in1=stt2[:, ss, 1:2], op=ALU.add),
                     reads=[stt2_res[ss]], writes=[stt2_res[ss]])
                r_ap = rstd2(stt2[:, ss, 2:3], stt2_res[ss], float(D), ss, 3)
                for c in range(2):
                    csl = slice(c * 512, (c + 1) * 512)
                    S.op("dve", I("scalar_tensor_tensor", out=Tt[ti][:, csl], in0=Tt[ti][:, csl], scalar=r_ap,
                                  in1=gpost2_bc[:, csl], op0=ALU.mult, op1=ALU.mult),
                         reads=[Tt_resh[ti][c], stt2_res[ss], c2_res], writes=[Tt_resh[ti][c]])
                xi = x2ring.next()
                S.dma(x2_sem[xi], I("dma_start", out=x2[xi][:], in_=x1s_d[row0:row0 + 128, :]),
                      writes=[x2_res[xi]])
                S.op("pool", I("tensor_tensor", out=Tt[ti][:], in0=Tt[ti][:], in1=x2[xi][:], op=ALU.add),
                     reads=Tt_resh[ti] + [x2_res[xi]], writes=Tt_resh[ti])
                S.dma(Tt_sem[ti], I("dma_start", out=out_d[row0:row0 + 128, :], in_=Tt[ti][:]),
                      reads=Tt_resh[ti])
        S.final_wait("sp", Tt_resh[0] + Tt_resh[1])
        S.flush()
    return nc


def _prep_shared(inp):
    f = np.float32
    c = {}
    c["w_in"] = np.ascontiguousarray(inp["w_in"][0], dtype=f)
    c["w_out"] = np.ascontiguousarray(inp["w_out"][0], dtype=f)
    c["w_ff1"] = np.ascontiguousarray(inp["w_ff1"][0], dtype=f)
    c["w_ff2"] = np.ascontiguousarray(inp["w_ff2"][0], dtype=f)
    c["gpre_pp"] = np.ascontiguousarray(inp["norm_mix_pre"][0].reshape(8, 128).T, dtype=f)
    gmix = np.concatenate([inp["g_out_na"][0], inp["g_out_sg"][0]])
    c["gmix_pp"] = np.ascontiguousarray(gmix.reshape(8, 128).T, dtype=f)
    c["lng_pp"] = np.ascontiguousarray(inp["sg_ln_g"][0].reshape(4, 128).T, dtype=f)
    c["gpost_bc"] = np.ascontiguousarray(np.broadcast_to(inp["norm_mix_post"][0][None, :], (128, D)), dtype=f)
    c["gpre2_bc"] = np.ascontiguousarray(np.broadcast_to(inp["norm_ffn_pre"][0][None, :], (128, D)), dtype=f)
    c["gpost2_bc"] = np.ascontiguousarray(np.broadcast_to(inp["norm_ffn_post"][0][None, :], (128, D)), dtype=f)
    rpb = np.asarray(inp["na_rpb"][0], dtype=f)
    cols = np.arange(GRID_W)
    dc_idx = np.clip(cols[None, :] - cols[:, None], -15, 15) + 15
    col_bias = rpb[:, :, dc_idx]
    relb = np.transpose(col_bias, (0, 2, 1, 3)).reshape(4, 2 * 64, 15 * 64)
    c["relb"] = np.ascontiguousarray(relb, dtype=f)
    col_start = np.clip(cols - 8, 0, GRID_W - 16)
    inwin = (cols[None, :] >= col_start[:, None]) & (cols[None, :] < col_start[:, None] + 16)
    m = np.where(inwin, 0.0, NEG).astype(f)
    m = np.broadcast_to(m[None, :, None, :], (2, 64, 15, 64)).reshape(128, 960)
    c["mask"] = np.ascontiguousarray(m, dtype=f)
    ws = np.asarray(inp["sg_w_s"][0], dtype=f)
    c["wsT"] = np.ascontiguousarray(np.transpose(ws, (2, 0, 1)), dtype=f)
    c["lnb_bc"] = np.ascontiguousarray(np.broadcast_to(inp["sg_ln_b"][0][None, :], (128, 512)), dtype=f)
    bs = np.asarray(inp["sg_b_s"][0], dtype=f)
    bsb = np.broadcast_to(bs.reshape(4, 2, 1, 128), (4, 2, 64, 128))
    c["bs_bc"] = np.ascontiguousarray(np.transpose(bsb.reshape(4, 128, 128), (1, 0, 2)), dtype=f)
    c["ident"] = np.eye(128, dtype=f)
    return c


_NC_CACHE = {}


def kernel(**inputs):
    x = np.asarray(inputs["x"], dtype=np.float32)
    B = x.shape[0]
    per = B // NCORES
    if per not in _NC_CACHE:
        _NC_CACHE[per] = build(nseq=per)
    nc = _NC_CACHE[per]
    shared = _prep_shared(inputs)
    in_maps = []
    for c in range(NCORES):
        m = dict(shared)
        m["x"] = np.ascontiguousarray(x[c * per:(c + 1) * per].reshape(per * SEQ, D))
        in_maps.append(m)
    res = run_bass_kernel_spmd(nc, in_maps, core_ids=list(range(NCORES)))
    outs = [np.asarray(r["out"], dtype=np.float32).reshape(per, SEQ, D) for r in res.results]
    return np.concatenate(outs, axis=0)
```

```python
import numpy as np
from contextlib import ExitStack
import concourse.bass as bass
import concourse.mybir as mybir
from concourse.bass_utils import run_bass_kernel_spmd
from concourse.alu_op_type import AluOpType as ALU

AF = mybir.ActivationFunctionType
F32, BF16 = mybir.dt.float32, mybir.dt.bfloat16

NCORES = 8
D = 1024
SEQ = 2048
NSEQ = 4
GRID_W = 64
ROWS = 32
EPS = 1e-6
NEG = -30000.0
ENGS = ("pe", "act", "dve", "pool", "sp")
CFG = dict(ringS=[0, 1, 4], ringPT=[5, 6, 7], ringO=[2, 3], ringA=[0, 1, 4, 5], ringT=[6, 7], ringTiny=[2, 3], sem_lat=350.0, prio="cp", prio2="prog")


class Res:
    __slots__ = ("w", "r", "const")

    def __init__(self, const=False):
        self.w = None
        self.r = []
        self.const = const


class Op:
    __slots__ = ("eng", "instrs", "deps", "odeps", "idx", "seg", "kind", "semkey", "val",
                 "occ", "lat", "n", "succ", "start", "finish", "cp", "pr")


def _free_elems(ap):
    try:
        sh = ap.shape
        n = 1
        for d in sh[1:]:
            n *= int(d)
        return n
    except Exception:
        return 512


def _estimate(eng, instrs):
    occ = 0.0
    for name, kw in instrs:
        if name == "matmul":
            occ += max(0.45 * _free_elems(kw["rhs"]) + 5, 0.8 * _free_elems(kw["lhsT"]) + 5)
        elif name == "transpose":
            occ += 107
        elif name == "dma_start":
            occ += 120
        elif eng == "act":
            occ += 220 + 0.85 * _free_elems(kw["in_"]) + (90 if kw.get("accum_out") is not None else 0)
        elif eng == "dve":
            key = "in_" if "in_" in kw else ("in0" if "in0" in kw else "ap")
            n = _free_elems(kw[key])
            occ += 110 + (8.0 if name == "reciprocal" else 1.05) * n
        elif eng == "pool":
            key = "in_" if "in_" in kw else ("in0" if "in0" in kw else "ap")
            occ += 260 + 1.9 * _free_elems(kw[key])
        else:
            occ += 100
    return occ


class Sched:
    SEM_LAT = 350.0

    def __init__(self, nc, st):
        self.nc = nc
        self.st = st
        self.semh = {}
        self.cnt = {}
        self.waited = {e: {} for e in ENGS}
        self.ops = []
        self.seg = 0
        self.nidx = 0
        self.last_dma = {}
        self.marks = []
        for e in ENGS:
            self.new_sem("e_" + e)

    def new_sem(self, key):
        self.semh[key] = self.st.enter_context(self.nc.semaphore(key))
        self.cnt[key] = 0
        return key

    def _new_op(self, eng, instrs, kind, reads, writes):
        o = Op()
        o.eng, o.instrs, o.kind = eng, instrs, kind
        o.idx = self.nidx
        self.nidx += 1
        o.seg = self.seg
        o.semkey = None
        o.val = None
        o.odeps = []
        deps = {}
        for r in reads:
            if r.w is not None:
                deps[id(r.w)] = r.w
        for w in writes:
            if w.w is not None:
                deps[id(w.w)] = w.w
            for x in w.r:
                deps[id(x)] = x
        o.deps = list(deps.values())
        for r in reads:
            if not r.const:
                r.r.append(o)
        for w in writes:
            w.w = o
            w.r = []
        self.ops.append(o)
        return o

    def op(self, eng, fns, reads=(), writes=()):
        if isinstance(fns, tuple):
            fns = [fns]
        o = self._new_op(eng, list(fns), "compute", reads, writes)
        o.semkey = "e_" + eng
        o.occ = _estimate(eng, o.instrs)
        o.lat = o.occ + CFG["sem_lat"]
        return o

    def dma(self, semkey, fn, reads=(), writes=(), eng="sp"):
        o = self._new_op(eng, [fn], "dma", reads, writes)
        o.semkey = semkey
        prev = self.last_dma.get(semkey)
        if prev is not None:
            o.odeps.append(prev)
        self.last_dma[semkey] = o
        nbytes = 4 * 128 * _free_elems(fn[1]["out"])
        o.occ = 120.0
        o.lat = 6000.0 + nbytes / 100.0
        return o

    def final_wait(self, eng, res_list):
        o = self._new_op(eng, [], "wait", (), ())
        deps = {}
        for r in res_list:
            if r.w is not None:
                deps[id(r.w)] = r.w
            for x in r.r:
                deps[id(x)] = x
        o.deps = list(deps.values())
        o.occ = 10.0
        o.lat = 10.0
        return o

    def _schedule(self, ops):
        import heapq
        seg = self.seg
        for o in ops:
            o.n = 0
            o.succ = []
            o.start = 0.0
            o.finish = 0.0
        for o in ops:
            for d in o.deps:
                if d.seg == seg:
                    d.succ.append(o)
                    o.n += 1
            for d in o.odeps:
                if d.seg == seg:
                    d.succ.append(o)
                    o.n += 1
        mode = CFG.get("prio%d" % self.seg, CFG.get("prio", "prog"))
        for o in reversed(ops):
            c = 0.0
            for s_ in o.succ:
                if s_.cp > c:
                    c = s_.cp
            o.cp = c + o.lat
        if mode == "prog":
            for o in ops:
                o.pr = o.idx
        elif mode == "cp":
            for o in ops:
                o.pr = -o.cp
        else:
            w = CFG.get("mixw", 1.0)
            for o in ops:
                o.pr = o.idx * CFG.get("idx_ns", 300.0) - w * o.cp
        free = {e: 0.0 for e in ENGS}
        future = {e: [] for e in ENGS}
        ready = {e: [] for e in ENGS}
        order = {e: [] for e in ENGS}

        def push(o):
            rt = 0.0
            for d in o.deps:
                if d.seg == seg and d.finish > rt:
                    rt = d.finish
            for d in o.odeps:
                if d.seg == seg and d.start > rt:
                    rt = d.start
            heapq.heappush(future[o.eng], (rt, o.idx, o))

        for o in ops:
            if o.n == 0:
                push(o)
        remaining = len(ops)
        while remaining:
            best = None
            for e in ENGS:
                f, r = future[e], ready[e]
                fe = free[e]
                while f and f[0][0] <= fe:
                    rt, idx, o = heapq.heappop(f)
                    heapq.heappush(r, (o.pr, idx, o))
                if r:
                    cand = (fe, r[0][0], e, 0)
                elif f:
                    cand = (f[0][0], f[0][1], e, 1)
                else:
                    continue
                if best is None or cand < best:
                    best = cand
            start, _, e, kind = best
            o = heapq.heappop(ready[e])[2] if kind == 0 else heapq.heappop(future[e])[2]
            o.start = start
            o.finish = start + o.lat
            free[e] = start + o.occ
            order[e].append(o)
            remaining -= 1
            for s_ in o.succ:
                s_.n -= 1
                if s_.n == 0:
                    push(s_)
        self.sim_span = max(free.values())
        return order

    def flush(self, name=None):
        ops = self.ops
        self.ops = []
        order = self._schedule(ops)
        seg = self.seg
        for e in ENGS:
            for o in order[e]:
                if o.kind == "compute":
                    self.cnt[o.semkey] += 1
                    o.val = self.cnt[o.semkey]
                elif o.kind == "dma":
                    self.cnt[o.semkey] += 16
                    o.val = self.cnt[o.semkey]
        queues = {}
        for e in ENGS:
            q = []
            wd = self.waited[e]
            for o in order[e]:
                need = {}
                for d in o.deps:
                    if d.seg != seg and d.kind != "dma":
                        continue
                    if d.val is None:
                        continue
                    if need.get(d.semkey, 0) < d.val:
                        need[d.semkey] = d.val
                for k, v in need.items():
                    if wd.get(k, 0) >= v:
                        continue
                    wd[k] = v
                    q.append(("wait", self.semh[k], v))
                if o.kind == "compute":
                    h = self.semh[o.semkey]
                    for ins in o.instrs[:-1]:
                        q.append(("ins", ins, None, 0))
                    q.append(("ins", o.instrs[-1], h, 1))
                elif o.kind == "dma":
                    q.append(("ins", o.instrs[0], self.semh[o.semkey], 16))
            queues[e] = q
        self.seg += 1
        with self.nc.Block() as blk:
            for ename, attr in (("pe", "tensor"), ("act", "scalar"), ("dve", "vector"),
                                ("pool", "gpsimd"), ("sp", "sync")):
                q = queues[ename]

                def body(e, q=q):
                    for it in q:
                        if it[0] == "wait":
                            e.wait_ge(it[1], it[2])
                        else:
                            (n_, kw_), h, inc = it[1], it[2], it[3]
                            r = getattr(e, n_)(**kw_)
                            if h is not None:
                                r.then_inc(h, inc)
                getattr(blk, attr)(body)


def I(name, **kw):
    return (name, kw)


class Ring:
    def __init__(self, items):
        self.items = items
        self.i = 0

    def next(self):
        it = self.items[self.i % len(self.items)]
        self.i += 1
        return it


def build(nseq=NSEQ, debug=False):
    nc = bass.Bass("TRN2", target_bir_lowering=False, dynamic_dma_scratch_size=1024)
    NT = nseq * SEQ
    NTILE = NT // 128

    def din(name, shape):
        return nc.dram_tensor(name, list(shape), F32, kind="ExternalInput").ap()

    x_d = din("x", (NT, D))
    win_d = din("w_in", (D, 2560))
    wout_d = din("w_out", (D, D))
    w1_d = din("w_ff1", (D, 4096))
    w2_d = din("w_ff2", (4096, D))
    gpre_pp_d = din("gpre_pp", (128, 8))
    gmix_pp_d = din("gmix_pp", (128, 8))
    lng_pp_d = din("lng_pp", (128, 4))
    gpost_bc_d = din("gpost_bc", (128, D))
    gpre2_bc_d = din("gpre2_bc", (128, D))
    gpost2_bc_d = din("gpost2_bc", (128, D))
    relb_d = din("relb", (4, 128, 960))
    mask_d = din("mask", (128, 960))
    wsT_d = din("wsT", (128, 8, 128))
    lnb_bc_d = din("lnb_bc", (128, 512))
    bs_bc_d = din("bs_bc", (128, 4, 128))
    ident_d = din("ident", (128, 128))
    x1s_d = nc.dram_tensor("x1s", [NT, D], F32).ap()
    out_d = nc.dram_tensor("out", [NT, D], F32, kind="ExternalOutput").ap()

    with ExitStack() as st:
        S = Sched(nc, st)

        def sb(name, shape, dt):
            return st.enter_context(nc.sbuf_tensor(name, list(shape), dt))

        pf = [st.enter_context(nc.psum_tensor("pf%d" % i, [128, 512], F32)) for i in range(8)]
        pf_res = [Res() for _ in range(8)]
        pb = {i: pf[i][:].bitcast(BF16) for i in range(8)}
        pb_res = pf_res
        ringA = Ring(CFG["ringA"])
        ringO = Ring(CFG["ringO"])
        ringTiny = Ring(CFG["ringTiny"])
        ringW = Ring([0, 1, 4, 5, 6, 7])
        ringS = Ring(CFG["ringS"])
        ringPT = Ring(CFG["ringPT"])
        ringB = Ring([2, 3, 4, 5])
        ringT = Ring(CFG["ringT"])

        ident_bf = sb("ident_bf", (128, 128), BF16)
        ones_bf = sb("ones_bf", (128, 2), BF16)
        neghalf = sb("neghalf", (128, 2), F32)
        junk_l = [sb("junk_act%d" % i, (128, 1024), BF16) for i in range(1)]
        junk_res = [Res() for _ in range(1)]
        junk_ring = Ring([0])

        def junk():
            i = junk_ring.next()
            return junk_l[i], junk_res[i]
        c_res = Res(const=True)

        ph1 = ExitStack()
        st.enter_context(ph1)

        def sb1(name, shape, dt):
            return ph1.enter_context(nc.sbuf_tensor(name, list(shape), dt))

        Win = sb1("Win", (128, 8, 2560), BF16)
        Wout = sb1("Wout", (128, 8, 1024), BF16)
        WsT = sb1("WsT", (128, 8, 128), BF16)
        gpost_bc = sb1("gpost_bc_s", (128, D), F32)
        Bias = sb1("Bias_s", (128, 4, 960), BF16)
        bias2 = sb1("bias2_s", (128, 4, 128), F32)
        gpp = sb1("gpp", (128, 24), F32)

        hT_l = [sb1("hT0", (128, 8, 512), BF16)]
        QT = sb1("QT", (128, 4, SEQ), BF16)
        KTb = [sb1("KT%d" % i, (128, SEQ), BF16) for i in range(4)]
        V = sb1("V", (128, 16, 512), BF16)
        guT = sb1("guT", (128, 4, SEQ), BF16)
        ssqS = sb1("ssqS", (128, 16), F32)

        hT_resl = [[Res() for _ in range(4)] for _ in range(2)]
        QT_res = [[Res() for _ in range(4)] for _ in range(4)]
        KT_res = [[Res() for _ in range(4)] for _ in range(4)]
        at0_res = [Res() for _ in range(4)]
        V_res = [Res() for _ in range(16)]
        gu_res = [[Res() for _ in range(16)] for _ in range(4)]
        ssqS_res = [Res() for _ in range(16)]

        def attn_buf(hp):
            return QT[:, hp, :], QT_res[hp]

        NX = 3
        xin = [sb1("xin%d" % i, (128, D), F32) for i in range(3)]
        xin_res = [Res() for _ in range(NX)]
        xin_sem = [S.new_sem("xin%d" % i) for i in range(NX)]
        xring = Ring(list(range(NX)))
        NXR = 3
        xr = [sb1("xr%d" % i, (128, D), F32) for i in range(NXR)]
        xr_res = [Res() for _ in range(NXR)]
        xr_sem = [S.new_sem("xr%d" % i) for i in range(NXR)]
        xrring = Ring(list(range(NXR)))
        NY = 2
        yt = [sb1("yt%d" % i, (128, D), F32) for i in range(NY)]
        yt_res = [[Res(), Res()] for _ in range(NY)]
        yring = Ring(list(range(NY)))
        hb = [sb1("hb%d" % i, (128, D), BF16) for i in range(2)]
        hb_res = [Res() for _ in range(3)]
        hbring = Ring([0, 1, 2])
        NP = 6
        Pp = [sb1("Pp%d" % i, (128, 640), BF16) for i in range(3)]
        Pp_res = [Res() for _ in range(NP)]
        Ppring = Ring(list(range(NP)))
        PTs = [sb1("PTs%d" % i, (128, 640), BF16) for i in range(2)]
        PTs_res = [Res() for _ in range(4)]
        PTring = Ring([0, 1, 2, 3])
        NBDB = 3
        BDall = sb1("BDall", (128, 4 * NBDB, 128), BF16)
        BD_res = [Res() for _ in range(NBDB)]
        BDring = Ring(list(range(NBDB)))
        gv = [sb1("gv%d" % i, (128, 512), F32) for i in range(2)]
        gv_res = [Res() for _ in range(3)]
        gvring = Ring([0, 1, 2])
        nrm = [sb1("nrm%d" % i, (128, 512), BF16) for i in range(2)]
        nrm_res = [Res() for _ in range(2)]
        nrmring = Ring([0, 1])
        t1 = sb1("t1", (128, 4, 128), F32)
        t1_res = Res()
        sq = [sb1("sq%d" % i, (128, 4, 128), BF16) for i in range(2)]
        sq_res = [Res() for _ in range(2)]
        sqring = Ring([0, 1])
        NS = 16
        stt = sb1("stt", (128, NS, 16), F32)
        stt_res = [Res() for _ in range(NS)]
        sring = Ring(list(range(NS)))

        csem = S.new_sem("csem")
        stg_cm = ExitStack()
        stg = [stg_cm.enter_context(nc.sbuf_tensor("stg%d" % i, [128, 2560], F32)) for i in range(2)]
        stg_res = [Res(), Res()]
        stg_sem = [S.new_sem("stg0"), S.new_sem("stg1")]
        stgring = Ring([0, 1])
        cast_engs = Ring(["act", "dve", "pool"])

        S.dma(csem, I("dma_start", out=gpp[:, 0:8], in_=gpre_pp_d[:, :]), writes=[c_res])
        S.dma(csem, I("dma_start", out=gpp[:, 8:16], in_=gmix_pp_d[:, :]), writes=[c_res])
        S.dma(csem, I("dma_start", out=gpp[:, 16:20], in_=lng_pp_d[:, :]), writes=[c_res])
        S.dma(csem, I("dma_start", out=gpost_bc[:], in_=gpost_bc_d[:, :]), writes=[c_res])
        cs_res = Res(const=True)
        S.op("dve", [I("memset", ap=ones_bf[:], constant=1.0),
                     I("memset", ap=neghalf[:], constant=-0.5)], writes=[cs_res])
        S.op("pool", I("memset", ap=BDall[:], constant=0.0), writes=BD_res)
        for i in range(3):
            S.op("pool", I("memset", ap=Pp[i][:], constant=0.0), writes=[Pp_res[i]])

        def scaled_cast(eng, out_ap, in_ap, sc_ap, reads, writes):
            if eng == "act":
                S.op("act", I("activation", out=out_ap, in_=in_ap, func=AF.Copy, scale=sc_ap),
                     reads=reads, writes=writes)
            elif eng == "dve":
                S.op("dve", I("tensor_scalar", out=out_ap, in0=in_ap, scalar1=sc_ap, scalar2=None,
                                                      op0=ALU.mult), reads=reads, writes=writes)
            else:
                S.op("pool", I("tensor_scalar", out=out_ap, in0=in_ap, scalar1=sc_ap, scalar2=1.0,
                                                       op0=ALU.mult, op1=ALU.mult), reads=reads, writes=writes)

        def plain_cast(eng, out_ap, in_ap, reads, writes):
            if eng == "act":
                S.op("act", I("activation", out=out_ap, in_=in_ap, func=AF.Copy),
                     reads=reads, writes=writes)
            elif eng == "dve":
                S.op("dve", I("tensor_copy", out=out_ap, in_=in_ap), reads=reads, writes=writes)
            else:
                S.op("pool", I("tensor_copy", out=out_ap, in_=in_ap), reads=reads, writes=writes)

        for kc in range(8):
            si = stgring.next()
            S.dma(stg_sem[si], I("dma_start",
                out=stg[si][:, 0:2560], in_=win_d[kc * 128:(kc + 1) * 128, :]), writes=[stg_res[si]])
            for hlf in range(2):
                scaled_cast(cast_engs.next(), Win[:, kc, hlf * 1280:(hlf + 1) * 1280],
                            stg[si][:, hlf * 1280:(hlf + 1) * 1280], gpp[:, kc:kc + 1],
                            reads=[stg_res[si], c_res], writes=[c_res] if False else [Res()])
        si = stgring.next()
        S.dma(stg_sem[si], I("dma_start", out=stg[si][:, 0:128], in_=ident_d[:, :]),
              writes=[stg_res[si]])
        S.op("dve", I("tensor_copy", out=ident_bf[:], in_=stg[si][:, 0:128]),
             reads=[stg_res[si]], writes=[cs_res])
        S.dma(stg_sem[si], I("dma_start", out=stg[si][:, 128:1152], in_=wsT_d.rearrange("p g q -> p (g q)")),
              writes=[stg_res[si]])
        S.op("dve", I("tensor_copy", out=WsT[:].rearrange("p g q -> p (g q)"), in_=stg[si][:, 128:1152]),
             reads=[stg_res[si]], writes=[cs_res])
        S.dma(stg_sem[si], I("dma_start", out=stg[si][:, 1152:1664], in_=lnb_bc_d[:, :]),
              writes=[stg_res[si]])
        S.dma(stg_sem[si], I("dma_start", out=stg[si][:, 1664:2176], in_=bs_bc_d.rearrange("p g q -> p (g q)")),
              writes=[stg_res[si]])
        for gp in range(4):
            for gg in range(2):
                g = 2 * gp + gg
                bk = ringA.next()
                S.op("pe", I("matmul",
                    out=pf[bk][:, 0:128], lhsT=stg[si][:, 1152 + gp * 128:1152 + (gp + 1) * 128],
                    rhs=stg[si][:, 128 + g * 128:128 + (g + 1) * 128], start=True, stop=True),
                    reads=[stg_res[si]], writes=[pf_res[bk]])
                S.op("dve", I("tensor_tensor",
                    out=bias2[gg * 64:(gg + 1) * 64, gp, :], in0=pf[bk][gg * 64:(gg + 1) * 64, 0:128],
                    in1=stg[si][gg * 64:(gg + 1) * 64, 1664 + gp * 128:1664 + (gp + 1) * 128], op=ALU.add),
                    reads=[pf_res[bk], stg_res[si]], writes=[cs_res])
        S.flush()
        stg_cm.close()
        hT_l.append(sb1("hT1", (128, 8, 512), BF16))
        hb.append(sb1("hb2", (128, D), BF16))
        Pp.append(sb1("Pp3", (128, 640), BF16))
        Pp.append(sb1("Pp4", (128, 640), BF16))
        Pp.append(sb1("Pp5", (128, 640), BF16))
        PTs.append(sb1("PTs3", (128, 640), BF16))
        PTs.append(sb1("PTs2", (128, 640), BF16))
        gv.append(sb1("gv2", (128, 512), F32))
        for i in (3, 4, 5):
            S.op("pool", I("memset", ap=Pp[i][:], constant=0.0), writes=[Pp_res[i]])
        hTring = Ring([0, 1])

        def rstd_from_ssq(ssq_ap, ssq_res, n, sres_slot, col):
            v_ap = stt[:, sres_slot, col:col + 1]
            r_ap = stt[:, sres_slot, col + 1:col + 2]
            S.op("dve", I("tensor_scalar", out=v_ap, in0=ssq_ap, scalar1=1.0 / n, scalar2=EPS,
                                                  op0=ALU.mult, op1=ALU.add),
                 reads=[ssq_res], writes=[stt_res[sres_slot]])
            S.op("pool", I("tensor_tensor", out=r_ap, in0=v_ap, in1=neghalf[:, 0:1], op=ALU.pow),
                 reads=[stt_res[sres_slot], cs_res], writes=[stt_res[sres_slot]])
            return r_ap

        def load_tile(dram_ap, row0):
            xi = xring.next()
            last_x_load[0] = S.dma(xin_sem[xi], I("dma_start", out=xin[xi][:], in_=dram_ap[row0:row0 + 128, :]),
                                   writes=[xin_res[xi]])
            return xi

        def norm_transpose(xi, dstT, dst_res, col0, gbc=None):
            ss = sring.next()
            jk, jr = junk()
            S.op("act", I("activation", out=jk[:], in_=xin[xi][:], func=AF.Square,
                                               accum_out=stt[:, ss, 0:1]),
                 reads=[xin_res[xi]], writes=[stt_res[ss], jr])
            r_ap = rstd_from_ssq(stt[:, ss, 0:1], stt_res[ss], float(D), ss, 1)
            hi = hbring.next()
            if gbc is None:
                S.op("dve", I("tensor_scalar", out=hb[hi][:], in0=xin[xi][:], scalar1=r_ap, scalar2=None,
                                                      op0=ALU.mult),
                     reads=[xin_res[xi], stt_res[ss]], writes=[hb_res[hi]])
            else:
                S.op("dve", I("scalar_tensor_tensor", out=hb[hi][:], in0=xin[xi][:], scalar=r_ap,
                                                             in1=gbc[:], op0=ALU.mult, op1=ALU.mult),
                     reads=[xin_res[xi], stt_res[ss], c_res], writes=[hb_res[hi]])
            tb = ringT.next()
            S.op("pe", [I("transpose", out=pb[tb][:, kc * 128:(kc + 1) * 128],
                                                     in_=hb[hi][:, kc * 128:(kc + 1) * 128], identity=ident_bf[:])
                        for kc in range(8)],
                 reads=[hb_res[hi], cs_res], writes=[pb_res[tb]])
            S.op("act", I("activation", out=dstT[:, :, col0:col0 + 128],
                                               in_=pb[tb].rearrange("p (k c) -> p k c", k=8), func=AF.Copy),
                 reads=[pb_res[tb]], writes=[dst_res])

        wout_res = [Res(const=True) for _ in range(8)]
        bias_res = [Res(const=True) for _ in range(4)]
        last_x_load = [None]

        def late_prep():
            first = True
            for kc in range(8):
                xi = kc % NXR
                o = S.dma(xr_sem[xi], I("dma_start", out=xr[xi][:], in_=wout_d[kc * 128:(kc + 1) * 128, :]),
                          writes=[xr_res[xi]])
                if first and last_x_load[0] is not None:
                    o.odeps.append(last_x_load[0])
                first = False
                scaled_cast(cast_engs.next(), Wout[:, kc, :], xr[xi][:], gpp[:, 8 + kc:9 + kc],
                            reads=[xr_res[xi], c_res], writes=[wout_res[kc]])
            xm = 2
            S.dma(xr_sem[xm], I("dma_start", out=xr[xm][:, 0:960], in_=mask_d[:, :]), writes=[xr_res[xm]])
            for hp in range(4):
                xb = hp % 2
                S.dma(xr_sem[xb], I("dma_start", out=xr[xb][:, 0:960], in_=relb_d[hp, :, :]), writes=[xr_res[xb]])
                S.op("dve", I("tensor_tensor", out=Bias[:, hp, :], in0=xr[xb][:, 0:960], in1=xr[xm][:, 0:960],
                              op=ALU.add),
                     reads=[xr_res[xb], xr_res[xm]], writes=[bias_res[hp]])

        for s in range(nseq):
            tok0 = s * SEQ
            def p1ab(g):
                S.marks.append(("s%d P1ab g%d" % (s, g), S.nidx))
                hbuf = hTring.next()
                hT = hT_l[hbuf]
                hT_res = hT_resl[hbuf]
                for tt in range(4):
                    xi = load_tile(x_d, tok0 + (4 * g + tt) * 128)
                    norm_transpose(xi, hT, hT_res[tt], tt * 128)
                gsl = slice(g * 512, (g + 1) * 512)

                def proj_fm(col0, evac):
                    bk = ringA.next()
                    S.op("pe", [I("matmul", out=pf[bk][:], lhsT=Win[:, kc, col0:col0 + 128],
                                                          rhs=hT[:, kc, :], start=(kc == 0), stop=(kc == 7))
                                for kc in range(8)],
                         reads=hT_res + [c_res], writes=[pf_res[bk]])
                    evac(bk)

                def proj_tm(tt, col0, evac):
                    bk = ringA.next()
                    S.op("pe", [I("matmul", out=pf[bk][:], lhsT=hT[:, kc, tt * 128:(tt + 1) * 128],
                                                          rhs=Win[:, kc, col0:col0 + 512], start=(kc == 0),
                                                          stop=(kc == 7))
                                for kc in range(8)],
                         reads=[hT_res[tt], c_res], writes=[pf_res[bk]])
                    evac(bk)

                for c in range(4):
                    def ev_u(bk, c=c):
                        S.op("act", I("activation", out=guT[:, c, gsl], in_=pf[bk][:],
                                                           func=AF.Gelu_apprx_tanh),
                             reads=[pf_res[bk]], writes=[gu_res[c][4 * g + k] for k in range(4)])
                    proj_fm(1536 + c * 128, ev_u)
                for tt in range(4):
                    t = 4 * g + tt

                    def ev_v(bk, t=t):
                        S.op("dve", I("tensor_copy", out=V[:, t, :], in_=pf[bk][:]),
                             reads=[pf_res[bk]], writes=[V_res[t]])
                    proj_tm(tt, 1024, ev_v)

                    def ev_sg(bk, t=t):
                        gi = gvring.next()
                        S.op("act", I("activation", out=gv[gi][:], in_=pf[bk][:], func=AF.Gelu_apprx_tanh),
                             reads=[pf_res[bk]], writes=[gv_res[gi]])
                        ss = sring.next()
                        S.op("dve", I("bn_stats", out=stt[:, ss, 0:6], in_=gv[gi][:]),
                             reads=[gv_res[gi]], writes=[stt_res[ss]])
                        S.op("dve", I("bn_aggr", out=stt[:, ss, 6:8], in_=stt[:, ss, 0:6]),
                             reads=[stt_res[ss]], writes=[stt_res[ss]])
                        S.op("dve", I("tensor_scalar", out=stt[:, ss, 8:9], in0=stt[:, ss, 7:8], scalar1=EPS,
                                                              scalar2=None, op0=ALU.add),
                             reads=[stt_res[ss]], writes=[stt_res[ss]])
                        S.op("pool", I("tensor_tensor", out=stt[:, ss, 9:10], in0=stt[:, ss, 8:9],
                                                               in1=neghalf[:, 0:1], op=ALU.pow),
                             reads=[stt_res[ss], cs_res], writes=[stt_res[ss]])
                        ni = nrmring.next()
                        S.op("dve", I("tensor_scalar", out=nrm[ni][:], in0=gv[gi][:], scalar1=stt[:, ss, 6:7],
                                                              scalar2=stt[:, ss, 9:10], op0=ALU.subtract,
                                                              op1=ALU.mult),
                             reads=[gv_res[gi], stt_res[ss]], writes=[nrm_res[ni]])
                        b2 = ringA.next()
                        S.op("pe", [I("matmul",
                            out=pf[b2][(gq % 2) * 64:(gq % 2 + 1) * 64, (gq // 2) * 128:(gq // 2 + 1) * 128],
                            lhsT=nrm[ni][:, gq * 64:(gq + 1) * 64], rhs=WsT[:, gq, :], start=True, stop=True)
                            for gq in range(8)],
                            reads=[nrm_res[ni], cs_res], writes=[pf_res[b2]])
                        for gp in range(4):
                            S.op("dve", I("scalar_tensor_tensor",
                                out=t1[:, gp, :], in0=pf[b2][:, gp * 128:(gp + 1) * 128],
                                scalar=gpp[:, 16 + gp:17 + gp], in1=bias2[:, gp, :], op0=ALU.mult, op1=ALU.add),
                                reads=[pf_res[b2], c_res, cs_res], writes=[t1_res])
                        tsl = slice(t * 128, (t + 1) * 128)
                        gur = [gu_res[c][t] for c in range(4)]
                        S.op("pool", I("tensor_tensor", out=guT[:, :, tsl], in0=t1[:], in1=guT[:, :, tsl],
                                                               op=ALU.mult),
                             reads=[t1_res] + gur, writes=gur)
                        qi = sqring.next()
                        S.op("pool", I("tensor_tensor", out=sq[qi][:], in0=guT[:, :, tsl], in1=guT[:, :, tsl],
                                                               op=ALU.mult),
                             reads=gur, writes=[sq_res[qi]])
                        b3 = ringTiny.next()
                        S.op("pe", [I("matmul", out=pf[b3][:, 0:1], lhsT=sq[qi][:, gp, :],
                                                              rhs=ones_bf[:, 0:1], start=(gp == 0), stop=(gp == 3))
                                    for gp in range(4)],
                             reads=[sq_res[qi], cs_res], writes=[pf_res[b3]])
                        S.op("dve", I("tensor_copy", out=ssqS[:, t:t + 1], in_=pf[b3][:, 0:1]),
                             reads=[pf_res[b3]], writes=[ssqS_res[t]])
                    proj_tm(tt, 2048, ev_sg)
                for hp in range(4):
                    def ev_k(bk, hp=hp):
                        S.op("act", I("activation", out=KTb[hp][:, gsl], in_=pf[bk][:], func=AF.Copy),
                             reads=[pf_res[bk]], writes=[KT_res[hp][g]])
                    proj_fm(512 + hp * 128, ev_k)
                for hp in range(4):
                    def ev_q(bk, hp=hp):
                        S.op("dve", I("tensor_scalar", out=QT[:, hp, gsl], in0=pf[bk][:], scalar1=0.125,
                                                              scalar2=None, op0=ALU.mult),
                             reads=[pf_res[bk]], writes=[QT_res[hp][g]])
                    proj_fm(hp * 128, ev_q)

            def attn_rb(hps, rb):
                for hp in hps:
                    if rb == 0:
                        S.marks.append(("s%d attn hp%d" % (s, hp), S.nidx))
                    abuf, ares = attn_buf(hp)
                    ob = ringO.next()
                    for r in range(rb * 8, rb * 8 + 8):
                        rs = min(max(r - 4, 0), ROWS - 8)
                        dr0 = rs - r + 7
                        g_q = r // 8
                        if r % 4 == 0:
                            bb = BDring.next()
                            S.op("pool", [I("tensor_copy", out=BDall[0:64, bb * 4:bb * 4 + 4, 0:64],
                                            in_=QT[0:64, hp, r * 64:(r + 4) * 64].rearrange("p (r q) -> p r q", r=4)),
                                          I("tensor_copy", out=BDall[64:128, bb * 4:bb * 4 + 4, 64:128],
                                            in_=QT[64:128, hp, r * 64:(r + 4) * 64].rearrange("p (r q) -> p r q", r=4))],
                                 reads=[QT_res[hp][g_q]], writes=[BD_res[bb]])
                        bslot = bb * 4 + (r % 4)
                        kgs = sorted(set([(rs * 64) // 512, (rs * 64 + 511) // 512]))
                        sbk = ringS.next()
                        S.op("pe", I("matmul", out=pf[sbk][:], lhsT=BDall[:, bslot, :],
                                     rhs=KTb[hp][:, rs * 64:rs * 64 + 512], start=True, stop=True),
                             reads=[BD_res[bb]] + [KT_res[hp][k] for k in kgs],
                             writes=[pf_res[sbk]])
                        S.op("dve", I("tensor_tensor", out=pf[sbk][:], in0=pf[sbk][:],
                                      in1=Bias[:, hp, dr0 * 64:dr0 * 64 + 512], op=ALU.add),
                             reads=[pf_res[sbk], bias_res[hp]], writes=[pf_res[sbk]])
                        ss = sring.next()
                        S.op("dve", I("tensor_reduce", out=stt[:, ss, 0:1], in_=pf[sbk][:],
                                                              axis=mybir.AxisListType.X, op=ALU.max, negate=True),
                             reads=[pf_res[sbk]], writes=[stt_res[ss]])
                        pi = Ppring.next()
                        S.op("act", I("activation", out=Pp[pi][:, 64:576], in_=pf[sbk][:], func=AF.Exp,
                                                           bias=stt[:, ss, 0:1], scale=1.0,
                                                           accum_out=stt[:, ss, 1:2]),
                             reads=[pf_res[sbk], stt_res[ss]], writes=[Pp_res[pi], stt_res[ss]])
                        S.op("dve", I("reciprocal", out=stt[:, ss, 2:3], in_=stt[:, ss, 1:2]),
                             reads=[stt_res[ss]], writes=[stt_res[ss]])
                        S.op("pool", I("tensor_scalar", out=Pp[pi][:, 64:576], in0=Pp[pi][:, 64:576],
                                                               scalar1=stt[:, ss, 2:3], scalar2=1.0,
                                                               op0=ALU.mult, op1=ALU.mult),
                             reads=[stt_res[ss], Pp_res[pi]], writes=[Pp_res[pi]])
                        if rs % 2 == 0:
                            nch, c0, t0 = 4, 64, rs // 2
                        else:
                            nch, c0, t0 = 5, 0, (rs - 1) // 2
                        tb = ringPT.next()
                        S.op("pe", [I("transpose", out=pb[tb][:, c * 128:(c + 1) * 128],
                                                               in_=Pp[pi][:, c0 + c * 128:c0 + (c + 1) * 128],
                                                               identity=ident_bf[:])
                                    for c in range(nch)],
                             reads=[Pp_res[pi], cs_res], writes=[pb_res[tb]])
                        ti = PTring.next()
                        S.op("act", I("activation", out=PTs[ti][:, 0:nch * 128], in_=pb[tb][:, 0:nch * 128],
                                                           func=AF.Copy),
                             reads=[pb_res[tb]], writes=[PTs_res[ti]])
                        oc = (r % 8) * 64
                        fns = []
                        for hh in range(2):
                            for c in range(nch):
                                fns.append(I("matmul",
                                    out=pf[ob][hh * 64:(hh + 1) * 64, oc:oc + 64],
                                    lhsT=V[:, t0 + c, (2 * hp + hh) * 64:(2 * hp + hh + 1) * 64],
                                    rhs=PTs[ti][:, c * 128 + hh * 64:c * 128 + (hh + 1) * 64],
                                    start=(c == 0), stop=(c == nch - 1)))
                        S.op("pe", fns, reads=[PTs_res[ti]] + [V_res[t0 + c] for c in range(nch)],
                             writes=[pf_res[ob]])
                    S.op("dve", I("tensor_copy", out=abuf[:, rb * 512:(rb + 1) * 512], in_=pf[ob][:]),
                         reads=[pf_res[ob]], writes=[ares[rb]])

            def p1e(t):
                if t % 4 == 0:
                    S.marks.append(("s%d P1e t%d" % (s, t), S.nidx))
                tsl = slice(t * 128, (t + 1) * 128)
                g_t = t // 4
                qi = sqring.next()
                S.op("act", I("activation", out=sq[qi][:], in_=QT[:, :, tsl], func=AF.Square),
                     reads=[QT_res[hp][g_t] for hp in range(4)], writes=[sq_res[qi]])
                b3 = ringTiny.next()
                S.op("pe", [I("matmul", out=pf[b3][:, 0:1], lhsT=sq[qi][:, hp, :], rhs=ones_bf[:, 0:1],
                                                      start=(hp == 0), stop=(hp == 3)) for hp in range(4)],
                     reads=[sq_res[qi], cs_res], writes=[pf_res[b3]])
                ss = sring.next()
                S.op("dve", I("tensor_scalar", out=stt[:, ss, 0:1], in0=pf[b3][:, 0:1], scalar1=1.0 / 512,
                                                      scalar2=EPS, op0=ALU.mult, op1=ALU.add),
                     reads=[pf_res[b3]], writes=[stt_res[ss]])
                S.op("dve", I("tensor_scalar", out=stt[:, ss, 1:2], in0=ssqS[:, t:t + 1], scalar1=1.0 / 512,
                                                      scalar2=EPS, op0=ALU.mult, op1=ALU.add),
                     reads=[ssqS_res[t]], writes=[stt_res[ss]])
                S.op("pool", I("tensor_tensor", out=stt[:, ss, 2:4], in0=stt[:, ss, 0:2], in1=neghalf[:, 0:2],
                                                       op=ALU.pow),
                     reads=[stt_res[ss], cs_res], writes=[stt_res[ss]])
                yi = yring.next()
                for c in range(2):
                    csl = slice(c * 512, (c + 1) * 512)
                    bA = ringW.next()
                    fa = []
                    for hp in range(4):
                        abuf, ares = attn_buf(hp)
                        fa.append(I("matmul", out=pf[bA][:], lhsT=abuf[:, tsl],
                                                                       rhs=Wout[:, hp, csl], start=(hp == 0),
                                                                       stop=(hp == 3)))
                    S.op("pe", fa, reads=[attn_buf(hp)[1][g_t] for hp in range(4)] + wout_res[0:4],
                         writes=[pf_res[bA]])
                    bB = ringW.next()
                    S.op("pe", [I("matmul", out=pf[bB][:], lhsT=guT[:, gp, tsl],
                                                          rhs=Wout[:, 4 + gp, csl], start=(gp == 0), stop=(gp == 3))
                                for gp in range(4)],
                         reads=[gu_res[gp][t] for gp in range(4)] + wout_res[4:8], writes=[pf_res[bB]])
                    S.op("act", I("activation", out=yt[yi][:, csl], in_=pf[bA][:], func=AF.Copy,
                                                            scale=stt[:, ss, 2:3]),
                         reads=[pf_res[bA], stt_res[ss]], writes=[yt_res[yi][c]])
                    S.op("dve", I("scalar_tensor_tensor", out=yt[yi][:, csl], in0=pf[bB][:],
                                                                      scalar=stt[:, ss, 3:4], in1=yt[yi][:, csl],
                                                                      op0=ALU.mult, op1=ALU.add),
                         reads=[pf_res[bB], stt_res[ss], yt_res[yi][c]], writes=[yt_res[yi][c]])
                s2 = sring.next()
                jk, jr = junk()
                S.op("act", I("activation", out=jk[:], in_=yt[yi][:], func=AF.Square,
                                                   accum_out=stt[:, s2, 0:1]),
                     reads=yt_res[yi], writes=[stt_res[s2], jr])
                r_ap = rstd_from_ssq(stt[:, s2, 0:1], stt_res[s2], float(D), s2, 1)
                S.op("dve", I("scalar_tensor_tensor", out=yt[yi][:], in0=yt[yi][:], scalar=r_ap,
                                                             in1=gpost_bc[:], op0=ALU.mult, op1=ALU.mult),
                     reads=yt_res[yi] + [stt_res[s2], c_res], writes=yt_res[yi])
                xi = xrring.next()
                S.dma(xr_sem[xi], I("dma_start", out=xr[xi][:], in_=x_d[tok0 + t * 128:tok0 + (t + 1) * 128, :]),
                      writes=[xr_res[xi]])
                S.op("pool", I("tensor_tensor", out=xr[xi][:], in0=yt[yi][:], in1=xr[xi][:], op=ALU.add),
                     reads=yt_res[yi] + [xr_res[xi]], writes=[xr_res[xi]])
                S.dma(xr_sem[xi], I("dma_start", out=x1s_d[tok0 + t * 128:tok0 + (t + 1) * 128, :],
                                                         in_=xr[xi][:]),
                      reads=[xr_res[xi]])

            for g in range(4):
                p1ab(g)
                if s == 0 and g == 1:
                    late_prep()
            for hp in range(4):
                for rb in range(4):
                    attn_rb([hp], rb)
            for t in range(16):
                p1e(t)

        S.final_wait("sp", xr_res)
        S.flush()
        ph1.close()

        W1 = sb("W1", (128, 8, 4096), BF16)
        W2 = sb("W2", (128, 32, 1024), BF16)
        gpre2_bc = sb("gpre2_bc_s", (128, D), F32)
        gpost2_bc = sb("gpost2_bc_s", (128, D), F32)
        w1_res = [[Res(True), Res(True)] for _ in range(8)]
        w2_res = [[Res(True), Res(True)] for _ in range(8)]
        c2_res = Res(const=True)
        wst = [sb("wst%d" % i, (128, 2048), F32) for i in range(3)]
        wst_res = [Res(), Res(), Res()]
        wst_sem = [S.new_sem("wst0"), S.new_sem("wst1"), S.new_sem("wst2")]
        wring = Ring([0, 1, 2])
        c2sem = S.new_sem("c2sem")
        S.dma(c2sem, I("dma_start", out=gpre2_bc[:], in_=gpre2_bc_d[:, :]), writes=[c2_res])
        S.dma(c2sem, I("dma_start", out=gpost2_bc[:], in_=gpost2_bc_d[:, :]), writes=[c2_res])

        NX2 = 2
        x2 = [sb("x2_%d" % i, (128, D), F32) for i in range(NX2)]
        x2_res = [Res() for _ in range(NX2)]
        x2_sem = [S.new_sem("x2_%d" % i) for i in range(NX2)]
        x2ring = Ring(list(range(NX2)))
        Tt = [sb("Tt%d" % i, (128, D), F32) for i in range(2)]
        Tt_resh = [[Res(), Res()], [Res(), Res()]]
        Tt_sem = [S.new_sem("Tt0"), S.new_sem("Tt1")]
        Tring = Ring([0, 1])
        hb2 = [sb("hb2_%d" % i, (128, D), BF16) for i in range(4)]
        hb2_res = [Res() for _ in range(4)]
        hb2ring = Ring([0, 1, 2, 3])
        h2T = [sb("h2T%d" % i, (128, 8, 256), BF16) for i in range(2)]
        h2T_res = [[Res(), Res()] for _ in range(2)]
        h2ring = Ring([0, 1])
        NR = 4
        rtmp = [sb("rtmp%d" % i, (128, 256), F32) for i in range(2)]
        rtmp_res = [Res(), Res()]
        rtring = Ring([0, 1])
        rT = [sb("rT%d" % i, (128, 256), BF16) for i in range(NR)]
        rT_res = [Res() for _ in range(NR)]
        rTring = Ring(list(range(NR)))
        stt2 = sb("stt2", (128, NS, 16), F32)
        stt2_res = [Res() for _ in range(NS)]
        s2ring = Ring(list(range(NS)))
        f1ring = Ring([(0, 0), (1, 0), (6, 0)])
        ringT = Ring([7])
        f1_res = {(0, 0): pf_res[0], (1, 0): pf_res[1], (6, 0): pf_res[6]}

        for i in range(8):
            for half in range(2):
                wi = wring.next()
                S.dma(wst_sem[wi], I("dma_start",
                    out=wst[wi][:].rearrange("p (k c) -> p k c", k=4),
                    in_=w1_d[half * 512:(half + 1) * 512, i * 512:(i + 1) * 512].rearrange("(k p) c -> p k c", p=128)),
                    writes=[wst_res[wi]])
                plain_cast(cast_engs.next(), W1[:, half * 4:(half + 1) * 4, i * 512:(i + 1) * 512],
                           wst[wi][:].rearrange("p (k c) -> p k c", k=4), reads=[wst_res[wi]], writes=[w1_res[i][half]])
            for half in range(2):
                wi = wring.next()
                r0 = i * 512 + half * 256
                S.dma(wst_sem[wi], I("dma_start",
                    out=wst[wi][:].rearrange("p (k c) -> p k c", k=2),
                    in_=w2_d[r0:r0 + 256, :].rearrange("(k p) c -> p k c", p=128)),
                    writes=[wst_res[wi]])
                plain_cast(cast_engs.next(), W2[:, i * 4 + half * 2:i * 4 + half * 2 + 2, :],
                           wst[wi][:].rearrange("p (k c) -> p k c", k=2), reads=[wst_res[wi]], writes=[w2_res[i][half]])

        def rstd2(ssq_ap, res, n, slot, col):
            v_ap = stt2[:, slot, col:col + 1]
            r_ap = stt2[:, slot, col + 1:col + 2]
            S.op("dve", I("tensor_scalar", out=v_ap, in0=ssq_ap, scalar1=1.0 / n, scalar2=EPS,
                                                  op0=ALU.mult, op1=ALU.add), reads=[res], writes=[stt2_res[slot]])
            S.op("pool", I("tensor_tensor", out=r_ap, in0=v_ap, in1=neghalf[:, 0:1], op=ALU.pow),
                 reads=[stt2_res[slot], cs_res], writes=[stt2_res[slot]])
            return r_ap

        NG2 = NTILE // 2
        NXA = 4
        xa = [sb("xa%d" % i, (128, D), F32) for i in range(NXA)]
        xa_res = [Res() for _ in range(NXA)]
        xa_sem = [S.new_sem("xa%d" % i) for i in range(NXA)]
        xaring = Ring(list(range(NXA)))
        prep_state = {}

        def prep_load(G):
            hi2 = h2ring.next()
            xs = []
            for tt in range(2):
                row0 = (2 * G + tt) * 128
                xi = xaring.next()
                S.dma(xa_sem[xi], I("dma_start", out=xa[xi][:], in_=x1s_d[row0:row0 + 128, :]),
                      writes=[xa_res[xi]])
                xs.append(xi)
            prep_state[G] = dict(hi2=hi2, xs=xs, his=[])

        def prep_norm(G):
            ps = prep_state[G]
            for tt in range(2):
                xi = ps["xs"][tt]
                ss = s2ring.next()
                jk, jr = junk()
                S.op("act", I("activation", out=jk[:], in_=xa[xi][:], func=AF.Square,
                              accum_out=stt2[:, ss, 0:1]),
                     reads=[xa_res[xi]], writes=[stt2_res[ss], jr])
                r_ap = rstd2(stt2[:, ss, 0:1], stt2_res[ss], float(D), ss, 1)
                hi = hb2ring.next()
                S.op("dve", I("scalar_tensor_tensor", out=hb2[hi][:], in0=xa[xi][:], scalar=r_ap, in1=gpre2_bc[:],
                              op0=ALU.mult, op1=ALU.mult),
                     reads=[xa_res[xi], stt2_res[ss], c2_res], writes=[hb2_res[hi]])
                ps["his"].append(hi)

        def prep_tr(G):
            ps = prep_state[G]
            hi2 = ps["hi2"]
            for tt in range(2):
                hi = ps["his"][tt]
                tb = ringT.next()
                top = S.op("pe", [I("transpose", out=pb[tb][:, kc * 128:(kc + 1) * 128],
                                    in_=hb2[hi][:, kc * 128:(kc + 1) * 128], identity=ident_bf[:]) for kc in range(8)],
                           reads=[hb2_res[hi], cs_res], writes=[pb_res[tb]])
                if ff_anchor[0] is not None:
                    top.odeps.append(ff_anchor[0])
                S.op("act", I("activation", out=h2T[hi2][:, :, tt * 128:(tt + 1) * 128],
                              in_=pb[tb].rearrange("p (k c) -> p k c", k=8), func=AF.Copy),
                     reads=[pb_res[tb]], writes=[h2T_res[hi2][tt]])

        ff_anchor = [None]
        prep_load(0)
        prep_norm(0)
        prep_tr(0)
        for G in range(NG2):
            hi2 = prep_state[G]["hi2"]
            if G + 1 < NG2:
                prep_load(G + 1)
            acc = [[ringB.next() for c in range(2)] for tt in range(2)]

            def ff1(j):
                fb, fo = f1ring.next()
                fr = f1_res[(fb, fo)]
                ff_anchor[0] = S.op("pe", [I("matmul", out=pf[fb][:, fo:fo + 256],
                                             lhsT=W1[:, kc, j * 128:(j + 1) * 128], rhs=h2T[hi2][:, kc, :],
                                             start=(kc == 0), stop=(kc == 7)) for kc in range(8)],
                                    reads=h2T_res[hi2] + w1_res[j // 4], writes=[fr])
                ri = rtring.next()
                S.op("act", I("activation", out=rtmp[ri][:], in_=pf[fb][:, fo:fo + 256], func=AF.Relu),
                     reads=[fr], writes=[rtmp_res[ri]])
                qi = rTring.next()
                S.op("dve", I("tensor_tensor", out=rT[qi][:], in0=rtmp[ri][:], in1=pf[fb][:, fo:fo + 256],
                                                      op=ALU.mult),
                     reads=[rtmp_res[ri], fr], writes=[rT_res[qi]])
                return qi

            def ff2(j, qi):
                fns = []
                for tt in range(2):
                    for c in range(2):
                        fns.append(I("matmul",
                            out=pf[acc[tt][c]][:], lhsT=rT[qi][:, tt * 128:(tt + 1) * 128],
                            rhs=W2[:, j, c * 512:(c + 1) * 512], start=(j == 0), stop=(j == 31)))
                S.op("pe", fns, reads=[rT_res[qi], w2_res[j // 4][(j % 4) // 2]],
                     writes=[pf_res[acc[tt][c]] for tt in range(2) for c in range(2)])

            LAG = 2
            pend = []
            for j in range(32):
                pend.append((j, ff1(j)))
                if len(pend) > LAG:
                    ff2(*pend.pop(0))
                if G + 1 < NG2 and j == 1:
                    prep_norm(G + 1)
                if G + 1 < NG2 and j == 22:
                    prep_tr(G + 1)
            while pend:
                ff2(*pend.pop(0))

            tis = [Tring.next() for tt in range(2)]
            for tt in range(2):
                ti = tis[tt]
                for c in range(2):
                    csl = slice(c * 512, (c + 1) * 512)
                    if c == 0:
                        S.op("act", I("activation", out=Tt[ti][:, csl], in_=pf[acc[tt][c]][:], func=AF.Copy),
                             reads=[pf_res[acc[tt][c]]], writes=[Tt_resh[ti][c]])
                    else:
                        S.op("dve", I("tensor_copy", out=Tt[ti][:, csl], in_=pf[acc[tt][c]][:]),
                             reads=[pf_res[acc[tt][c]]], writes=[Tt_resh[ti][c]])
            for tt in range(2):
                row0 = (2 * G + tt) * 128
                ti = tis[tt]
                ss = s2ring.next()
                for c in range(2):
                    csl = slice(c * 512, (c + 1) * 512)
                    jk, jr = junk()
                    S.op("act", I("activation", out=jk[:, 0:512], in_=Tt[ti][:, csl], func=AF.Square,
                                  accum_out=stt2[:, ss, c:c + 1]),
                         reads=[Tt_resh[ti][c]], writes=[stt2_res[ss], jr])
                S.op("dve", I("tensor_tensor", out=stt2[:, ss, 2:3], in0=stt2[:, ss, 0:1],
                              in1=stt2[:, ss, 1:2], op=ALU.add),
                     reads=[stt2_res[ss]], writes=[stt2_res[ss]])
                r_ap = rstd2(stt2[:, ss, 2:3], stt2_res[ss], float(D), ss, 3)
                for c in range(2):
                    csl = slice(c * 512, (c + 1) * 512)
                    S.op("dve", I("scalar_tensor_tensor", out=Tt[ti][:, csl], in0=Tt[ti][:, csl], scalar=r_ap,
                                  in1=gpost2_bc[:, csl], op0=ALU.mult, op1=ALU.mult),
                         reads=[Tt_resh[ti][c], stt2_res[ss], c2_res], writes=[Tt_resh[ti][c]])
                xi = x2ring.next()
                S.dma(x2_sem[xi], I("dma_start", out=x2[xi][:], in_=x1s_d[row0:row0 + 128, :]),
                      writes=[x2_res[xi]])
                S.op("pool", I("tensor_tensor", out=Tt[ti][:], in0=Tt[ti][:], in1=x2[xi][:], op=ALU.add),
                     reads=Tt_resh[ti] + [x2_res[xi]], writes=Tt_resh[ti])
                S.dma(Tt_sem[ti], I("dma_start", out=out_d[row0:row0 + 128, :], in_=Tt[ti][:]),
                      reads=Tt_resh[ti])
        S.final_wait("sp", Tt_resh[0] + Tt_resh[1])
        S.flush()
    return nc


def _prep_shared(inp):
    f = np.float32
    c = {}
    c["w_in"] = np.ascontiguousarray(inp["w_in"][0], dtype=f)
    c["w_out"] = np.ascontiguousarray(inp["w_out"][0], dtype=f)
    c["w_ff1"] = np.ascontiguousarray(inp["w_ff1"][0], dtype=f)
    c["w_ff2"] = np.ascontiguousarray(inp["w_ff2"][0], dtype=f)
    c["gpre_pp"] = np.ascontiguousarray(inp["norm_mix_pre"][0].reshape(8, 128).T, dtype=f)
    gmix = np.concatenate([inp["g_out_na"][0], inp["g_out_sg"][0]])
    c["gmix_pp"] = np.ascontiguousarray(gmix.reshape(8, 128).T, dtype=f)
    c["lng_pp"] = np.ascontiguousarray(inp["sg_ln_g"][0].reshape(4, 128).T, dtype=f)
    c["gpost_bc"] = np.ascontiguousarray(np.broadcast_to(inp["norm_mix_post"][0][None, :], (128, D)), dtype=f)
    c["gpre2_bc"] = np.ascontiguousarray(np.broadcast_to(inp["norm_ffn_pre"][0][None, :], (128, D)), dtype=f)
    c["gpost2_bc"] = np.ascontiguousarray(np.broadcast_to(inp["norm_ffn_post"][0][None, :], (128, D)), dtype=f)
    rpb = np.asarray(inp["na_rpb"][0], dtype=f)
    cols = np.arange(GRID_W)
    dc_idx = np.clip(cols[None, :] - cols[:, None], -15, 15) + 15
    col_bias = rpb[:, :, dc_idx]
    relb = np.transpose(col_bias, (0, 2, 1, 3)).reshape(4, 2 * 64, 15 * 64)
    c["relb"] = np.ascontiguousarray(relb, dtype=f)
    col_start = np.clip(cols - 8, 0, GRID_W - 16)
    inwin = (cols[None, :] >= col_start[:, None]) & (cols[None, :] < col_start[:, None] + 16)
    m = np.where(inwin, 0.0, NEG).astype(f)
    m = np.broadcast_to(m[None, :, None, :], (2, 64, 15, 64)).reshape(128, 960)
    c["mask"] = np.ascontiguousarray(m, dtype=f)
    ws = np.asarray(inp["sg_w_s"][0], dtype=f)
    c["wsT"] = np.ascontiguousarray(np.transpose(ws, (2, 0, 1)), dtype=f)
    c["lnb_bc"] = np.ascontiguousarray(np.broadcast_to(inp["sg_ln_b"][0][None, :], (128, 512)), dtype=f)
    bs = np.asarray(inp["sg_b_s"][0], dtype=f)
    bsb = np.broadcast_to(bs.reshape(4, 2, 1, 128), (4, 2, 64, 128))
    c["bs_bc"] = np.ascontiguousarray(np.transpose(bsb.reshape(4, 128, 128), (1, 0, 2)), dtype=f)
    c["ident"] = np.eye(128, dtype=f)
    return c


_NC_CACHE = {}


def kernel(**inputs):
    x = np.asarray(inputs["x"], dtype=np.float32)
    B = x.shape[0]
    per = B // NCORES
    if per not in _NC_CACHE:
        _NC_CACHE[per] = build(nseq=per)
    nc = _NC_CACHE[per]
    shared = _prep_shared(inputs)
    in_maps = []
    for c in range(NCORES):
        m = dict(shared)
        m["x"] = np.ascontiguousarray(x[c * per:(c + 1) * per].reshape(per * SEQ, D))
        in_maps.append(m)
    res = run_bass_kernel_spmd(nc, in_maps, core_ids=list(range(NCORES)))
    outs = [np.asarray(r["out"], dtype=np.float32).reshape(per, SEQ, D) for r in res.results]
    return np.concatenate(outs, axis=0)
```
